# Optimizing a Trainium2 kernel written in Bass

```python
import jax, jax.numpy as jnp
from jax import lax
import numpy as np

D_MODEL = 2048
BATCH = 4
SEQ = 2048
DEPTH = 1
DEC_BATCH = 128
DEC_SEQ = 1
PAST_LEN = 16384
PAGE_SIZE = 128

HEAD_DIM = 64
D_RWKV = D_MODEL // 2
H_A = D_RWKV // HEAD_DIM
D_SGU = D_MODEL - D_RWKV
CHUNK = 128
SGU_GROUP_DIM = 128
G_SGU = D_SGU // SGU_GROUP_DIM
LORA_W = 64
LORA_A = 64
LORA_G = 160
D_SHIFT = 3 * D_RWKV + LORA_W + LORA_A + LORA_G
D_PROJ = D_SHIFT + 2 * D_SGU
D_FF = 4 * D_MODEL
RMS_EPS = 1e-5
LN_EPS = 1e-5
GN_EPS = 64e-5

kernel_name = "rwkv7_sgu_hybrid_step"


def _rmsnorm(x, g):
    xf = x.astype(jnp.float32)
    y = xf * lax.rsqrt(jnp.mean(xf * xf, axis=-1, keepdims=True) + RMS_EPS)
    return (y * g.astype(jnp.float32)).astype(x.dtype)


def _layernorm(x, g, b):
    xf = x.astype(jnp.float32)
    mu = jnp.mean(xf, axis=-1, keepdims=True)
    var = jnp.mean(jnp.square(xf - mu), axis=-1, keepdims=True)
    y = (xf - mu) * lax.rsqrt(var + LN_EPS)
    return (y * g.astype(jnp.float32) + b.astype(jnp.float32)).astype(x.dtype)


def _wkv7_scan(s0, r, w, k, v, aa, bb):
    def step(S, inp):
        r_t, w_t, k_t, v_t, a_t, b_t = inp
        sa = jnp.einsum('bhvk,bhk->bhv', S, a_t)
        S = S * w_t[:, :, None, :] + sa[..., None] * b_t[:, :, None, :] + v_t[..., None] * k_t[:, :, None, :]
        o = jnp.einsum('bhvk,bhk->bhv', S, r_t)
        return S, o
    xs = (jnp.swapaxes(r, 0, 1), jnp.swapaxes(w, 0, 1), jnp.swapaxes(k, 0, 1),
          jnp.swapaxes(v, 0, 1), jnp.swapaxes(aa, 0, 1), jnp.swapaxes(bb, 0, 1))
    s_final, o = lax.scan(step, s0, xs)
    return jnp.swapaxes(o, 0, 1), s_final


def _layer(x, wkv0, shift0, norm1_g, w_in, mu_shift, w0, w_up, a0, a_up, g_up, k_k, k_a, r_k,
           lnx_g, lnx_b, sgu_norm_g, sgu_norm_b, sgu_w, sgu_b, w_out, norm2_g, w_ffn_up, w_ffn_down):
    B, T, _ = x.shape
    f32 = jnp.float32
    h = _rmsnorm(x, norm1_g)
    p = h @ w_in
    p_rw, p_sg = p[..., :D_SHIFT], p[..., D_SHIFT:]

    prev = jnp.concatenate([shift0[:, None, :].astype(p_rw.dtype), p_rw[:, :-1]], axis=1)
    m = p_rw + (prev - p_rw) * mu_shift
    new_shift = p_rw[:, -1].astype(f32)
    r, k, v, w_lo, a_lo, g_lo = jnp.split(
        m, [D_RWKV, 2 * D_RWKV, 3 * D_RWKV, 3 * D_RWKV + LORA_W, 3 * D_RWKV + LORA_W + LORA_A], axis=-1)
    w_log = -jax.nn.softplus(-(w0 + jnp.tanh(w_lo) @ w_up).astype(f32)) - 0.5
    decay = jnp.exp(-jnp.exp(w_log))
    a = jax.nn.sigmoid((a0 + a_lo @ a_up).astype(f32))
    g = jax.nn.sigmoid(g_lo) @ g_up
    hs = (B, T, H_A, HEAD_DIM)
    rh = r.astype(f32).reshape(hs)
    kh = k.astype(f32).reshape(hs)
    vh = v.astype(f32).reshape(hs)
    ah = a.reshape(hs)
    wh = decay.reshape(hs)
    kk = kh * k_k.astype(f32)
    kk = kk / jnp.maximum(jnp.sqrt(jnp.sum(kk * kk, axis=-1, keepdims=True)), 1e-12)
    kh = kh * (1.0 + (ah - 1.0) * k_a.astype(f32))
    o, wkv_new = _wkv7_scan(wkv0.astype(f32), rh, wh, kh, vh, -kk, kk * ah)
    mu = jnp.mean(o, axis=-1, keepdims=True)
    var = jnp.mean(jnp.square(o - mu), axis=-1, keepdims=True)
    o = (o - mu) * lax.rsqrt(var + GN_EPS) * lnx_g.astype(f32) + lnx_b.astype(f32)
    o = o + jnp.sum(rh * kh * r_k.astype(f32), axis=-1, keepdims=True) * vh
    y_a = o.reshape(B, T, D_RWKV).astype(x.dtype) * g

    z = jax.nn.gelu(p_sg)
    u, vs = z[..., :D_SGU], z[..., D_SGU:]
    vs = _layernorm(vs, sgu_norm_g, sgu_norm_b)
    n_ch = -(-T // CHUNK)
    pad = n_ch * CHUNK - T
    vpad = jnp.pad(vs, ((0, 0), (0, pad), (0, 0))).reshape(B, n_ch, CHUNK, G_SGU, SGU_GROUP_DIM)
    tril = jnp.tril(jnp.ones((CHUNK, CHUNK), dtype=bool))
    ws = jnp.where(tril[None], sgu_w, jnp.zeros_like(sgu_w))
    mix = jnp.einsum('gij,bcjgd->bcigd', ws, vpad) + jnp.transpose(sgu_b)[None, None, :, :, None]
    mix = mix.reshape(B, n_ch * CHUNK, D_SGU)[:, :T]
    y_b = u * mix

    x = x + jnp.concatenate([y_a, y_b], axis=-1) @ w_out
    h2 = _rmsnorm(x, norm2_g)
    x = x + jnp.square(jax.nn.relu(h2 @ w_ffn_up)) @ w_ffn_down
    return x, wkv_new, new_shift, vs.astype(f32)


def setup_inputs(seed: int = 0) -> dict:
    key = jax.random.key(seed)
    ks = jax.random.split(key, 32)
    nrm = jax.random.normal
    L = DEPTH
    inp = {}
    inp["x_prompt"] = nrm(ks[0], (BATCH, SEQ, D_MODEL), jnp.float32)
    inp["x_sample"] = nrm(ks[1], (DEC_BATCH, DEC_SEQ, D_MODEL), jnp.float32)
    inp["state_wkv"] = 0.5 * nrm(ks[2], (L, DEC_BATCH, H_A, HEAD_DIM, HEAD_DIM), jnp.float32)
    inp["state_shift"] = nrm(ks[3], (L, DEC_BATCH, D_SHIFT), jnp.float32)
    inp["norm1_g"] = 1.0 + 0.02 * nrm(ks[4], (L, D_MODEL), jnp.float32)
    inp["w_in"] = nrm(ks[5], (L, D_MODEL, D_PROJ), jnp.float32) * D_MODEL ** -0.5
    inp["mu_shift"] = jax.random.uniform(ks[6], (L, D_SHIFT), jnp.float32)
    inp["w0"] = -1.0 + 0.5 * nrm(ks[7], (L, D_RWKV), jnp.float32)
    inp["w_up"] = 0.5 * nrm(ks[8], (L, LORA_W, D_RWKV), jnp.float32) * LORA_W ** -0.5
    inp["a0"] = 0.1 * nrm(ks[9], (L, D_RWKV), jnp.float32)
    inp["a_up"] = nrm(ks[10], (L, LORA_A, D_RWKV), jnp.float32) * LORA_A ** -0.5
    inp["g_up"] = nrm(ks[11], (L, LORA_G, D_RWKV), jnp.float32) * LORA_G ** -0.5
    inp["k_k"] = 0.85 + 0.05 * nrm(ks[12], (L, H_A, HEAD_DIM), jnp.float32)
    inp["k_a"] = 1.0 + 0.05 * nrm(ks[13], (L, H_A, HEAD_DIM), jnp.float32)
    inp["r_k"] = 0.1 * nrm(ks[14], (L, H_A, HEAD_DIM), jnp.float32)
    inp["lnx_g"] = 1.0 + 0.02 * nrm(ks[15], (L, H_A, HEAD_DIM), jnp.float32)
    inp["lnx_b"] = 0.02 * nrm(ks[16], (L, H_A, HEAD_DIM), jnp.float32)
    inp["sgu_norm_g"] = 1.0 + 0.02 * nrm(ks[17], (L, D_SGU), jnp.float32)
    inp["sgu_norm_b"] = 0.02 * nrm(ks[18], (L, D_SGU), jnp.float32)
    inp["sgu_w"] = nrm(ks[19], (L, G_SGU, CHUNK, CHUNK), jnp.float32) * CHUNK ** -0.5
    inp["sgu_b"] = 1.0 + 0.1 * nrm(ks[20], (L, G_SGU, CHUNK), jnp.float32)
    inp["w_out"] = nrm(ks[21], (L, D_MODEL, D_MODEL), jnp.float32) * D_MODEL ** -0.5
    inp["norm2_g"] = 1.0 + 0.02 * nrm(ks[22], (L, D_MODEL), jnp.float32)
    inp["w_ffn_up"] = nrm(ks[23], (L, D_MODEL, D_FF), jnp.float32) * D_MODEL ** -0.5
    inp["w_ffn_down"] = nrm(ks[24], (L, D_FF, D_MODEL), jnp.float32) * D_FF ** -0.5
    inp["norm_f_g"] = 1.0 + 0.02 * nrm(ks[25], (D_MODEL,), jnp.float32)
    return inp


def reference(x_prompt, x_sample, state_wkv, state_shift, norm1_g, w_in, mu_shift, w0, w_up, a0, a_up,
              g_up, k_k, k_a, r_k, lnx_g, lnx_b, sgu_norm_g, sgu_norm_b, sgu_w, sgu_b, w_out, norm2_g,
              w_ffn_up, w_ffn_down, norm_f_g):
    xp, xs = x_prompt, x_sample
    wkv_p, shift_p, wkv_s, shift_s, vrows_s = [], [], [], [], []
    zero_wkv = jnp.zeros((x_prompt.shape[0], H_A, HEAD_DIM, HEAD_DIM), jnp.float32)
    zero_shift = jnp.zeros((x_prompt.shape[0], D_SHIFT), jnp.float32)
    for l in range(DEPTH):
        lw = (norm1_g[l], w_in[l], mu_shift[l], w0[l], w_up[l], a0[l], a_up[l], g_up[l], k_k[l], k_a[l],
              r_k[l], lnx_g[l], lnx_b[l], sgu_norm_g[l], sgu_norm_b[l], sgu_w[l], sgu_b[l], w_out[l],
              norm2_g[l], w_ffn_up[l], w_ffn_down[l])
        xp, s_p, sh_p, _ = _layer(xp, zero_wkv, zero_shift, *lw)
        xs, s_s, sh_s, v_s = _layer(xs, state_wkv[l], state_shift[l], *lw)
        wkv_p.append(s_p); shift_p.append(sh_p)
        wkv_s.append(s_s); shift_s.append(sh_s); vrows_s.append(v_s)
    y_prompt = _rmsnorm(xp, norm_f_g)
    y_sample = _rmsnorm(xs, norm_f_g)
    wkv_prompt = jnp.stack(wkv_p)
    shift_prompt = jnp.stack(shift_p)
    wkv_sample = jnp.stack(wkv_s)
    shift_sample = jnp.stack(shift_s)
    sgu_v_sample = jnp.stack(vrows_s)
    return (y_prompt, y_sample, wkv_prompt, shift_prompt, wkv_sample, shift_sample, sgu_v_sample)
```

```python
import contextlib
import numpy as np
import concourse.bass as bass
import concourse.mybir as mybir
from concourse.bass_utils import run_bass_kernel_spmd

F32 = mybir.dt.float32
BF16 = mybir.dt.bfloat16
AF = mybir.ActivationFunctionType
ALU = mybir.AluOpType
AX = mybir.AxisListType


class Tk:
    __slots__ = ("ap", "w", "rs", "name")

    def __init__(self, ap, name=""):
        self.ap = ap
        self.w = None
        self.rs = []
        self.name = name

    def __getitem__(self, k):
        return self.ap[k]


class Q:
    def __init__(self, name, h, sem):
        self.name = name
        self.h = h
        self.sem = sem
        self.cnt = 0
        self.seen = {}


class Sched:
    N_DMA_SEMS = 30
    N_SW_SEMS = 64

    def __init__(self, nc, stack):
        self.nc = nc
        self.sems = {}
        self.q = {}
        for name, h in (("pe", nc.tensor), ("act", nc.scalar), ("dve", nc.vector),
                        ("pool", nc.gpsimd), ("sp", nc.sync)):
            s = stack.enter_context(nc.semaphore("q_" + name))
            self.sems[name] = s
            self.q[name] = Q(name, h, s)
        self.dma_sems = []
        for i in range(self.N_DMA_SEMS):
            key = "d%d" % i
            s = stack.enter_context(nc.semaphore(key))
            self.sems[key] = s
            self.dma_sems.append([key, 0])
        self.dma_rr = 0
        self.sw_sems = []
        for i in range(self.N_SW_SEMS):
            key = "w%d" % i
            s = stack.enter_context(nc.semaphore(key))
            self.sems[key] = s
            self.sw_sems.append([key, 0])
        self.sw_next = 0
        self.out_deps = []
        self.n_ins = 0
        self.n_wait = 0
        self.stopped = False

    def _wait(self, q, deps):
        best = {}
        for d in deps:
            if d is None:
                continue
            k, v = d
            if best.get(k, 0) < v:
                best[k] = v
        for k, v in best.items():
            if q.seen.get(k, 0) >= v:
                continue
            if k == q.name and k == "pe":
                continue
            q.h.wait_ge(self.sems[k], v)
            q.seen[k] = v
            self.n_wait += 1

    def _deps(self, reads, writes):
        deps = []
        for t in reads:
            deps.append(t.w)
        for t in writes:
            deps.append(t.w)
            deps.extend(t.rs)
        return deps

    def _mark(self, tok, reads, writes):
        for t in reads:
            t.rs.append(tok)
        for t in writes:
            t.w = tok
            t.rs = []

    def op(self, qn, fn, reads=(), writes=(), inc=True):
        if self.stopped:
            return None
        q = self.q[qn]
        self._wait(q, self._deps(reads, writes))
        ins = fn(q.h)
        self.n_ins += 1
        if inc:
            q.cnt += 1
            ins.then_inc(q.sem, 1)
            tok = (qn, q.cnt)
            q.seen[qn] = max(q.seen.get(qn, 0), 0)
        else:
            tok = (qn, q.cnt + 1)
        self._mark(tok, reads, writes)
        return ins

    def dma(self, qn, out, in_, reads=(), writes=(), is_output=False, **kw):
        if self.stopped:
            return None
        q = self.q[qn]
        if qn == "pool":
            slot = self.sw_sems[self.sw_next]
            self.sw_next += 1
        else:
            slot = self.dma_sems[self.dma_rr]
            self.dma_rr = (self.dma_rr + 1) % len(self.dma_sems)
        deps = self._deps(reads, writes)
        if slot[1] > 0:
            deps.append((slot[0], slot[1]))
        self._wait(q, deps)
        ins = q.h.dma_start(out=out, in_=in_, **kw)
        slot[1] += 16
        ins.then_inc(self.sems[slot[0]], 16)
        self.n_ins += 1
        tok = (slot[0], slot[1])
        self._mark(tok, reads, writes)
        if is_output:
            self.out_deps.append(tok)
        return tok

    def finish(self, qn="sp"):
        q = self.q[qn]
        self._wait(q, self.out_deps)

    def barrier(self):
        if self.stopped:
            return
        toks = []
        for qn, q in self.q.items():
            if q.cnt > 0:
                toks.append((qn, q.cnt))
        for k, v in self.dma_sems + self.sw_sems:
            if v > 0:
                toks.append((k, v))
        for qn, q in self.q.items():
            self._wait(q, toks)

D = 2048
DSH = 3360
DFF = 8192
C_DEC = float(np.exp(-0.5))
RMS_EPS = 1e-5
LN_EPS = 1e-5
GN_EPS = 64e-5

O_ID = 0
O_CM = O_ID + 128
O_T2 = O_CM + 258
O_F64 = O_T2 + 128
O_M4 = O_F64 + 128
O_MA = O_M4 + 512
O_BO = O_MA + 512
O_BS = O_BO + 128
O_OM = O_BS + 2
O_SEL = O_OM + 128
NCB = O_SEL + 1024
V_MU = 0
V_KK = 27
V_KA = 35
V_RK = 43
V_A0 = 51
V_SG = 59
V_SB = 67
V_W00 = 75
V_B0 = 83
NVB = 91


def make_consts():
    cb = np.zeros((128, NCB), np.float32)
    t = np.arange(128)
    cb[:, O_ID:O_ID + 128] = np.eye(128)
    le63 = (t <= 63).astype(np.float32)[:, None]
    cb[:, O_CM:O_CM + 128] = -C_DEC * ((t[:, None] <= t[None, :]).astype(np.float32) - le63)
    cb[:, O_CM + 128:O_CM + 256] = -C_DEC * ((t[:, None] < t[None, :]).astype(np.float32) - le63)
    cb[:, O_CM + 256] = -C_DEC * (t >= 64)
    cb[:, O_CM + 257] = -C_DEC * (t <= 63)
    cb[:, O_T2:O_T2 + 128] = -C_DEC * (t[:, None] > t[None, :])
    cb[:, O_F64:O_F64 + 128] = -C_DEC * le63
    mus = (t[:, None] < t[None, :]).astype(np.float32)
    mui = (t[:, None] <= t[None, :]).astype(np.float32)
    cb[:, O_M4:O_M4 + 512] = np.concatenate([mus, mui, mus, mui], 1)
    mls = (t[None, :] < t[:, None]).astype(np.float32)
    cb[:, O_MA:O_MA + 512] = np.concatenate([mls] * 4, 1)
    cb[:, O_BO:O_BO + 128] = (t[:, None] // 64 == t[None, :] // 64)
    cb[:, O_BS] = (t < 64)
    cb[:, O_BS + 1] = (t >= 64)
    cb[:, O_OM:O_OM + 128] = 1.0 / 1024
    for b in range(16):
        cb[b, O_SEL + b * 64:O_SEL + (b + 1) * 64] = 1.0
    return cb


class StopBuild(Exception):
    pass


def make_sel2():
    z = np.zeros((32, 16 * 128), np.float32)
    for b in range(16):
        z[b, b * 128:b * 128 + 64] = 1.0
        z[16 + b, b * 128 + 64:(b + 1) * 128] = 1.0
    return z


def build():
    import os
    STAGE = int(os.environ.get("KSTAGE", "0"))

    def ck(n):
        if STAGE == n and n > 0:
            SREF[0].stopped = True
    SREF = [None]
    nc = bass.Bass("TRN2", target_bir_lowering=False)

    def din(name, shape):
        return nc.dram_tensor(name, shape, F32, kind="ExternalInput").ap()

    def dout(name, shape):
        return nc.dram_tensor(name, shape, F32, kind="ExternalOutput").ap()

    xin = din("xin", [2048, D])
    xsm = din("xsm", [17, D])
    swkv = din("swkv", [16, 16, 64, 64])
    shT_d = din("shT", [128, 27 * 16])
    w_in = din("w_in", [D, 5408])
    w_out = din("w_out", [D, D])
    w_fu = din("w_fu", [D, DFF])
    w_fd = din("w_fd", [DFF, D])
    wupa_d = din("wupa", [65, 1024])
    aup_d = din("aup", [64, 1024])
    gup_d = din("gup", [160, 1024])
    g1_d = din("g1", [1, D])
    g2_d = din("g2", [1, D])
    gf_d = din("gf", [1, D])
    lg_d = din("lg", [1, 1024])
    lb_d = din("lb", [1, 1024])
    sgb_d = din("sgb", [1, 1024])
    sgw_d = din("sgw", [8, 128, 128])
    cb_d = din("cb", [128, NCB])
    vb_d = din("vb", [128, NVB])
    sel2_d = din("sel2", [32, 16 * 128])

    y_o = dout("y", [1024, D])
    ysm_o = dout("ysm", [16, D])
    wkvp_o = dout("wkvp", [16, 64, 64])
    shp_o = dout("shp", [17, 27 * 128])
    wkvs_o = dout("wkvs", [16, 16, 64, 64])
    sguv_o = dout("sguv", [16, 1024])

    YT_d = nc.dram_tensor("YT", [16, 128, 1040], BF16).ap()
    PR_d = nc.dram_tensor("PR", [24, 128, 1041], F32).ap()
    VZ_d = nc.dram_tensor("VZ", [8, 128, 1040], F32).ap()
    UZ_d = nc.dram_tensor("UZ", [8, 128, 1040], BF16).ap()

    w_in_v = w_in.rearrange("(kc p) c -> p kc c", p=128)
    w_fu_v = w_fu.rearrange("(kc p) c -> p kc c", p=128)
    w_out_v = w_out.rearrange("(kb p) d -> p kb d", p=128)

    with contextlib.ExitStack() as st0:
        S = Sched(nc, st0)
        SREF[0] = S

        uid = [0]

        def sb(st, name, shape, dt=F32):
            uid[0] += 1
            nm = "s%d_%s" % (uid[0], name)
            return Tk(st.enter_context(nc.sbuf_tensor(nm, shape, dt)), nm)

        PS = [Tk(st0.enter_context(nc.psum_tensor("ps%d" % i, [128, 512], F32)), "ps%d" % i) for i in range(8)]
        psi = [0]

        pool_sel = [None]
        pool_cnt = {}

        def ps():
            if pool_sel[0] is not None:
                base, size = pool_sel[0]
                k = pool_cnt.get(base, 0)
                pool_cnt[base] = k + 1
                return PS[base + k % size]
            t = PS[psi[0] % 8]
            psi[0] += 1
            return t

        ACT = lambda fn, r, w: S.op("act", fn, r, w)
        DVE = lambda fn, r, w: S.op("dve", fn, r, w)
        PE = lambda fn, r, w, inc=True: S.op("pe", fn, r, w, inc=inc)
        YTk = Tk(YT_d, "YT")
        PRtk = [Tk(PR_d[b], "PR%d" % b) for b in range(24)]
        VZtk = Tk(VZ_d, "VZd")
        UZtk = Tk(UZ_d, "UZd")

        CB = sb(st0, "CB", [128, NCB])
        VB = sb(st0, "VB", [128, NVB])
        S.dma("sp", CB.ap[:], cb_d, writes=[CB])
        S.dma("sp", VB.ap[:], vb_d, writes=[VB])
        IDB = sb(st0, "IDB", [128, 128], BF16)
        BOB = sb(st0, "BOB", [128, 128], BF16)
        BSB = sb(st0, "BSB", [128, 2], BF16)
        DVE(lambda e: e.tensor_copy(out=IDB.ap[:], in_=CB.ap[:, O_ID:O_ID + 128]), [CB], [IDB])
        DVE(lambda e: e.tensor_copy(out=BOB.ap[:], in_=CB.ap[:, O_BO:O_BO + 128]), [CB], [BOB])
        DVE(lambda e: e.tensor_copy(out=BSB.ap[:], in_=CB.ap[:, O_BS:O_BS + 2]), [CB], [BSB])
        ident = CB.ap[:, O_ID:O_ID + 128]
        SST = [sb(st0, "sst%d" % j, [128, 128]) for j in range(8)]
        SBF = [sb(st0, "sbf%d" % j, [128, 128], BF16) for j in range(8)]
        for j in range(8):
            DVE(lambda e: e.memset(SST[j].ap[:], 0.0), [], [SST[j]])
            DVE(lambda e: e.memset(SBF[j].ap[:], 0.0), [], [SBF[j]])
        SH = sb(st0, "SH", [128, 27, 17])
        DVE(lambda e: e.memset(SH.ap[:], 0.0), [], [SH])
        SMZ = sb(st0, "SMZ", [128, 8, 5, 32])
        SMv = sb(st0, "SMv", [128, 8, 16])
        DVE(lambda e: e.memset(SMZ.ap[:], 0.0), [], [SMZ])
        RKS = sb(st0, "RKS", [128, 16])
        ss_all = sb(st0, "ss", [128, 32])
        ss_list = [Tk(ss_all.ap[:, 4 * i:4 * i + 4], "ss%d" % i) for i in range(8)]
        ss_rr = [0]

        def next_ss():
            t = ss_list[ss_rr[0] % 8]
            ss_rr[0] += 1
            return t
        wupa = sb(st0, "wupa", [128, 1024], BF16)
        aup = sb(st0, "aup", [128, 1024], BF16)
        gup1 = sb(st0, "gup1", [128, 1024], BF16)
        gup2 = sb(st0, "gup2", [128, 1024], BF16)
        S.dma("pool", wupa.ap[0:65, :], wupa_d, writes=[wupa])
        S.dma("pool", aup.ap[64:128, :], aup_d, writes=[aup])
        S.dma("pool", gup1.ap[:, :], gup_d[0:128, :], writes=[gup1])
        S.dma("pool", gup2.ap[0:32, :], gup_d[128:160, :], writes=[gup2])

        def norm_S1(xt, xap, n, hb, ss_t):
            DVE(lambda e: e.memset(ss_t.ap[0:n, 0:1], 0.0), [], [ss_t])
            ACT(lambda e: e.activation(out=hb.ap[0:n, :], in_=xap[0:n, :], func=AF.Square, accum_out=ss_t.ap[0:n, 0:1]), [xt], [hb, ss_t])

        def norm_S2(xt, xap, n, G, hb, ss_t):
            DVE(lambda e: e.tensor_scalar(out=ss_t.ap[0:n, 1:2], in0=ss_t.ap[0:n, 0:1], scalar1=1.0 / D, scalar2=RMS_EPS, op0=ALU.mult, op1=ALU.add), [ss_t], [ss_t])
            ACT(lambda e: e.activation(out=ss_t.ap[0:n, 2:3], in_=ss_t.ap[0:n, 1:2], func=AF.Sqrt), [ss_t], [ss_t])
            DVE(lambda e: e.reciprocal(out=ss_t.ap[0:n, 3:4], in_=ss_t.ap[0:n, 2:3]), [ss_t], [ss_t])
            DVE(lambda e: e.scalar_tensor_tensor(out=hb.ap[0:n, :], in0=xap[0:n, :], scalar=ss_t.ap[0:n, 3:4], in1=G.ap[0:n, :], op0=ALU.mult, op1=ALU.mult), [xt, ss_t, G], [hb])

        def norm_S3(n, hb, hT, col0):
            for half in range(2):
                P = ps()
                pb = P.ap[:].bitcast(BF16)
                for k8 in range(8):
                    kc = half * 8 + k8
                    PE(lambda e: e.transpose(out=pb[:, k8 * 128:k8 * 128 + n], in_=hb.ap[0:n, kc * 128:(kc + 1) * 128], identity=IDB.ap[0:n, 0:n]), [hb, IDB], [P], inc=(k8 == 7))
                ACT(lambda e: e.activation(out=hT.ap[:, half * 8:(half + 1) * 8, col0:col0 + n], in_=pb.rearrange("p (k t) -> p k t", k=8)[:, :, 0:n], func=AF.Copy), [P], [hT])

        def norm_pipe(items, G, hbs, hT):
            nI = len(items)
            st = {}
            for step in range(nI + 2):
                if step < nI:
                    ld, n, col0 = items[step]
                    xt, xap = ld()
                    st[step] = (xt, xap, next_ss())
                    norm_S1(xt, xap, n, hbs[step % len(hbs)], st[step][2])
                q = step - 1
                if 0 <= q < nI:
                    xt, xap, sst = st[q]
                    norm_S2(xt, xap, items[q][1], G, hbs[q % len(hbs)], sst)
                q = step - 2
                if 0 <= q < nI:
                    norm_S3(items[q][1], hbs[q % len(hbs)], hT, items[q][2])

        def proj(wt, c0, npart, hT, groups, evac):
            for (t0, t1) in groups:
                P = ps()
                for kc in range(16):
                    PE(lambda e: e.matmul(P.ap[0:npart, 0:t1 - t0], lhsT=wt.ap[:, kc, c0:c0 + npart], rhs=hT.ap[:, kc, t0:t1], start=(kc == 0), stop=(kc == 15)), [wt, hT], [P], inc=(kc == 15))
                evac(P, t0, t1)

        def post(n, C, H, O_tk, O3, O4unused, Q1, Q1_3, Q1_4, Q2, Q2_3, st16, LG4, LB4, LGtk, LBtk):
            DVE(lambda e: e.tensor_reduce(out=st16.ap[0:n, 0, :], in_=O3, axis=AX.X, op=ALU.add), [O_tk], [st16])
            ACT(lambda e: e.activation(out=Q2_3, in_=O3, func=AF.Square), [O_tk], [Q2])
            DVE(lambda e: e.tensor_reduce(out=st16.ap[0:n, 1, :], in_=Q2_3, axis=AX.X, op=ALU.add), [Q2, st16], [st16])
            DVE(lambda e: e.tensor_scalar(out=st16.ap[0:n, 0, :], in0=st16.ap[0:n, 0, :], scalar1=1.0 / 64, scalar2=None, op0=ALU.mult), [st16], [st16])
            DVE(lambda e: e.tensor_tensor(out=st16.ap[0:n, 2, :], in0=st16.ap[0:n, 0, :], in1=st16.ap[0:n, 0, :], op=ALU.mult), [st16], [st16])
            DVE(lambda e: e.scalar_tensor_tensor(out=st16.ap[0:n, 1, :], in0=st16.ap[0:n, 1, :], scalar=1.0 / 64, in1=st16.ap[0:n, 2, :], op0=ALU.mult, op1=ALU.subtract), [st16], [st16])
            DVE(lambda e: e.tensor_scalar(out=st16.ap[0:n, 1, :], in0=st16.ap[0:n, 1, :], scalar1=GN_EPS, scalar2=None, op0=ALU.add), [st16], [st16])
            ACT(lambda e: e.activation(out=st16.ap[0:n, 2, :], in_=st16.ap[0:n, 1, :], func=AF.Sqrt), [st16], [st16])
            DVE(lambda e: e.reciprocal(out=st16.ap[0:n, 3, :], in_=st16.ap[0:n, 2, :]), [st16], [st16])
            DVE(lambda e: e.tensor_tensor(out=Q1_3, in0=O3, in1=st16.ap[0:n, 0, :].unsqueeze(2).to_broadcast([n, 16, 64]), op=ALU.subtract), [O_tk, st16], [Q1])
            DVE(lambda e: e.tensor_tensor(out=Q1_3, in0=Q1_3, in1=st16.ap[0:n, 3, :].unsqueeze(2).to_broadcast([n, 16, 64]), op=ALU.mult), [Q1, st16], [Q1])
            DVE(lambda e: e.tensor_tensor(out=Q1_4, in0=Q1_4, in1=LG4, op=ALU.mult), [Q1, LGtk], [Q1])
            DVE(lambda e: e.tensor_tensor(out=Q1_4, in0=Q1_4, in1=LB4, op=ALU.add), [Q1, LBtk], [Q1])

        groups3 = [(0, 512), (512, 1024), (1024, 1040)]

        def sample_gen(sts, stp_ctx):
            sgT1, sgT2, gup1, gup2 = stp_ctx
            if True:
                ROWS2 = sb(sts, "ROWS2", [32, 5, 512])
                ROWSV = sb(sts, "ROWSV", [16, 1024])
                SEL2 = sb(sts, "SEL2", [32, 16 * 128])
                I2 = sb(sts, "I2", [128, 64])
                OS = sb(sts, "OS", [128, 8, 16])
                Sin = [sb(sts, "Sin%d" % i, [128, 8, 64]) for i in range(2)]
                X1 = sb(sts, "X1", [128, 8, 64])
                X2 = [sb(sts, "X2%d" % i, [128, 8, 64]) for i in range(2)]
                sa = sb(sts, "sa", [128, 8])
                OSM = sb(sts, "OSM", [16, 1024])
                Q1s = sb(sts, "Q1s", [16, 1024])
                Q2s = sb(sts, "Q2s", [16, 1024])
                st16s = sb(sts, "st16s", [16, 6, 16])
                LGf = sb(sts, "LGf", [16, 1024])
                LBf = sb(sts, "LBf", [16, 1024])
                yas = sb(sts, "yas", [16, 1024], BF16)
                yaTs = sb(sts, "yaTs", [128, 8, 16], BF16)
                S.dma("sp", SEL2.ap[:], sel2_d, writes=[SEL2])
                S.dma("sp", LGf.ap[:], lg_d.partition_broadcast(16), writes=[LGf])
                S.dma("sp", LBf.ap[:], lb_d.partition_broadcast(16), writes=[LBf])
                DVE(lambda e: e.tensor_tensor(out=I2.ap[:], in0=CB.ap[:, O_ID:O_ID + 64], in1=CB.ap[:, O_ID + 64:O_ID + 128], op=ALU.add), [CB], [I2])
                for vec in range(5):
                    for half in range(2):
                        P = ps()
                        for jj in range(4):
                            j = half * 4 + jj
                            PE(lambda e: e.matmul(P.ap[0:32, jj * 64:(jj + 1) * 64], lhsT=SMZ.ap[:, j, vec, :], rhs=I2.ap[:], start=True, stop=True), [SMZ, I2], [P], inc=(jj == 3))
                        ACT(lambda e: e.activation(out=ROWS2.ap[:, vec, half * 256:(half + 1) * 256], in_=P.ap[0:32, 0:256], func=AF.Copy), [P], [ROWS2])
                for half in range(2):
                    P = ps()
                    for jj in range(4):
                        j = half * 4 + jj
                        PE(lambda e: e.transpose(out=P.ap[0:16, jj * 128:(jj + 1) * 128], in_=SMv.ap[:, j, :], identity=ident), [SMv, CB], [P], inc=(jj == 3))
                    ACT(lambda e: e.activation(out=ROWSV.ap[:, half * 512:(half + 1) * 512], in_=P.ap[0:16, :], func=AF.Copy), [P], [ROWSV])
                yield

                def bc(vec, b):
                    Pq = ps()
                    PE(lambda e: e.matmul(Pq.ap[:, :], lhsT=SEL2.ap[0:32, b * 128:(b + 1) * 128], rhs=ROWS2.ap[0:32, vec, :], start=True, stop=True), [ROWS2, SEL2], [Pq])
                    return Pq

                pv = lambda P_: P_.ap[:, :].rearrange("p (h k) -> p h k", h=8)
                for b in range(16):
                    Si = Sin[b % 2]
                    Xo = X2[b % 2]
                    S.dma("sp", Si.ap[:], swkv[b].rearrange("(hp h2) v k -> (h2 v) hp k", h2=2), writes=[Si])
                    Pq = bc(0, b)
                    DVE(lambda e: e.tensor_tensor(out=X1.ap[:], in0=Si.ap[:], in1=pv(Pq), op=ALU.mult), [Si, Pq], [X1])
                    DVE(lambda e: e.tensor_reduce(out=sa.ap[:], in_=X1.ap[:], axis=AX.X, op=ALU.add), [X1], [sa])
                    Pq = bc(4, b)
                    DVE(lambda e: e.tensor_tensor(out=Xo.ap[:], in0=Si.ap[:], in1=pv(Pq), op=ALU.mult), [Si, Pq], [Xo])
                    Pq = bc(1, b)
                    DVE(lambda e: e.tensor_tensor(out=X1.ap[:], in0=pv(Pq), in1=sa.ap[:].unsqueeze(2).to_broadcast([128, 8, 64]), op=ALU.mult), [sa, Pq], [X1])
                    DVE(lambda e: e.tensor_tensor(out=Xo.ap[:], in0=Xo.ap[:], in1=X1.ap[:], op=ALU.add), [Xo, X1], [Xo])
                    Pq = bc(2, b)
                    DVE(lambda e: e.tensor_tensor(out=X1.ap[:], in0=pv(Pq), in1=SMv.ap[:, :, b:b + 1].to_broadcast([128, 8, 64]), op=ALU.mult), [SMv, Pq], [X1])
                    DVE(lambda e: e.tensor_tensor(out=Xo.ap[:], in0=Xo.ap[:], in1=X1.ap[:], op=ALU.add), [Xo, X1], [Xo])
                    S.dma("sp", wkvs_o[b].rearrange("(hp h2) v k -> (h2 v) hp k", h2=2), Xo.ap[:], reads=[Xo], is_output=True)
                    Pq = bc(3, b)
                    DVE(lambda e: e.tensor_tensor(out=X1.ap[:], in0=Xo.ap[:], in1=pv(Pq), op=ALU.mult), [Xo, Pq], [X1])
                    DVE(lambda e: e.tensor_reduce(out=OS.ap[:, :, b], in_=X1.ap[:], axis=AX.X, op=ALU.add), [X1], [OS])
                    yield
                for half in range(2):
                    P = ps()
                    for jj in range(4):
                        hp_ = half * 4 + jj
                        PE(lambda e: e.transpose(out=P.ap[0:16, jj * 128:(jj + 1) * 128], in_=OS.ap[:, hp_, :], identity=ident), [OS, CB], [P], inc=(jj == 3))
                    ACT(lambda e: e.activation(out=OSM.ap[:, half * 512:(half + 1) * 512], in_=P.ap[0:16, :], func=AF.Copy), [P], [OSM])
                v3 = lambda t: t.ap[:].rearrange("p (h v) -> p h v", h=16)
                v4 = lambda t: t.ap[:].rearrange("p (c h v) -> p c h v", c=1, h=16)
                post(16, 1, 16, OSM, v3(OSM), None, Q1s, v3(Q1s), v4(Q1s), Q2s, v3(Q2s), st16s, v4(LGf), v4(LBf), LGf, LBf)
                DVE(lambda e: e.tensor_tensor(out=v3(Q2s), in0=ROWSV.ap[:, :].rearrange("p (h v) -> p h v", h=16), in1=RKS.ap[0:16, :].unsqueeze(2).to_broadcast([16, 16, 64]), op=ALU.mult), [ROWSV, RKS], [Q2s])
                DVE(lambda e: e.tensor_tensor(out=Q1s.ap[:], in0=Q1s.ap[:], in1=Q2s.ap[:], op=ALU.add), [Q1s, Q2s], [Q1s])
                for half in range(2):
                    P = ps()
                    cs = slice(half * 512, (half + 1) * 512)
                    PE(lambda e: e.matmul(P.ap[0:16, :], lhsT=sgT1.ap[:, 1024:1040], rhs=gup1.ap[:, cs], start=True, stop=False), [sgT1, gup1], [P], inc=False)
                    PE(lambda e: e.matmul(P.ap[0:16, :], lhsT=sgT2.ap[0:32, 1024:1040], rhs=gup2.ap[0:32, cs], start=False, stop=True), [sgT2, gup2], [P])
                    DVE(lambda e: e.tensor_tensor(out=yas.ap[:, cs], in0=P.ap[0:16, :], in1=Q1s.ap[:, cs], op=ALU.mult), [P, Q1s], [yas])
                P = ps()
                pb = P.ap[:].bitcast(BF16)
                for j in range(8):
                    PE(lambda e: e.transpose(out=pb[:, j * 16:(j + 1) * 16], in_=yas.ap[0:16, j * 128:(j + 1) * 128], identity=IDB.ap[0:16, 0:16]), [yas, IDB], [P], inc=(j == 7))
                ACT(lambda e: e.activation(out=yaTs.ap[:].rearrange("p j t -> p (j t)"), in_=pb[:, 0:128], func=AF.Copy), [P], [yaTs])
                S.dma("sp", YT_d[0:8, :, 1024:1040].rearrange("j p t -> p j t"), yaTs.ap[:], reads=[yaTs], writes=[YTk])

        def sgu_gen(stg):
            if True:
                VZ = [sb(stg, "VZ%d" % g, [128, 1040]) for g in range(8)]
                UZ = sb(stg, "UZ", [128, 8, 1040], BF16)
                MEAN = sb(stg, "MEAN", [128, 1040])
                RSTD = sb(stg, "RSTD", [128, 1040])
                SQ = sb(stg, "SQ", [128, 1040])
                WsT = sb(stg, "WsT", [128, 8, 128], BF16)
                Wsf = sb(stg, "Wsf", [128, 8, 128])
                SGB = sb(stg, "SGB", [128, 1024])
                VTb = sb(stg, "VTb", [128, 4, 128], BF16)
                tmpm = sb(stg, "tmpm", [128, 4, 128])
                ybT = [sb(stg, "ybT%d" % i, [128, 1040], BF16) for i in range(2)]
                VNs = sb(stg, "VNs", [128, 8, 16])
                sguvT = sb(stg, "sguvT", [16, 1024])
                S.dma("sp", Wsf.ap[:], sgw_d.rearrange("g i j -> i g j"), writes=[Wsf])
                S.dma("sp", SGB.ap[:], sgb_d.partition_broadcast(128), writes=[SGB])
                yield
                for g4 in range(2):
                    P = ps()
                    for gg in range(4):
                        g = g4 * 4 + gg
                        PE(lambda e: e.transpose(out=P.ap[:, gg * 128:(gg + 1) * 128], in_=Wsf.ap[:, g, :], identity=ident), [Wsf, CB], [P], inc=(gg == 3))
                    DVE(lambda e: e.tensor_tensor(out=WsT.ap[:, g4 * 4:(g4 + 1) * 4, :], in0=P.ap[:].rearrange("p (g t) -> p g t", g=4),
                                                  in1=CB.ap[:, O_M4 + 128:O_M4 + 256].unsqueeze(1).to_broadcast([128, 4, 128]), op=ALU.mult), [P, CB], [WsT])
                yield
                for g in range(8):
                    S.dma("sp", VZ[g].ap[:], VZ_d[g], reads=[VZtk], writes=[VZ[g]])
                S.dma("sp", UZ.ap[:], UZ_d.rearrange("g p t -> p g t"), reads=[UZtk], writes=[UZ])
                yield
                for (t0, t1) in groups3:
                    n = t1 - t0
                    P1 = ps()
                    for g in range(8):
                        PE(lambda e: e.matmul(P1.ap[:, 0:n], lhsT=CB.ap[:, O_OM:O_OM + 128], rhs=VZ[g].ap[:, t0:t1], start=(g == 0), stop=(g == 7)), [VZ[g], CB], [P1], inc=(g == 7))
                    ACT(lambda e: e.activation(out=MEAN.ap[:, t0:t1], in_=P1.ap[:, 0:n], func=AF.Copy), [P1], [MEAN])
                    P2 = ps()
                    for g in range(8):
                        ACT(lambda e: e.activation(out=SQ.ap[:, 0:n], in_=VZ[g].ap[:, t0:t1], func=AF.Square), [VZ[g]], [SQ])
                        PE(lambda e: e.matmul(P2.ap[:, 0:n], lhsT=CB.ap[:, O_OM:O_OM + 128], rhs=SQ.ap[:, 0:n], start=(g == 0), stop=(g == 7)), [SQ, CB], [P2], inc=True)
                    ACT(lambda e: e.activation(out=RSTD.ap[:, t0:t1], in_=P2.ap[:, 0:n], func=AF.Copy), [P2], [RSTD])
                for _ in range(8):
                    yield
                DVE(lambda e: e.tensor_tensor(out=SQ.ap[:], in0=MEAN.ap[:], in1=MEAN.ap[:], op=ALU.mult), [MEAN], [SQ])
                DVE(lambda e: e.tensor_tensor(out=RSTD.ap[:], in0=RSTD.ap[:], in1=SQ.ap[:], op=ALU.subtract), [RSTD, SQ], [RSTD])
                DVE(lambda e: e.tensor_scalar(out=RSTD.ap[:], in0=RSTD.ap[:], scalar1=LN_EPS, scalar2=None, op0=ALU.add), [RSTD], [RSTD])
                ACT(lambda e: e.activation(out=RSTD.ap[:], in_=RSTD.ap[:], func=AF.Sqrt), [RSTD], [RSTD])
                DVE(lambda e: e.reciprocal(out=RSTD.ap[:], in_=RSTD.ap[:]), [RSTD], [RSTD])
                yield
                for g in range(8):
                    DVE(lambda e: e.tensor_tensor(out=VZ[g].ap[:], in0=VZ[g].ap[:], in1=MEAN.ap[:], op=ALU.subtract), [VZ[g], MEAN], [VZ[g]])
                    DVE(lambda e: e.tensor_tensor(out=VZ[g].ap[:], in0=VZ[g].ap[:], in1=RSTD.ap[:], op=ALU.mult), [VZ[g], RSTD], [VZ[g]])
                    DVE(lambda e: e.tensor_scalar(out=VZ[g].ap[:], in0=VZ[g].ap[:], scalar1=VB.ap[:, V_SG + g:V_SG + g + 1], scalar2=VB.ap[:, V_SB + g:V_SB + g + 1], op0=ALU.mult, op1=ALU.add), [VZ[g], VB], [VZ[g]])
                    DVE(lambda e: e.tensor_copy(out=VNs.ap[:, g, :], in_=VZ[g].ap[:, 1024:1040]), [VZ[g]], [VNs])
                    yield
                yield
                for half in range(2):
                    P = ps()
                    for gg in range(4):
                        g = half * 4 + gg
                        PE(lambda e: e.transpose(out=P.ap[0:16, gg * 128:(gg + 1) * 128], in_=VNs.ap[:, g, :], identity=ident), [VNs, CB], [P], inc=(gg == 3))
                    ACT(lambda e: e.activation(out=sguvT.ap[:, half * 512:(half + 1) * 512], in_=P.ap[0:16, :], func=AF.Copy), [P], [sguvT])
                S.dma("sp", sguv_o, sguvT.ap[:], reads=[sguvT], is_output=True)
                yield
                for g in range(8):
                    yb = ybT[g % 2]
                    for c4 in range(2):
                        P = ps()
                        for cc in range(4):
                            c = c4 * 4 + cc
                            PE(lambda e: e.transpose(out=P.ap[:, cc * 128:(cc + 1) * 128], in_=VZ[g].ap[:, c * 128:(c + 1) * 128], identity=ident), [VZ[g], CB], [P], inc=(cc == 3))
                        ACT(lambda e: e.activation(out=VTb.ap[:], in_=P.ap[:].rearrange("p (c t) -> p c t", c=4), func=AF.Copy), [P], [VTb])
                        PM = ps()
                        for cc in range(4):
                            PE(lambda e: e.matmul(PM.ap[:, cc * 128:(cc + 1) * 128], lhsT=VTb.ap[:, cc, :], rhs=WsT.ap[:, g, :], start=True, stop=True), [VTb, WsT], [PM], inc=(cc == 3))
                        DVE(lambda e: e.tensor_tensor(out=tmpm.ap[:], in0=PM.ap[:].rearrange("p (c t) -> p c t", c=4), in1=SGB.ap[:, g * 128:(g + 1) * 128].unsqueeze(1).to_broadcast([128, 4, 128]), op=ALU.add), [PM, SGB], [tmpm])
                        DVE(lambda e: e.tensor_tensor(out=yb.ap[:, c4 * 512:(c4 + 1) * 512], in0=tmpm.ap[:].rearrange("p c t -> p (c t)"), in1=UZ.ap[:, g, c4 * 512:(c4 + 1) * 512], op=ALU.mult), [tmpm, UZ], [yb])
                    DVE(lambda e: e.tensor_scalar(out=tmpm.ap[:, 0, 0:16], in0=VZ[g].ap[:, 1024:1040], scalar1=VB.ap[:, V_W00 + g:V_W00 + g + 1], scalar2=VB.ap[:, V_B0 + g:V_B0 + g + 1], op0=ALU.mult, op1=ALU.add), [VZ[g], VB], [tmpm])
                    DVE(lambda e: e.tensor_tensor(out=yb.ap[:, 1024:1040], in0=tmpm.ap[:, 0, 0:16], in1=UZ.ap[:, g, 1024:1040], op=ALU.mult), [tmpm, UZ], [yb])
                    S.dma("sp", YT_d[8 + g], yb.ap[:], reads=[yb], writes=[YTk])
                    yield

        def rest_phase():
            with contextlib.ExitStack() as str_:
                x1 = sb(str_, "x1", [128, 9, D])
                h2T = sb(str_, "h2T", [128, 16, 1040], BF16)
                tiles = [(ti, 128 if ti < 8 else 16) for ti in range(9)]
                x1t = [Tk(x1.ap[:, ti, :], "x1_%d" % ti) for ti in range(9)]
                with contextlib.ExitStack() as s1:
                    yT = sb(s1, "yT", [128, 16, 1040], BF16)
                    wos = [sb(s1, "wo%d" % i, [128, 16, 512], BF16) for i in range(2)]
                    S.dma("sp", yT.ap[:, 0:8, :], YT_d[0:8].rearrange("k p t -> p k t"), reads=[YTk], writes=[yT])
                    S.dma("sp", yT.ap[:, 8:16, :], YT_d[8:16].rearrange("k p t -> p k t"), reads=[YTk], writes=[yT])
                    for ti in range(8):
                        S.dma("sp", x1.ap[:, ti, :], xin[1024 + ti * 128:1024 + (ti + 1) * 128, :], writes=[x1t[ti]])
                    S.dma("sp", x1.ap[0:16, 8, :], xsm[0:16, :], writes=[x1t[8]])
                    for dg in range(4):
                        ds_ = slice(dg * 512, (dg + 1) * 512)
                        wo = wos[dg % 2]
                        S.dma("pool", wo.ap[:], w_out_v[:, :, ds_], writes=[wo])
                        for ti, n in tiles:
                            cols = slice(ti * 128, ti * 128 + n)
                            P = ps()
                            for kb in range(16):
                                PE(lambda e: e.matmul(P.ap[0:n, :], lhsT=yT.ap[:, kb, cols], rhs=wo.ap[:, kb, :], start=(kb == 0), stop=(kb == 15)), [yT, wo], [P], inc=(kb == 15))
                            DVE(lambda e: e.tensor_tensor(out=x1.ap[0:n, ti, ds_], in0=x1.ap[0:n, ti, ds_], in1=P.ap[0:n, :], op=ALU.add), [x1t[ti], P], [x1t[ti]])
                    S.barrier()
                with contextlib.ExitStack() as s1b:
                    G2 = sb(s1b, "G2", [128, D])
                    hbs2 = [sb(s1b, "hb2_%d" % i, [128, D], BF16) for i in range(3)]
                    S.dma("sp", G2.ap[:], g2_d.partition_broadcast(128), writes=[G2])
                    norm_pipe([((lambda ti=ti: (x1t[ti], x1.ap[:, ti, :])), n, ti * 128) for ti, n in tiles], G2, hbs2, h2T)
                    S.barrier()
                ck(12)
                with contextlib.ExitStack() as s2:
                    wfu = [sb(s2, "wfu%d" % i, [128, 16, 512], BF16) for i in range(2)]
                    wfd = sb(s2, "wfd", [128, 4, D], BF16)
                    actT = sb(s2, "actT", [128, 4, 1040], BF16)
                    RL = sb(s2, "RL", [128, 512])
                    for f in range(16):
                        wu = wfu[f % 2]
                        S.dma("pool", wu.ap[:], w_fu_v[:, :, f * 512:(f + 1) * 512], writes=[wu])
                        S.dma("pool", wfd.ap[:], w_fd[f * 512:(f + 1) * 512, :].rearrange("(b p) d -> p b d", p=128), writes=[wfd])
                        for fb in range(4):
                            for (t0, t1) in groups3:
                                n = t1 - t0
                                P = ps()
                                for kc in range(16):
                                    PE(lambda e: e.matmul(P.ap[:, 0:n], lhsT=wu.ap[:, kc, fb * 128:(fb + 1) * 128], rhs=h2T.ap[:, kc, t0:t1], start=(kc == 0), stop=(kc == 15)), [wu, h2T], [P], inc=(kc == 15))
                                ACT(lambda e: e.activation(out=RL.ap[:, 0:n], in_=P.ap[:, 0:n], func=AF.Relu), [P], [RL])
                                DVE(lambda e: e.tensor_tensor(out=actT.ap[:, fb, t0:t1], in0=RL.ap[:, 0:n], in1=RL.ap[:, 0:n], op=ALU.mult), [RL], [actT])
                        for ti, n in tiles:
                            cols = slice(ti * 128, ti * 128 + n)
                            for dg in range(4):
                                ds_ = slice(dg * 512, (dg + 1) * 512)
                                P = ps()
                                for fb in range(4):
                                    PE(lambda e: e.matmul(P.ap[0:n, :], lhsT=actT.ap[:, fb, cols], rhs=wfd.ap[:, fb, ds_], start=(fb == 0), stop=(fb == 3)), [actT, wfd], [P], inc=(fb == 3))
                                DVE(lambda e: e.tensor_tensor(out=x1.ap[0:n, ti, ds_], in0=x1.ap[0:n, ti, ds_], in1=P.ap[0:n, :], op=ALU.add), [x1t[ti], P], [x1t[ti]])
                    S.barrier()
                with contextlib.ExitStack() as s3:
                    GF = sb(s3, "GF", [128, D])
                    yt = [sb(s3, "yt%d" % i, [128, D]) for i in range(3)]
                    SHT = sb(s3, "SHT", [17, 27 * 128])
                    S.dma("sp", GF.ap[:], gf_d.partition_broadcast(128), writes=[GF])
                    fst = {}
                    for step in range(len(tiles) + 1):
                        if step < len(tiles):
                            ti, n = tiles[step]
                            xa = x1.ap[:, ti, :]
                            y_ = yt[ti % 3]
                            ss_t = next_ss()
                            fst[step] = ss_t
                            DVE(lambda e: e.memset(ss_t.ap[0:n, 0:1], 0.0), [], [ss_t])
                            ACT(lambda e: e.activation(out=y_.ap[0:n, :], in_=xa[0:n, :], func=AF.Square, accum_out=ss_t.ap[0:n, 0:1]), [x1t[ti]], [y_, ss_t])
                        q = step - 1
                        if 0 <= q < len(tiles):
                            ti, n = tiles[q]
                            xa = x1.ap[:, ti, :]
                            y_ = yt[ti % 3]
                            ss_t = fst[q]
                            DVE(lambda e: e.tensor_scalar(out=ss_t.ap[0:n, 1:2], in0=ss_t.ap[0:n, 0:1], scalar1=1.0 / D, scalar2=RMS_EPS, op0=ALU.mult, op1=ALU.add), [ss_t], [ss_t])
                            ACT(lambda e: e.activation(out=ss_t.ap[0:n, 2:3], in_=ss_t.ap[0:n, 1:2], func=AF.Sqrt), [ss_t], [ss_t])
                            DVE(lambda e: e.reciprocal(out=ss_t.ap[0:n, 3:4], in_=ss_t.ap[0:n, 2:3]), [ss_t], [ss_t])
                            DVE(lambda e: e.scalar_tensor_tensor(out=y_.ap[0:n, :], in0=xa[0:n, :], scalar=ss_t.ap[0:n, 3:4], in1=GF.ap[0:n, :], op0=ALU.mult, op1=ALU.mult), [x1t[ti], ss_t, GF], [y_])
                            if ti < 8:
                                S.dma("sp", y_o[ti * 128:(ti + 1) * 128, :], y_.ap[:], reads=[y_], is_output=True)
                            else:
                                S.dma("sp", ysm_o, y_.ap[0:16, :], reads=[y_], is_output=True)
                    for b4 in range(7):
                        P = ps()
                        nb = min(4, 27 - b4 * 4)
                        for bb in range(nb):
                            blk = b4 * 4 + bb
                            PE(lambda e: e.transpose(out=P.ap[0:17, bb * 128:(bb + 1) * 128], in_=SH.ap[:, blk, :], identity=ident), [SH, CB], [P], inc=(bb == nb - 1))
                        ACT(lambda e: e.activation(out=SHT.ap[:, b4 * 512:b4 * 512 + nb * 128], in_=P.ap[0:17, 0:nb * 128], func=AF.Copy), [P], [SHT])
                    S.dma("sp", shp_o, SHT.ap[:], reads=[SHT], is_output=True)

        def run_tasks(tasks):
            done = set()
            running = []
            pending = list(tasks)
            while pending or running:
                for t in list(pending):
                    if all(d in done for d in t[2]):
                        pending.remove(t)
                        running.append((t[0], t[1](), t[3]))
                assert running, "task graph stuck"
                for r in list(running):
                    pool_sel[0] = r[2]
                    try:
                        next(r[1])
                    except StopIteration:
                        running.remove(r)
                        done.add(r[0])
                pool_sel[0] = None


        ck(1)
        for own in (False, True):
            NT = 1041 if own else 1024
            NM = 1040 if own else 1024
            NCH = 8
            groups = [(0, 512), (512, 1024)] + ([(1024, 1041)] if own else [])
            with contextlib.ExitStack() as stp:
                TW = sb(stp, "TW", [128, NT], BF16)
                aLo = sb(stp, "aLo", [128, NT], BF16)
                sgT1 = sb(stp, "sgT1", [128, NT], BF16)
                sgT2 = sb(stp, "sgT2", [128, NT], BF16)
                DVE(lambda e: e.memset(TW.ap[64:65, :], 1.0), [], [TW])
                shT = sb(stp, "shT", [128, 27 * 16])
                sth = contextlib.ExitStack()
                hT = sb(sth, "hT", [128, 16, NT], BF16)
                with contextlib.ExitStack() as sta:
                    G1 = sb(sta, "G1", [128, D])
                    S.dma("sp", G1.ap[:], g1_d.partition_broadcast(128), writes=[G1])
                    xts = [sb(sta, "xt%d" % i, [128, D]) for i in range(3)]
                    hbs = [sb(sta, "hb%d" % i, [128, D], BF16) for i in range(3)]
                    base = 1024 if own else 0
                    items = []
                    for i in range(8):
                        def ld(i=i):
                            xt = xts[i % 3]
                            S.dma("sp", xt.ap[:], xin[base + i * 128:base + (i + 1) * 128, :], writes=[xt])
                            return xt, xt.ap
                        items.append((ld, 128, i * 128))
                    if own:
                        def ld17():
                            xt = xts[8 % 3]
                            S.dma("sp", xt.ap[0:17, :], xsm, writes=[xt])
                            return xt, xt.ap
                        items.append((ld17, 17, 1024))
                    norm_pipe(items, G1, hbs, hT)
                    S.barrier()
                ck(7 if own else 2)
                if own:
                    S.dma("sp", shT.ap[:], shT_d, writes=[shT])

                def evac_copy(dst):
                    def f(P, t0, t1, dst=dst):
                        n = dst_np[0]
                        ACT(lambda e: e.activation(out=dst.ap[0:n, t0:t1], in_=P.ap[0:n, 0:t1 - t0], func=AF.Copy), [P], [dst])
                    return f
                dst_np = [128]

                def mix(p, d, blk, npart):
                    mu = VB.ap[0:npart, V_MU + blk:V_MU + blk + 1]
                    if own:
                        DVE(lambda e: e.tensor_copy(out=SH.ap[0:npart, blk, :], in_=p.ap[0:npart, 1023:1040]), [p], [SH])
                    DVE(lambda e: e.tensor_tensor(out=d.ap[0:npart, 1:1024], in0=p.ap[0:npart, 0:1023], in1=p.ap[0:npart, 1:1024], op=ALU.subtract), [p], [d])
                    if own:
                        DVE(lambda e: e.tensor_tensor(out=d.ap[0:npart, 0:1], in0=p.ap[0:npart, 1040:1041], in1=p.ap[0:npart, 0:1], op=ALU.subtract), [p], [d])
                        DVE(lambda e: e.tensor_tensor(out=d.ap[0:npart, 1024:1040], in0=shT.ap[0:npart, blk * 16:(blk + 1) * 16], in1=p.ap[0:npart, 1024:1040], op=ALU.subtract), [p, shT], [d])
                    else:
                        DVE(lambda e: e.tensor_scalar(out=d.ap[0:npart, 0:1], in0=p.ap[0:npart, 0:1], scalar1=-1.0, scalar2=None, op0=ALU.mult), [p], [d])
                    DVE(lambda e: e.scalar_tensor_tensor(out=d.ap[0:npart, 0:NM], in0=d.ap[0:npart, 0:NM], scalar=mu, in1=p.ap[0:npart, 0:NM], op0=ALU.mult, op1=ALU.add), [p, d, VB], [d])

                with contextlib.ExitStack() as stl:
                    wL = sb(stl, "wL", [128, 16, 288], BF16)
                    T0 = sb(stl, "La", [128, NT])
                    T1 = sb(stl, "Lb", [128, NT])
                    ncl = 288 if own else 128
                    S.dma("pool", wL.ap[:, :, 0:ncl], w_in_v[:, :, 3072:3072 + ncl], writes=[wL])
                    dst_np[0] = 128
                    proj(wL, 0, 128, hT, groups, evac_copy(T0))
                    mix(T0, T1, 24, 128)
                    ACT(lambda e: e.activation(out=TW.ap[0:64, 0:NM], in_=T1.ap[0:64, 0:NM], func=AF.Tanh), [T1], [TW])
                    ACT(lambda e: e.activation(out=aLo.ap[64:128, 0:NM], in_=T1.ap[64:128, 0:NM], func=AF.Copy), [T1], [aLo])
                    if own:
                        proj(wL, 128, 128, hT, groups, evac_copy(T0))
                        mix(T0, T1, 25, 128)
                        ACT(lambda e: e.activation(out=sgT1.ap[:, 0:NM], in_=T1.ap[:, 0:NM], func=AF.Sigmoid), [T1], [sgT1])
                        dst_np[0] = 32
                        proj(wL, 256, 32, hT, groups, evac_copy(T0))
                        mix(T0, T1, 26, 32)
                        ACT(lambda e: e.activation(out=sgT2.ap[0:32, 0:NM], in_=T1.ap[0:32, 0:NM], func=AF.Sigmoid), [T1], [sgT2])
                        dst_np[0] = 128
                    S.barrier()

                with contextlib.ExitStack() as stq:
                    wPs = [sb(stq, "wP%d" % i, [128, 16, 384], BF16) for i in range(2)]
                    RB = [sb(stq, "RB%d" % i, [128, NT]) for i in range(4)]
                    RM = [sb(stq, "RM%d" % i, [128, NT]) for i in range(3)]
                    rbi = [0]
                    for j in range(8):
                        wP = wPs[j % 2]
                        ncw = 384 if own else 256
                        S.dma("pool", wP.ap[:, :, 0:ncw], w_in_v[:, :, j * 384:j * 384 + ncw], writes=[wP])
                        for q in range(3 if own else 2):
                            rb = RB[rbi[0] % 4]
                            rm = RM[rbi[0] % 3]
                            rbi[0] += 1
                            proj(wP, q * 128, 128, hT, groups, evac_copy(rb))
                            mix(rb, rm, (8 + j, 16 + j, j)[q], 128)
                            S.dma("sp", PR_d[3 * j + q, :, 0:NM], rm.ap[:, 0:NM], reads=[rm], writes=[PRtk[3 * j + q]])
                    if own:
                        wSs = [sb(stq, "wS%d" % i, [128, 16, 512], BF16) for i in range(2)]
                        RU = [sb(stq, "RU%d" % i, [128, 1040], BF16) for i in range(2)]
                        for half, c0 in ((0, 4384), (1, 3360)):
                            for g4 in range(2):
                                wS = wSs[g4]
                                S.dma("pool", wS.ap[:], w_in_v[:, :, c0 + g4 * 512:c0 + (g4 + 1) * 512], writes=[wS])
                                for gg in range(4):
                                    g = g4 * 4 + gg
                                    if half == 0:
                                        rb = RB[rbi[0] % 4]
                                    else:
                                        rb = RU[rbi[0] % 2]
                                    rbi[0] += 1

                                    def evg(P, t0, t1, rb=rb):
                                        ACT(lambda e: e.activation(out=rb.ap[:, t0:t1], in_=P.ap[:, 0:t1 - t0], func=AF.Gelu), [P], [rb])
                                    proj(wS, gg * 128, 128, hT, groups3, evg)
                                    if half == 0:
                                        S.dma("sp", VZ_d[g], rb.ap[:, 0:1040], reads=[rb], writes=[VZtk])
                                    else:
                                        S.dma("sp", UZ_d[g], rb.ap[:, 0:1040], reads=[rb], writes=[UZtk])
                    S.barrier()
                sth.close()
                with contextlib.ExitStack() as stb:
                    TS = [sb(stb, "TS%d" % i, [128, NT]) for i in range(6)]
                    T0, T1, T2, T3, T5, T6 = TS
                    sqb = sb(stb, "sqb", [128, NT], BF16)
                    st16w = sb(stb, "st16w", [128, 16])
                    sgtok = sb(stb, "sgtok", [128, 8, 128])
                    Lf = sb(stb, "Lf", [128, 8, 258])
                    ED = sb(stb, "ED", [128, 8, 128])

                    class Ctx:
                        pass
                    bufs = []
                    for bi in range(2):
                        B = Ctx()
                        B.SC = sb(stb, "SC", [128, 16])
                        B.aT = sb(stb, "aT", [128, 1024], BF16)
                        B.bT = sb(stb, "bT", [128, 1024], BF16)
                        B.kT = sb(stb, "kT", [128, 1024], BF16)
                        B.rT = sb(stb, "rT", [128, 1024], BF16)
                        B.prodb = sb(stb, "prodb", [128, 1040], BF16)
                        B.Zar = sb(stb, "Zar", [128, 8, 4, 128], BF16)
                        B.Zb = sb(stb, "Zb", [128, 8, 2, 128], BF16)
                        DVE(lambda e: e.memset(B.Zar.ap[:], 0.0), [], [B.Zar])
                        DVE(lambda e: e.memset(B.Zb.ap[:], 0.0), [], [B.Zb])
                        B.Bhat = sb(stb, "Bhat", [128, 8, 128], BF16)
                        B.Khat = sb(stb, "Khat", [128, 8, 128], BF16)
                        B.Vtok = sb(stb, "Vtok", [128, 8, 128], BF16)
                        B.LGp = sb(stb, "LGp", [128, 128])
                        B.LBp = sb(stb, "LBp", [128, 128])
                        bufs.append(B)
                    ctxs = []
                    for ci in range(4):
                        cx = Ctx()
                        cx.ATm = sb(stb, "ATm", [128, 4, 256], BF16)
                        cx.AKm = sb(stb, "AKm", [128, 4, 256], BF16)
                        cx.Am = sb(stb, "Am", [128, 4, 128], BF16)
                        cx.An = [sb(stb, "An%d" % i, [128, 4, 128], BF16) for i in range(2)]
                        cx.Bn = [sb(stb, "Bn%d" % i, [128, 4, 128], BF16) for i in range(2)]
                        cx.Pt = [sb(stb, "Pt%d" % i, [128, 4, 128], BF16) for i in range(2)]
                        ctxs.append(cx)
                    Yb = sb(stb, "Yb", [128, 128], BF16)
                    Ub = sb(stb, "Ub", [128, 128], BF16)
                    Otok = sb(stb, "Otok", [128, 8, 128])
                    Q1 = sb(stb, "Q1", [128, 8, 128])
                    Q2 = sb(stb, "Q2", [128, 8, 128])
                    st16 = sb(stb, "st16", [128, 6, 16])
                    yatok = sb(stb, "yatok", [128, 8, 128], BF16)
                    yaT = sb(stb, "yaT", [128, 1040], BF16)
                    print("SBUF free in pairs scope", nc.sbuf_bytes_remaining)

                    def amats(cx, cp, j, B):
                        NW = 512 if own else 256
                        mstr = CB.ap[:, O_M4:O_M4 + 128].unsqueeze(1).to_broadcast([128, 2, 128])
                        minc = CB.ap[:, O_M4 + 128:O_M4 + 256].unsqueeze(1).to_broadcast([128, 2, 128])
                        for cl in range(2):
                            c = cp * 2 + cl
                            tcs = slice(c * 128, (c + 1) * 128)
                            us = slice(cl * 2, cl * 2 + 2)
                            PAT, PAK, PA = ps(), ps(), ps()
                            zr = B.Zar.ap[:, c, :, :].rearrange("p a t -> p (a t)")[:, 0:NW]
                            PE(lambda e: e.matmul(PAT.ap[:, 0:NW], lhsT=B.bT.ap[:, tcs], rhs=zr, start=True, stop=True), [B.bT, B.Zar], [PAT])
                            PE(lambda e: e.matmul(PAK.ap[:, 0:NW], lhsT=B.kT.ap[:, tcs], rhs=zr, start=True, stop=True), [B.kT, B.Zar], [PAK])
                            PE(lambda e: e.matmul(PA.ap[:, 0:256], lhsT=B.aT.ap[:, tcs], rhs=B.Zb.ap[:, c, :, :].rearrange("p a t -> p (a t)"), start=True, stop=True), [B.aT, B.Zb], [PA])
                            DVE(lambda e: e.tensor_tensor(out=cx.ATm.ap[:, us, 0:128], in0=PAT.ap[:, 0:256].rearrange("p (u t) -> p u t", u=2), in1=mstr, op=ALU.mult), [PAT, CB], [cx.ATm])
                            DVE(lambda e: e.tensor_tensor(out=cx.AKm.ap[:, us, 0:128], in0=PAK.ap[:, 0:256].rearrange("p (u t) -> p u t", u=2), in1=mstr, op=ALU.mult), [PAK, CB], [cx.AKm])
                            if own:
                                DVE(lambda e: e.tensor_tensor(out=cx.ATm.ap[:, us, 128:256], in0=PAT.ap[:, 256:512].rearrange("p (u t) -> p u t", u=2), in1=minc, op=ALU.mult), [PAT, CB], [cx.ATm])
                                DVE(lambda e: e.tensor_tensor(out=cx.AKm.ap[:, us, 128:256], in0=PAK.ap[:, 256:512].rearrange("p (u t) -> p u t", u=2), in1=minc, op=ALU.mult), [PAK, CB], [cx.AKm])
                            DVE(lambda e: e.tensor_tensor(out=cx.Am.ap[:, us, :], in0=PA.ap[:, 0:256].rearrange("p (u t) -> p u t", u=2), in1=CB.ap[:, O_MA:O_MA + 256].rearrange("p (u t) -> p u t", u=2), op=ALU.mult), [PA, CB], [cx.Am])
                        DVE(lambda e: e.tensor_tensor(out=cx.Pt[0].ap[:], in0=cx.ATm.ap[:, :, 0:128], in1=IDB.ap[:].unsqueeze(1).to_broadcast([128, 4, 128]), op=ALU.add), [cx.ATm, IDB], [cx.Pt[0]])
                        cx.Acur, cx.Bcur, cx.Pcur = cx.Am, cx.ATm, cx.Pt[0]
                        cx.Bcur_ap = cx.ATm.ap[:, :, 0:128]

                    def inverse(cxs):
                        for lvl in range(1, 7):
                            for cx in cxs:
                                cx.PAn = ps()
                                for u in range(4):
                                    PE(lambda e: e.matmul(cx.PAn.ap[:, u * 128:(u + 1) * 128], lhsT=cx.Bcur_ap[:, u, :], rhs=cx.Acur.ap[:, u, :], start=True, stop=True), [cx.Bcur, cx.Acur], [cx.PAn], inc=(u == 3))
                                if lvl < 6:
                                    cx.PBn = ps()
                                    for u in range(4):
                                        PE(lambda e: e.matmul(cx.PBn.ap[:, u * 128:(u + 1) * 128], lhsT=cx.Acur.ap[:, u, :], rhs=cx.Bcur_ap[:, u, :], start=True, stop=True), [cx.Bcur, cx.Acur], [cx.PBn], inc=(u == 3))
                            yield
                            for cx in cxs:
                                cx.Anew = cx.An[lvl % 2]
                                ACT(lambda e: e.activation(out=cx.Anew.ap[:], in_=cx.PAn.ap[:].rearrange("p (u t) -> p u t", u=4), func=AF.Copy), [cx.PAn], [cx.Anew])
                                if lvl < 6:
                                    cx.Bnew = cx.Bn[lvl % 2]
                                    ACT(lambda e: e.activation(out=cx.Bnew.ap[:], in_=cx.PBn.ap[:].rearrange("p (u t) -> p u t", u=4), func=AF.Copy), [cx.PBn], [cx.Bnew])
                            for cx in cxs:
                                cx.PP = ps()
                                for u in range(4):
                                    PE(lambda e: e.matmul(cx.PP.ap[:, u * 128:(u + 1) * 128], lhsT=cx.Anew.ap[:, u, :], rhs=cx.Pcur.ap[:, u, :], start=True, stop=True), [cx.Anew, cx.Pcur], [cx.PP], inc=(u == 3))
                            yield
                            for cx in cxs:
                                Pnew = cx.Pt[lvl % 2]
                                DVE(lambda e: e.tensor_tensor(out=Pnew.ap[:], in0=cx.PP.ap[:].rearrange("p (u t) -> p u t", u=4), in1=cx.Pcur.ap[:], op=ALU.add), [cx.PP, cx.Pcur], [Pnew])
                                cx.Acur, cx.Pcur = cx.Anew, Pnew
                                if lvl < 6:
                                    cx.Bcur = cx.Bnew
                                    cx.Bcur_ap = cx.Bnew.ap[:]

                    def chain(cx, cp, j, B):
                        for cl in range(2):
                            c = cp * 2 + cl
                            tcs = slice(c * 128, (c + 1) * 128)
                            PY = ps()
                            for h in range(2):
                                u = cl * 2 + h
                                hp = slice(h * 64, (h + 1) * 64)
                                PE(lambda e: e.matmul(PY.ap[:, hp], lhsT=B.aT.ap[:, tcs], rhs=SBF[j].ap[:, hp], start=True, stop=False), [B.aT, SBF[j]], [PY], inc=False)
                                PE(lambda e: e.matmul(PY.ap[:, hp], lhsT=cx.AKm.ap[:, u, 0:128], rhs=B.Vtok.ap[:, c, hp], start=False, stop=True), [cx.AKm, B.Vtok], [PY], inc=(h == 1))
                            ACT(lambda e: e.activation(out=Yb.ap[:], in_=PY.ap[:, 0:128], func=AF.Copy), [PY], [Yb])
                            PU = ps()
                            for h in range(2):
                                u = cl * 2 + h
                                hp = slice(h * 64, (h + 1) * 64)
                                PE(lambda e: e.matmul(PU.ap[:, hp], lhsT=cx.Pcur.ap[:, u, :], rhs=Yb.ap[:, hp], start=True, stop=True), [cx.Pcur, Yb], [PU], inc=(h == 1))
                            ACT(lambda e: e.activation(out=Ub.ap[:], in_=PU.ap[:, 0:128], func=AF.Copy), [PU], [Ub])
                            if own:
                                PO = ps()
                                for h in range(2):
                                    u = cl * 2 + h
                                    hp = slice(h * 64, (h + 1) * 64)
                                    PE(lambda e: e.matmul(PO.ap[:, hp], lhsT=B.rT.ap[:, tcs], rhs=SBF[j].ap[:, hp], start=True, stop=False), [B.rT, SBF[j]], [PO], inc=False)
                                    PE(lambda e: e.matmul(PO.ap[:, hp], lhsT=cx.ATm.ap[:, u, 128:256], rhs=Ub.ap[:, hp], start=False, stop=False), [cx.ATm, Ub], [PO], inc=False)
                                    PE(lambda e: e.matmul(PO.ap[:, hp], lhsT=cx.AKm.ap[:, u, 128:256], rhs=B.Vtok.ap[:, c, hp], start=False, stop=True), [cx.AKm, B.Vtok], [PO], inc=(h == 1))
                                ACT(lambda e: e.activation(out=Otok.ap[:, c, :], in_=PO.ap[:, 0:128], func=AF.Copy), [PO], [Otok])
                            PSn = ps()
                            for h in range(2):
                                hp = slice(h * 64, (h + 1) * 64)
                                PE(lambda e: e.matmul(PSn.ap[:, hp], lhsT=B.Bhat.ap[:, c, :], rhs=Ub.ap[:, hp], start=True, stop=False), [B.Bhat, Ub], [PSn], inc=False)
                                PE(lambda e: e.matmul(PSn.ap[:, hp], lhsT=B.Khat.ap[:, c, :], rhs=B.Vtok.ap[:, c, hp], start=False, stop=True), [B.Khat, B.Vtok], [PSn], inc=(h == 1))
                            DVE(lambda e: e.scalar_tensor_tensor(out=SST[j].ap[:], in0=SST[j].ap[:], scalar=B.SC.ap[:, c:c + 1], in1=PSn.ap[:, 0:128], op0=ALU.mult, op1=ALU.add), [SST[j], B.SC, PSn], [SST[j]])
                            DVE(lambda e: e.tensor_tensor(out=SBF[j].ap[:], in0=SST[j].ap[:], in1=CB.ap[:, O_BO:O_BO + 128], op=ALU.mult), [SST[j], CB], [SBF[j]])
                            yield


                    def prep(j, B):
                        jc = slice(j * 128, (j + 1) * 128)
                        kkc = VB.ap[:, V_KK + j:V_KK + j + 1]
                        kac = VB.ap[:, V_KA + j:V_KA + j + 1]
                        rkc = VB.ap[:, V_RK + j:V_RK + j + 1]
                        a0c = VB.ap[:, V_A0 + j:V_A0 + j + 1]
                        S.dma("sp", T1.ap[:, 0:NM], PR_d[3 * j, :, 0:NM], reads=[PRtk[3 * j]], writes=[T1])
                        S.dma("sp", T6.ap[:, 0:NM], PR_d[3 * j + 1, :, 0:NM], reads=[PRtk[3 * j + 1]], writes=[T6])
                        if own:
                            S.dma("sp", T5.ap[:, 0:NM], PR_d[3 * j + 2, :, 0:NM], reads=[PRtk[3 * j + 2]], writes=[T5])
                            S.dma("sp", B.LGp.ap[:], lg_d[:, jc].partition_broadcast(128), writes=[B.LGp])
                            S.dma("sp", B.LBp.ap[:], lb_d[:, jc].partition_broadcast(128), writes=[B.LBp])
                        yield
                        for c4 in range(2):
                            P = ps()
                            for cc in range(4):
                                c = c4 * 4 + cc
                                PE(lambda e: e.matmul(P.ap[:, cc * 128:(cc + 1) * 128], lhsT=TW.ap[0:65, c * 128:(c + 1) * 128], rhs=wupa.ap[0:65, jc], start=True, stop=True), [TW, wupa], [P], inc=(cc == 3))
                            ACT(lambda e: e.activation(out=sgtok.ap[:, c4 * 4:(c4 + 1) * 4, :], in_=P.ap[:].rearrange("p (c t) -> p c t", c=4), func=AF.Sigmoid), [P], [sgtok])
                        yield
                        if own:
                            P = ps()
                            PE(lambda e: e.matmul(P.ap[:, 0:16], lhsT=wupa.ap[0:65, jc], rhs=TW.ap[0:65, 1024:1040], start=True, stop=True), [TW, wupa], [P])
                            ACT(lambda e: e.activation(out=st16w.ap[:, 0:16], in_=P.ap[:, 0:16], func=AF.Sigmoid), [P], [st16w])
                            for hh in range(2):
                                hq = slice(hh * 64, (hh + 1) * 64)
                                ACT(lambda e: e.activation(out=SMZ.ap[hq, j, 4, hh * 16:(hh + 1) * 16], in_=st16w.ap[hq, 0:16], func=AF.Exp, scale=-C_DEC), [st16w], [SMZ])
                        yield
                        for (t0, t1) in groups:
                            t1 = min(t1, NM)
                            P = ps()
                            PE(lambda e: e.matmul(P.ap[:, 0:t1 - t0], lhsT=aup.ap[64:128, jc], rhs=aLo.ap[64:128, t0:t1], start=True, stop=True), [aup, aLo], [P])
                            ACT(lambda e: e.activation(out=T2.ap[:, t0:t1], in_=P.ap[:, 0:t1 - t0], func=AF.Sigmoid, bias=a0c), [P, VB], [T2])
                        yield
                        yield
                        for c in range(NCH):
                            P = ps()
                            PE(lambda e: e.matmul(P.ap[:, 0:258], lhsT=sgtok.ap[:, c, :], rhs=CB.ap[:, O_CM:O_CM + 258], start=True, stop=True), [sgtok, CB], [P])
                            ACT(lambda e: e.activation(out=Lf.ap[:, c, :], in_=P.ap[:, 0:258], func=AF.Copy), [P], [Lf])
                        yield
                        for c4 in range(2):
                            P = ps()
                            for cc in range(4):
                                c = c4 * 4 + cc
                                last = (c == NCH - 1)
                                PE(lambda e: e.matmul(P.ap[:, cc * 128:(cc + 1) * 128], lhsT=CB.ap[:, O_T2:O_T2 + 128], rhs=sgtok.ap[:, c, :], start=True, stop=last), [sgtok, CB], [P], inc=last)
                                if not last:
                                    PE(lambda e: e.matmul(P.ap[:, cc * 128:(cc + 1) * 128], lhsT=CB.ap[:, O_F64:O_F64 + 128], rhs=sgtok.ap[:, c + 1, :], start=False, stop=True), [sgtok, CB], [P], inc=True)
                            ACT(lambda e: e.activation(out=ED.ap[:, c4 * 4:(c4 + 1) * 4, :], in_=P.ap[:].rearrange("p (c t) -> p c t", c=4), func=AF.Exp), [P], [ED])
                        yield
                        yield
                        DVE(lambda e: e.tensor_tensor(out=B.SC.ap[:, 0:NCH - 1].unsqueeze(2), in0=Lf.ap[:, 0:NCH - 1, 256:257], in1=Lf.ap[:, 1:NCH, 257:258], op=ALU.add), [Lf], [B.SC])
                        yield
                        DVE(lambda e: e.tensor_copy(out=B.SC.ap[:, NCH - 1:NCH], in_=Lf.ap[:, NCH - 1, 256:257]), [Lf], [B.SC])
                        yield
                        DVE(lambda e: e.tensor_copy(out=B.SC.ap[:, 8:9], in_=Lf.ap[:, 0, 257:258]), [Lf], [B.SC])
                        yield
                        ACT(lambda e: e.activation(out=B.SC.ap[:, 0:9], in_=B.SC.ap[:, 0:9], func=AF.Exp), [B.SC], [B.SC])
                        yield
                        DVE(lambda e: e.tensor_scalar(out=SST[j].ap[:], in0=SST[j].ap[:], scalar1=B.SC.ap[:, 8:9], scalar2=None, op0=ALU.mult), [SST[j], B.SC], [SST[j]])
                        yield
                        DVE(lambda e: e.tensor_tensor(out=SBF[j].ap[:], in0=SST[j].ap[:], in1=CB.ap[:, O_BO:O_BO + 128], op=ALU.mult), [SST[j], CB], [SBF[j]])
                        yield
                        yield
                        yield
                        ACT(lambda e: e.activation(out=sqb.ap[:, 0:NM], in_=T1.ap[:, 0:NM], func=AF.Square, scale=kkc), [T1, VB], [sqb])
                        yield
                        for (t0, t1) in groups:
                            t1 = min(t1, NM)
                            P = ps()
                            PE(lambda e: e.matmul(P.ap[:, 0:t1 - t0], lhsT=BOB.ap[:], rhs=sqb.ap[:, t0:t1], start=True, stop=True), [BOB, sqb], [P])
                            DVE(lambda e: e.tensor_scalar(out=T0.ap[:, t0:t1], in0=P.ap[:, 0:t1 - t0], scalar1=1e-24, scalar2=None, op0=ALU.max), [P], [T0])
                        yield
                        ACT(lambda e: e.activation(out=T0.ap[:, 0:NM], in_=T0.ap[:, 0:NM], func=AF.Sqrt), [T0], [T0])
                        yield
                        ACT(lambda e: e.activation(out=T0.ap[:, 0:NM], in_=T0.ap[:, 0:NM], func=AF.Ln), [T0], [T0])
                        yield
                        ACT(lambda e: e.activation(out=T0.ap[:, 0:NM], in_=T0.ap[:, 0:NM], func=AF.Exp, scale=-1.0), [T0], [T0])
                        yield
                        DVE(lambda e: e.scalar_tensor_tensor(out=T0.ap[:, 0:NM], in0=T1.ap[:, 0:NM], scalar=kkc, in1=T0.ap[:, 0:NM], op0=ALU.mult, op1=ALU.mult), [T0, T1, VB], [T0])
                        yield
                        DVE(lambda e: e.tensor_tensor(out=T3.ap[:, 0:NM], in0=T0.ap[:, 0:NM], in1=T2.ap[:, 0:NM], op=ALU.mult), [T0, T2], [T3])
                        yield
                        DVE(lambda e: e.tensor_scalar(out=T2.ap[:, 0:NM], in0=T2.ap[:, 0:NM], scalar1=-1.0, scalar2=kac, op0=ALU.add, op1=ALU.mult), [T2, VB], [T2])
                        yield
                        DVE(lambda e: e.scalar_tensor_tensor(out=T2.ap[:, 0:NM], in0=T2.ap[:, 0:NM], scalar=1.0, in1=T1.ap[:, 0:NM], op0=ALU.add, op1=ALU.mult), [T2, T1], [T2])
                        yield
                        T1v = T1.ap[:, 0:1024].rearrange("p (c t) -> p c t", c=8)
                        yield
                        ACT(lambda e: e.activation(out=T1v, in_=Lf.ap[:, :, 128:256], func=AF.Exp), [Lf], [T1])
                        yield
                        DVE(lambda e: e.scalar_tensor_tensor(out=B.aT.ap[:], in0=T0.ap[:, 0:1024], scalar=-1.0, in1=T1.ap[:, 0:1024], op0=ALU.mult, op1=ALU.mult), [T0, T1], [B.aT])
                        yield
                        ACT(lambda e: e.activation(out=T1v, in_=Lf.ap[:, :, 0:128], func=AF.Exp, scale=-1.0), [Lf], [T1])
                        yield
                        DVE(lambda e: e.tensor_tensor(out=B.bT.ap[:], in0=T3.ap[:, 0:1024], in1=T1.ap[:, 0:1024], op=ALU.mult), [T3, T1], [B.bT])
                        yield
                        DVE(lambda e: e.tensor_tensor(out=B.kT.ap[:], in0=T2.ap[:, 0:1024], in1=T1.ap[:, 0:1024], op=ALU.mult), [T2, T1], [B.kT])
                        yield
                        for hh in range(2):
                            hq = slice(hh * 64, (hh + 1) * 64)
                            ACT(lambda e: e.activation(out=B.Zar.ap[hq, :, hh, :], in_=B.aT.ap[hq, :].rearrange("p (c t) -> p c t", c=8), func=AF.Copy), [B.aT], [B.Zar])
                            ACT(lambda e: e.activation(out=B.Zb.ap[hq, :, hh, :], in_=B.bT.ap[hq, :].rearrange("p (c t) -> p c t", c=8), func=AF.Copy), [B.bT], [B.Zb])
                        yield
                        if own:
                            ACT(lambda e: e.activation(out=T1v, in_=Lf.ap[:, :, 0:128], func=AF.Exp), [Lf], [T1])
                            DVE(lambda e: e.tensor_tensor(out=B.rT.ap[:], in0=T5.ap[:, 0:1024], in1=T1.ap[:, 0:1024], op=ALU.mult), [T5, T1], [B.rT])
                            for hh in range(2):
                                hq = slice(hh * 64, (hh + 1) * 64)
                                ACT(lambda e: e.activation(out=B.Zar.ap[hq, :, 2 + hh, :], in_=B.rT.ap[hq, :].rearrange("p (c t) -> p c t", c=8), func=AF.Copy), [B.rT], [B.Zar])
                            DVE(lambda e: e.scalar_tensor_tensor(out=B.prodb.ap[:, 0:1040], in0=T5.ap[:, 0:1040], scalar=rkc, in1=T2.ap[:, 0:1040], op0=ALU.mult, op1=ALU.mult), [T5, T2, VB], [B.prodb])
                        yield
                        yield
                        for c4 in range(2):
                            c4s = slice(c4 * 4, (c4 + 1) * 4)
                            for (src, kind) in ((T3, 0), (T2, 1), (T6, 2)):
                                Pq_ = ps()
                                for cc in range(4):
                                    c = c4 * 4 + cc
                                    tcs = slice(c * 128, (c + 1) * 128)
                                    PE(lambda e: e.transpose(out=Pq_.ap[:, cc * 128:(cc + 1) * 128], in_=src.ap[:, tcs], identity=ident), [src, CB], [Pq_], inc=(cc == 3))
                                pv = Pq_.ap[:].rearrange("p (c t) -> p c t", c=4)
                                if kind == 0:
                                    DVE(lambda e: e.tensor_tensor(out=B.Bhat.ap[:, c4s, :], in0=pv, in1=ED.ap[:, c4s, :], op=ALU.mult), [Pq_, ED], [B.Bhat])
                                elif kind == 1:
                                    DVE(lambda e: e.tensor_tensor(out=B.Khat.ap[:, c4s, :], in0=pv, in1=ED.ap[:, c4s, :], op=ALU.mult), [Pq_, ED], [B.Khat])
                                else:
                                    ACT(lambda e: e.activation(out=B.Vtok.ap[:, c4s, :], in_=pv, func=AF.Copy), [Pq_], [B.Vtok])
                            yield
                        yield
                        if own:
                            for hh in range(2):
                                hq = slice(hh * 64, (hh + 1) * 64)
                                zc = slice(hh * 16, (hh + 1) * 16)
                                DVE(lambda e: e.tensor_scalar(out=SMZ.ap[hq, j, 0, zc], in0=T0.ap[hq, 1024:1040], scalar1=-1.0, scalar2=None, op0=ALU.mult), [T0], [SMZ])
                                DVE(lambda e: e.tensor_copy(out=SMZ.ap[hq, j, 1, zc], in_=T3.ap[hq, 1024:1040]), [T3], [SMZ])
                                DVE(lambda e: e.tensor_copy(out=SMZ.ap[hq, j, 2, zc], in_=T2.ap[hq, 1024:1040]), [T2], [SMZ])
                                DVE(lambda e: e.tensor_copy(out=SMZ.ap[hq, j, 3, zc], in_=T5.ap[hq, 1024:1040]), [T5], [SMZ])
                            DVE(lambda e: e.tensor_copy(out=SMv.ap[:, j, :], in_=T6.ap[:, 1024:1040]), [T6], [SMv])
                            P = ps()
                            PE(lambda e: e.matmul(P.ap[0:16, 0:2], lhsT=B.prodb.ap[:, 1024:1040], rhs=BSB.ap[:], start=True, stop=True), [B.prodb, BSB], [P])
                            ACT(lambda e: e.activation(out=RKS.ap[0:16, 2 * j:2 * j + 2], in_=P.ap[0:16, 0:2], func=AF.Copy), [P], [RKS])

                    def Xg(j, g, B):
                        cxa, cxb = ctxs[2 * g], ctxs[2 * g + 1]
                        amats(cxa, 2 * g, j, B)
                        yield
                        amats(cxb, 2 * g + 1, j, B)
                        yield
                        for _ in inverse([cxa, cxb]):
                            yield

                    def Cg(j, g, B):
                        for _ in chain(ctxs[2 * g], 2 * g, j, B):
                            yield
                        for _ in chain(ctxs[2 * g + 1], 2 * g + 1, j, B):
                            yield

                    def postg(j, B):
                        jc = slice(j * 128, (j + 1) * 128)
                        yield
                        if own:
                            yield
                            P = ps()
                            PE(lambda e: e.transpose(out=P.ap[:, 0:128], in_=SST[j].ap[:], identity=ident), [SST[j], CB], [P])
                            ACT(lambda e: e.activation(out=Q1.ap[:, 0, :], in_=P.ap[:, 0:128], func=AF.Copy), [P], [Q1])
                            for h in range(2):
                                hp = slice(h * 64, (h + 1) * 64)
                                S.dma("sp", wkvp_o[2 * j + h], Q1.ap[hp, 0, hp], reads=[Q1], is_output=True)
                            post(128, 8, 2, Otok, Otok.ap[:].rearrange("p c (h v) -> p (c h) v", h=2), None,
                                 Q1, Q1.ap[:].rearrange("p c (h v) -> p (c h) v", h=2), Q1.ap[:].rearrange("p c (h v) -> p c h v", h=2),
                                 Q2, Q2.ap[:].rearrange("p c (h v) -> p (c h) v", h=2), st16,
                                 B.LGp.ap[:].rearrange("p (h v) -> p h v", h=2).unsqueeze(1).to_broadcast([128, 8, 2, 64]),
                                 B.LBp.ap[:].rearrange("p (h v) -> p h v", h=2).unsqueeze(1).to_broadcast([128, 8, 2, 64]), B.LGp, B.LBp)
                            P = ps()
                            for c in range(8):
                                PE(lambda e: e.matmul(P.ap[:, 2 * c:2 * c + 2], lhsT=B.prodb.ap[:, c * 128:(c + 1) * 128], rhs=BSB.ap[:], start=True, stop=True), [B.prodb, BSB], [P], inc=(c == 7))
                            ACT(lambda e: e.activation(out=st16.ap[:, 5, :], in_=P.ap[:, 0:16], func=AF.Copy), [P], [st16])
                            Q1v = Q1.ap[:].rearrange("p c (h v) -> p (c h) v", h=2)
                            Q2v = Q2.ap[:].rearrange("p c (h v) -> p (c h) v", h=2)
                            DVE(lambda e: e.tensor_tensor(out=Q2v, in0=B.Vtok.ap[:].rearrange("p c (h v) -> p (c h) v", h=2), in1=st16.ap[:, 5, :].unsqueeze(2).to_broadcast([128, 16, 64]), op=ALU.mult), [B.Vtok, st16], [Q2])
                            DVE(lambda e: e.tensor_tensor(out=Q1.ap[:], in0=Q1.ap[:], in1=Q2.ap[:], op=ALU.add), [Q1, Q2], [Q1])
                            for c4 in range(2):
                                P = ps()
                                for cc in range(4):
                                    c = c4 * 4 + cc
                                    tcs = slice(c * 128, (c + 1) * 128)
                                    PE(lambda e: e.matmul(P.ap[:, cc * 128:(cc + 1) * 128], lhsT=sgT1.ap[:, tcs], rhs=gup1.ap[:, jc], start=True, stop=False), [sgT1, gup1], [P], inc=False)
                                    PE(lambda e: e.matmul(P.ap[:, cc * 128:(cc + 1) * 128], lhsT=sgT2.ap[0:32, tcs], rhs=gup2.ap[0:32, jc], start=False, stop=True), [sgT2, gup2], [P], inc=(cc == 3))
                                c4s = slice(c4 * 4, (c4 + 1) * 4)
                                DVE(lambda e: e.tensor_tensor(out=yatok.ap[:, c4s, :], in0=P.ap[:].rearrange("p (c t) -> p c t", c=4), in1=Q1.ap[:, c4s, :], op=ALU.mult), [P, Q1], [yatok])
                            P = ps()
                            pb = P.ap[:].bitcast(BF16)
                            for c in range(8):
                                PE(lambda e: e.transpose(out=pb[:, c * 128:(c + 1) * 128], in_=yatok.ap[:, c, :], identity=IDB.ap[:]), [yatok, IDB], [P], inc=(c == 7))
                            ACT(lambda e: e.activation(out=yaT.ap[:, 0:1024], in_=pb, func=AF.Copy), [P], [yaT])
                            S.dma("sp", YT_d[j, :, 0:1024], yaT.ap[:, 0:1024], reads=[yaT], writes=[YTk])

                    tasks = []
                    PX, PC, PP_ = (0, 4), (4, 2), (6, 2)
                    for j in range(8):
                        B = bufs[j % 2]
                        tasks.append((("P", j), (lambda j=j, B=B: prep(j, B)), [("P", j - 1)] * (j > 0) + [("O", j - 2)] * (j > 1), PP_))
                        tasks.append((("X", j, 0), (lambda j=j, B=B: Xg(j, 0, B)), [("P", j)] + [("C", j - 1, 0)] * (j > 0), PX))
                        tasks.append((("X", j, 1), (lambda j=j, B=B: Xg(j, 1, B)), [("X", j, 0)] + [("C", j - 1, 1)] * (j > 0), PX))
                        tasks.append((("C", j, 0), (lambda j=j, B=B: Cg(j, 0, B)), [("X", j, 0)] + [("O", j - 1)] * (j > 0), PC))
                        tasks.append((("C", j, 1), (lambda j=j, B=B: Cg(j, 1, B)), [("X", j, 1), ("C", j, 0)], PC))
                        tasks.append((("O", j), (lambda j=j, B=B: postg(j, B)), [("C", j, 1)], PC))
                    run_tasks(tasks)
                    S.barrier()
                    ck(9 if own else 6)
                if own:
                    sts_ = contextlib.ExitStack()
                    stg_ = contextlib.ExitStack()
                    run_tasks([("S", (lambda: sample_gen(sts_, (sgT1, sgT2, gup1, gup2))), [], (0, 4)),
                               ("G", (lambda: sgu_gen(stg_)), [], (4, 4))])
                    S.barrier()
                    stg_.close()
                    sts_.close()
                    ck(11)
                S.barrier()
        rest_phase()
        S.stopped = False
        S.finish("sp")
        print("instructions", S.n_ins, "waits", S.n_wait)
    return nc


_NC_CACHE = {}


def _perm_w_in(w):
    idx = []
    for j in range(8):
        idx += list(range(1024 + j * 128, 1024 + (j + 1) * 128))
        idx += list(range(2048 + j * 128, 2048 + (j + 1) * 128))
        idx += list(range(j * 128, (j + 1) * 128))
    idx += list(range(3072, 5408))
    return np.ascontiguousarray(w[:, np.asarray(idx)])


def kernel(**inp):
    f32 = np.float32
    xp = np.asarray(inp["x_prompt"], f32)
    xs = np.asarray(inp["x_sample"], f32)
    swkv = np.asarray(inp["state_wkv"], f32)
    ssh = np.asarray(inp["state_shift"], f32)
    if "nc" not in _NC_CACHE:
        _NC_CACHE["nc"] = build()
    nc = _NC_CACHE["nc"]
    cb = make_consts()
    vb = np.zeros((128, NVB), f32)
    mu = np.zeros(27 * 128, f32)
    mu[:DSH] = np.asarray(inp["mu_shift"], f32)[0]
    vb[:, V_MU:V_MU + 27] = mu.reshape(27, 128).T
    vb[:, V_KK:V_KK + 8] = np.asarray(inp["k_k"], f32)[0].reshape(8, 128).T
    vb[:, V_KA:V_KA + 8] = np.asarray(inp["k_a"], f32)[0].reshape(8, 128).T
    vb[:, V_RK:V_RK + 8] = np.asarray(inp["r_k"], f32)[0].reshape(8, 128).T
    vb[:, V_A0:V_A0 + 8] = np.asarray(inp["a0"], f32)[0].reshape(8, 128).T
    vb[:, V_SG:V_SG + 8] = np.asarray(inp["sgu_norm_g"], f32)[0].reshape(8, 128).T
    vb[:, V_SB:V_SB + 8] = np.asarray(inp["sgu_norm_b"], f32)[0].reshape(8, 128).T
    vb[:, V_W00:V_W00 + 8] = np.asarray(inp["sgu_w"], f32)[0][:, 0, 0][None, :]
    vb[:, V_B0:V_B0 + 8] = np.asarray(inp["sgu_b"], f32)[0][:, 0][None, :]
    wupa = np.concatenate([np.asarray(inp["w_up"], f32)[0], np.asarray(inp["w0"], f32)[0][None, :]], 0)
    common = {
        "w_in": _perm_w_in(np.asarray(inp["w_in"], f32)[0]),
        "w_out": np.ascontiguousarray(np.asarray(inp["w_out"], f32)[0]),
        "w_fu": np.ascontiguousarray(np.asarray(inp["w_ffn_up"], f32)[0]),
        "w_fd": np.ascontiguousarray(np.asarray(inp["w_ffn_down"], f32)[0]),
        "wupa": np.ascontiguousarray(wupa),
        "aup": np.ascontiguousarray(np.asarray(inp["a_up"], f32)[0]),
        "gup": np.ascontiguousarray(np.asarray(inp["g_up"], f32)[0]),
        "g1": np.ascontiguousarray(np.asarray(inp["norm1_g"], f32).reshape(1, D)),
        "g2": np.ascontiguousarray(np.asarray(inp["norm2_g"], f32).reshape(1, D)),
        "gf": np.ascontiguousarray(np.asarray(inp["norm_f_g"], f32).reshape(1, D)),
        "lg": np.ascontiguousarray(np.asarray(inp["lnx_g"], f32).reshape(1, 1024)),
        "lb": np.ascontiguousarray(np.asarray(inp["lnx_b"], f32).reshape(1, 1024)),
        "sgb": np.ascontiguousarray(np.asarray(inp["sgu_b"], f32).reshape(1, 1024)),
        "sgw": np.ascontiguousarray(np.asarray(inp["sgu_w"], f32)[0]),
        "cb": cb,
        "sel2": make_sel2(),
        "vb": vb,
    }
    in_maps = []
    for c in range(8):
        b, half = divmod(c, 2)
        xin = np.zeros((2048, D), f32)
        xsm = np.zeros((17, D), f32)
        if half == 1:
            xin[:] = xp[b]
            xsm[16] = xp[b, 1023]
        else:
            xin[1024:] = xp[b, :1024]
        xsm[:16] = xs[16 * c:16 * (c + 1), 0]
        shpad = np.zeros((16, 27 * 128), f32)
        shpad[:, :DSH] = ssh[0, 16 * c:16 * (c + 1)]
        shT = np.ascontiguousarray(shpad.reshape(16, 27, 128).transpose(2, 1, 0).reshape(128, 27 * 16))
        m = dict(common)
        m.update({"xin": xin, "xsm": xsm, "swkv": np.ascontiguousarray(swkv[0, 16 * c:16 * (c + 1)]), "shT": shT})
        in_maps.append(m)
    res = run_bass_kernel_spmd(nc, in_maps, core_ids=list(range(8)))
    R = res.results
    y_prompt = np.zeros((4, 2048, D), f32)
    y_sample = np.zeros((128, 1, D), f32)
    wkv_prompt = np.zeros((1, 4, 16, 64, 64), f32)
    shift_prompt = np.zeros((1, 4, DSH), f32)
    wkv_sample = np.zeros((1, 128, 16, 64, 64), f32)
    shift_sample = np.zeros((1, 128, DSH), f32)
    sgu_v_sample = np.zeros((1, 128, 1, 1024), f32)
    for c in range(8):
        b, half = divmod(c, 2)
        r = R[c]
        y_prompt[b, half * 1024:(half + 1) * 1024] = r["y"]
        y_sample[16 * c:16 * (c + 1), 0] = r["ysm"]
        if half == 1:
            wkv_prompt[0, b] = r["wkvp"]
            shift_prompt[0, b] = r["shp"][0, :DSH]
        wkv_sample[0, 16 * c:16 * (c + 1)] = r["wkvs"]
        shift_sample[0, 16 * c:16 * (c + 1)] = r["shp"][1:17, :DSH]
        sgu_v_sample[0, 16 * c:16 * (c + 1), 0] = r["sguv"]
    return (y_prompt, y_sample, wkv_prompt, shift_prompt, wkv_sample, shift_sample, sgu_v_sample)
```

```python
import contextlib
import numpy as np
import concourse.bass as bass
import concourse.mybir as mybir
from concourse.bass_utils import run_bass_kernel_spmd

F32 = mybir.dt.float32
BF16 = mybir.dt.bfloat16
AF = mybir.ActivationFunctionType
ALU = mybir.AluOpType
AX = mybir.AxisListType


class Tk:
    __slots__ = ("ap", "w", "rs", "name")

    def __init__(self, ap, name=""):
        self.ap = ap
        self.w = None
        self.rs = []
        self.name = name

    def __getitem__(self, k):
        return self.ap[k]


class Q:
    def __init__(self, name, h, sem):
        self.name = name
        self.h = h
        self.sem = sem
        self.cnt = 0
        self.seen = {}


class Sched:
    N_DMA_SEMS = 30
    N_SW_SEMS = 64

    def __init__(self, nc, stack):
        self.nc = nc
        self.sems = {}
        self.q = {}
        for name, h in (("pe", nc.tensor), ("act", nc.scalar), ("dve", nc.vector),
                        ("pool", nc.gpsimd), ("sp", nc.sync)):
            s = stack.enter_context(nc.semaphore("q_" + name))
            self.sems[name] = s
            self.q[name] = Q(name, h, s)
        self.dma_sems = []
        for i in range(self.N_DMA_SEMS):
            key = "d%d" % i
            s = stack.enter_context(nc.semaphore(key))
            self.sems[key] = s
            self.dma_sems.append([key, 0])
        self.dma_rr = 0
        self.sw_sems = []
        for i in range(self.N_SW_SEMS):
            key = "w%d" % i
            s = stack.enter_context(nc.semaphore(key))
            self.sems[key] = s
            self.sw_sems.append([key, 0])
        self.sw_next = 0
        self.out_deps = []
        self.n_ins = 0
        self.n_wait = 0
        self.stopped = False

    def _wait(self, q, deps):
        best = {}
        for d in deps:
            if d is None:
                continue
            k, v = d
            if best.get(k, 0) < v:
                best[k] = v
        for k, v in best.items():
            if q.seen.get(k, 0) >= v:
                continue
            if k == q.name and k == "pe":
                continue
            q.h.wait_ge(self.sems[k], v)
            q.seen[k] = v
            self.n_wait += 1

    def _deps(self, reads, writes):
        deps = []
        for t in reads:
            deps.append(t.w)
        for t in writes:
            deps.append(t.w)
            deps.extend(t.rs)
        return deps

    def _mark(self, tok, reads, writes):
        for t in reads:
            t.rs.append(tok)
        for t in writes:
            t.w = tok
            t.rs = []

    def op(self, qn, fn, reads=(), writes=(), inc=True):
        if self.stopped:
            return None
        q = self.q[qn]
        self._wait(q, self._deps(reads, writes))
        ins = fn(q.h)
        self.n_ins += 1
        if inc:
            q.cnt += 1
            ins.then_inc(q.sem, 1)
            tok = (qn, q.cnt)
            q.seen[qn] = max(q.seen.get(qn, 0), 0)
        else:
            tok = (qn, q.cnt + 1)
        self._mark(tok, reads, writes)
        return ins

    def dma(self, qn, out, in_, reads=(), writes=(), is_output=False, **kw):
        if self.stopped:
            return None
        q = self.q[qn]
        if qn == "pool":
            slot = self.sw_sems[self.sw_next]
            self.sw_next += 1
        else:
            slot = self.dma_sems[self.dma_rr]
            self.dma_rr = (self.dma_rr + 1) % len(self.dma_sems)
        deps = self._deps(reads, writes)
        if slot[1] > 0:
            deps.append((slot[0], slot[1]))
        self._wait(q, deps)
        ins = q.h.dma_start(out=out, in_=in_, **kw)
        slot[1] += 16
        ins.then_inc(self.sems[slot[0]], 16)
        self.n_ins += 1
        tok = (slot[0], slot[1])
        self._mark(tok, reads, writes)
        if is_output:
            self.out_deps.append(tok)
        return tok

    def finish(self, qn="sp"):
        q = self.q[qn]
        self._wait(q, self.out_deps)

    def barrier(self):
        if self.stopped:
            return
        toks = []
        for qn, q in self.q.items():
            if q.cnt > 0:
                toks.append((qn, q.cnt))
        for k, v in self.dma_sems + self.sw_sems:
            if v > 0:
                toks.append((k, v))
        for qn, q in self.q.items():
            self._wait(q, toks)

D = 2048
DSH = 3360
DFF = 8192
C_DEC = float(np.exp(-0.5))
RMS_EPS = 1e-5
LN_EPS = 1e-5
GN_EPS = 64e-5

O_ID = 0
O_CM = O_ID + 128
O_T2 = O_CM + 258
O_F64 = O_T2 + 128
O_M4 = O_F64 + 128
O_MA = O_M4 + 512
O_BO = O_MA + 512
O_BS = O_BO + 128
O_OM = O_BS + 2
O_SEL = O_OM + 128
NCB = O_SEL + 1024
V_MU = 0
V_KK = 27
V_KA = 35
V_RK = 43
V_A0 = 51
V_SG = 59
V_SB = 67
V_W00 = 75
V_B0 = 83
NVB = 91


def make_consts():
    cb = np.zeros((128, NCB), np.float32)
    t = np.arange(128)
    cb[:, O_ID:O_ID + 128] = np.eye(128)
    le63 = (t <= 63).astype(np.float32)[:, None]
    cb[:, O_CM:O_CM + 128] = -C_DEC * ((t[:, None] <= t[None, :]).astype(np.float32) - le63)
    cb[:, O_CM + 128:O_CM + 256] = -C_DEC * ((t[:, None] < t[None, :]).astype(np.float32) - le63)
    cb[:, O_CM + 256] = -C_DEC * (t >= 64)
    cb[:, O_CM + 257] = -C_DEC * (t <= 63)
    cb[:, O_T2:O_T2 + 128] = -C_DEC * (t[:, None] > t[None, :])
    cb[:, O_F64:O_F64 + 128] = -C_DEC * le63
    mus = (t[:, None] < t[None, :]).astype(np.float32)
    mui = (t[:, None] <= t[None, :]).astype(np.float32)
    cb[:, O_M4:O_M4 + 512] = np.concatenate([mus, mui, mus, mui], 1)
    mls = (t[None, :] < t[:, None]).astype(np.float32)
    cb[:, O_MA:O_MA + 512] = np.concatenate([mls] * 4, 1)
    cb[:, O_BO:O_BO + 128] = (t[:, None] // 64 == t[None, :] // 64)
    cb[:, O_BS] = (t < 64)
    cb[:, O_BS + 1] = (t >= 64)
    cb[:, O_OM:O_OM + 128] = 1.0 / 1024
    for b in range(16):
        cb[b, O_SEL + b * 64:O_SEL + (b + 1) * 64] = 1.0
    return cb


class StopBuild(Exception):
    pass


def make_sel2():
    z = np.zeros((32, 16 * 128), np.float32)
    for b in range(16):
        z[b, b * 128:b * 128 + 64] = 1.0
        z[16 + b, b * 128 + 64:(b + 1) * 128] = 1.0
    return z


def build():
    import os
    STAGE = int(os.environ.get("KSTAGE", "0"))

    def ck(n):
        if STAGE == n and n > 0:
            SREF[0].stopped = True
    SREF = [None]
    nc = bass.Bass("TRN2", target_bir_lowering=False)

    def din(name, shape):
        return nc.dram_tensor(name, shape, F32, kind="ExternalInput").ap()

    def dout(name, shape):
        return nc.dram_tensor(name, shape, F32, kind="ExternalOutput").ap()

    xin = din("xin", [2048, D])
    xsm = din("xsm", [17, D])
    swkv = din("swkv", [16, 16, 64, 64])
    shT_d = din("shT", [128, 27 * 16])
    w_in = din("w_in", [D, 5408])
    w_out = din("w_out", [D, D])
    w_fu = din("w_fu", [D, DFF])
    w_fd = din("w_fd", [DFF, D])
    wupa_d = din("wupa", [65, 1024])
    aup_d = din("aup", [64, 1024])
    gup_d = din("gup", [160, 1024])
    g1_d = din("g1", [1, D])
    g2_d = din("g2", [1, D])
    gf_d = din("gf", [1, D])
    lg_d = din("lg", [1, 1024])
    lb_d = din("lb", [1, 1024])
    sgb_d = din("sgb", [1, 1024])
    sgw_d = din("sgw", [8, 128, 128])
    cb_d = din("cb", [128, NCB])
    vb_d = din("vb", [128, NVB])
    sel2_d = din("sel2", [32, 16 * 128])

    y_o = dout("y", [1024, D])
    ysm_o = dout("ysm", [16, D])
    wkvp_o = dout("wkvp", [16, 64, 64])
    shp_o = dout("shp", [17, 27 * 128])
    wkvs_o = dout("wkvs", [16, 16, 64, 64])
    sguv_o = dout("sguv", [16, 1024])

    YT_d = nc.dram_tensor("YT", [16, 128, 1040], BF16).ap()
    PR_d = nc.dram_tensor("PR", [24, 128, 1041], F32).ap()
    VZ_d = nc.dram_tensor("VZ", [8, 128, 1040], F32).ap()
    UZ_d = nc.dram_tensor("UZ", [8, 128, 1040], BF16).ap()

    w_in_v = w_in.rearrange("(kc p) c -> p kc c", p=128)
    w_fu_v = w_fu.rearrange("(kc p) c -> p kc c", p=128)
    w_out_v = w_out.rearrange("(kb p) d -> p kb d", p=128)

    with contextlib.ExitStack() as st0:
        S = Sched(nc, st0)
        SREF[0] = S

        uid = [0]

        def sb(st, name, shape, dt=F32):
            uid[0] += 1
            nm = "s%d_%s" % (uid[0], name)
            return Tk(st.enter_context(nc.sbuf_tensor(nm, shape, dt)), nm)

        PS = [Tk(st0.enter_context(nc.psum_tensor("ps%d" % i, [128, 512], F32)), "ps%d" % i) for i in range(8)]
        psi = [0]

        pool_sel = [None]
        pool_cnt = {}

        def ps():
            if pool_sel[0] is not None:
                base, size = pool_sel[0]
                k = pool_cnt.get(base, 0)
                pool_cnt[base] = k + 1
                return PS[base + k % size]
            t = PS[psi[0] % 8]
            psi[0] += 1
            return t

        ACT = lambda fn, r, w: S.op("act", fn, r, w)
        DVE = lambda fn, r, w: S.op("dve", fn, r, w)
        PE = lambda fn, r, w, inc=True: S.op("pe", fn, r, w, inc=inc)
        YTk = Tk(YT_d, "YT")
        PRtk = [Tk(PR_d[b], "PR%d" % b) for b in range(24)]
        VZtk = Tk(VZ_d, "VZd")
        UZtk = Tk(UZ_d, "UZd")

        CB = sb(st0, "CB", [128, NCB])
        VB = sb(st0, "VB", [128, NVB])
        S.dma("sp", CB.ap[:], cb_d, writes=[CB])
        S.dma("sp", VB.ap[:], vb_d, writes=[VB])
        IDB = sb(st0, "IDB", [128, 128], BF16)
        BOB = sb(st0, "BOB", [128, 128], BF16)
        BSB = sb(st0, "BSB", [128, 2], BF16)
        DVE(lambda e: e.tensor_copy(out=IDB.ap[:], in_=CB.ap[:, O_ID:O_ID + 128]), [CB], [IDB])
        DVE(lambda e: e.tensor_copy(out=BOB.ap[:], in_=CB.ap[:, O_BO:O_BO + 128]), [CB], [BOB])
        DVE(lambda e: e.tensor_copy(out=BSB.ap[:], in_=CB.ap[:, O_BS:O_BS + 2]), [CB], [BSB])
        ident = CB.ap[:, O_ID:O_ID + 128]
        SST = [sb(st0, "sst%d" % j, [128, 128]) for j in range(8)]
        SBF = [sb(st0, "sbf%d" % j, [128, 128], BF16) for j in range(8)]
        for j in range(8):
            DVE(lambda e: e.memset(SST[j].ap[:], 0.0), [], [SST[j]])
            DVE(lambda e: e.memset(SBF[j].ap[:], 0.0), [], [SBF[j]])
        SH = sb(st0, "SH", [128, 27, 17])
        DVE(lambda e: e.memset(SH.ap[:], 0.0), [], [SH])
        SMZ = sb(st0, "SMZ", [128, 8, 5, 32])
        SMv = sb(st0, "SMv", [128, 8, 16])
        DVE(lambda e: e.memset(SMZ.ap[:], 0.0), [], [SMZ])
        RKS = sb(st0, "RKS", [128, 16])
        ss_all = sb(st0, "ss", [128, 32])
        ss_list = [Tk(ss_all.ap[:, 4 * i:4 * i + 4], "ss%d" % i) for i in range(8)]
        ss_rr = [0]

        def next_ss():
            t = ss_list[ss_rr[0] % 8]
            ss_rr[0] += 1
            return t
        wupa = sb(st0, "wupa", [128, 1024], BF16)
        aup = sb(st0, "aup", [128, 1024], BF16)
        gup1 = sb(st0, "gup1", [128, 1024], BF16)
        gup2 = sb(st0, "gup2", [128, 1024], BF16)
        S.dma("pool", wupa.ap[0:65, :], wupa_d, writes=[wupa])
        S.dma("pool", aup.ap[64:128, :], aup_d, writes=[aup])
        S.dma("pool", gup1.ap[:, :], gup_d[0:128, :], writes=[gup1])
        S.dma("pool", gup2.ap[0:32, :], gup_d[128:160, :], writes=[gup2])

        def norm_S1(xt, xap, n, hb, ss_t):
            DVE(lambda e: e.memset(ss_t.ap[0:n, 0:1], 0.0), [], [ss_t])
            ACT(lambda e: e.activation(out=hb.ap[0:n, :], in_=xap[0:n, :], func=AF.Square, accum_out=ss_t.ap[0:n, 0:1]), [xt], [hb, ss_t])

        def norm_S2(xt, xap, n, G, hb, ss_t):
            DVE(lambda e: e.tensor_scalar(out=ss_t.ap[0:n, 1:2], in0=ss_t.ap[0:n, 0:1], scalar1=1.0 / D, scalar2=RMS_EPS, op0=ALU.mult, op1=ALU.add), [ss_t], [ss_t])
            ACT(lambda e: e.activation(out=ss_t.ap[0:n, 2:3], in_=ss_t.ap[0:n, 1:2], func=AF.Sqrt), [ss_t], [ss_t])
            DVE(lambda e: e.reciprocal(out=ss_t.ap[0:n, 3:4], in_=ss_t.ap[0:n, 2:3]), [ss_t], [ss_t])
            DVE(lambda e: e.scalar_tensor_tensor(out=hb.ap[0:n, :], in0=xap[0:n, :], scalar=ss_t.ap[0:n, 3:4], in1=G.ap[0:n, :], op0=ALU.mult, op1=ALU.mult), [xt, ss_t, G], [hb])

        def norm_S3(n, hb, hT, col0):
            for half in range(2):
                P = ps()
                pb = P.ap[:].bitcast(BF16)
                for k8 in range(8):
                    kc = half * 8 + k8
                    PE(lambda e: e.transpose(out=pb[:, k8 * 128:k8 * 128 + n], in_=hb.ap[0:n, kc * 128:(kc + 1) * 128], identity=IDB.ap[0:n, 0:n]), [hb, IDB], [P], inc=(k8 == 7))
                ACT(lambda e: e.activation(out=hT.ap[:, half * 8:(half + 1) * 8, col0:col0 + n], in_=pb.rearrange("p (k t) -> p k t", k=8)[:, :, 0:n], func=AF.Copy), [P], [hT])

        def norm_pipe(items, G, hbs, hT):
            nI = len(items)
            st = {}
            for step in range(nI + 2):
                if step < nI:
                    ld, n, col0 = items[step]
                    xt, xap = ld()
                    st[step] = (xt, xap, next_ss())
                    norm_S1(xt, xap, n, hbs[step % len(hbs)], st[step][2])
                q = step - 1
                if 0 <= q < nI:
                    xt, xap, sst = st[q]
                    norm_S2(xt, xap, items[q][1], G, hbs[q % len(hbs)], sst)
                q = step - 2
                if 0 <= q < nI:
                    norm_S3(items[q][1], hbs[q % len(hbs)], hT, items[q][2])

        def proj(wt, c0, npart, hT, groups, evac):
            for (t0, t1) in groups:
                P = ps()
                for kc in range(16):
                    PE(lambda e: e.matmul(P.ap[0:npart, 0:t1 - t0], lhsT=wt.ap[:, kc, c0:c0 + npart], rhs=hT.ap[:, kc, t0:t1], start=(kc == 0), stop=(kc == 15)), [wt, hT], [P], inc=(kc == 15))
                evac(P, t0, t1)

        def post(n, C, H, O_tk, O3, O4unused, Q1, Q1_3, Q1_4, Q2, Q2_3, st16, LG4, LB4, LGtk, LBtk):
            DVE(lambda e: e.tensor_reduce(out=st16.ap[0:n, 0, :], in_=O3, axis=AX.X, op=ALU.add), [O_tk], [st16])
            ACT(lambda e: e.activation(out=Q2_3, in_=O3, func=AF.Square), [O_tk], [Q2])
            DVE(lambda e: e.tensor_reduce(out=st16.ap[0:n, 1, :], in_=Q2_3, axis=AX.X, op=ALU.add), [Q2, st16], [st16])
            DVE(lambda e: e.tensor_scalar(out=st16.ap[0:n, 0, :], in0=st16.ap[0:n, 0, :], scalar1=1.0 / 64, scalar2=None, op0=ALU.mult), [st16], [st16])
            DVE(lambda e: e.tensor_tensor(out=st16.ap[0:n, 2, :], in0=st16.ap[0:n, 0, :], in1=st16.ap[0:n, 0, :], op=ALU.mult), [st16], [st16])
            DVE(lambda e: e.scalar_tensor_tensor(out=st16.ap[0:n, 1, :], in0=st16.ap[0:n, 1, :], scalar=1.0 / 64, in1=st16.ap[0:n, 2, :], op0=ALU.mult, op1=ALU.subtract), [st16], [st16])
            DVE(lambda e: e.tensor_scalar(out=st16.ap[0:n, 1, :], in0=st16.ap[0:n, 1, :], scalar1=GN_EPS, scalar2=None, op0=ALU.add), [st16], [st16])
            ACT(lambda e: e.activation(out=st16.ap[0:n, 2, :], in_=st16.ap[0:n, 1, :], func=AF.Sqrt), [st16], [st16])
            DVE(lambda e: e.reciprocal(out=st16.ap[0:n, 3, :], in_=st16.ap[0:n, 2, :]), [st16], [st16])
            DVE(lambda e: e.tensor_tensor(out=Q1_3, in0=O3, in1=st16.ap[0:n, 0, :].unsqueeze(2).to_broadcast([n, 16, 64]), op=ALU.subtract), [O_tk, st16], [Q1])
            DVE(lambda e: e.tensor_tensor(out=Q1_3, in0=Q1_3, in1=st16.ap[0:n, 3, :].unsqueeze(2).to_broadcast([n, 16, 64]), op=ALU.mult), [Q1, st16], [Q1])
            DVE(lambda e: e.tensor_tensor(out=Q1_4, in0=Q1_4, in1=LG4, op=ALU.mult), [Q1, LGtk], [Q1])
            DVE(lambda e: e.tensor_tensor(out=Q1_4, in0=Q1_4, in1=LB4, op=ALU.add), [Q1, LBtk], [Q1])

        groups3 = [(0, 512), (512, 1024), (1024, 1040)]

        def sample_gen(sts, stp_ctx):
            sgT1, sgT2, gup1, gup2 = stp_ctx
            if True:
                ROWS2 = sb(sts, "ROWS2", [32, 5, 512])
                ROWSV = sb(sts, "ROWSV", [16, 1024])
                SEL2 = sb(sts, "SEL2", [32, 16 * 128])
                I2 = sb(sts, "I2", [128, 64])
                OS = sb(sts, "OS", [128, 8, 16])
                Sin = [sb(sts, "Sin%d" % i, [128, 8, 64]) for i in range(2)]
                X1 = sb(sts, "X1", [128, 8, 64])
                X2 = [sb(sts, "X2%d" % i, [128, 8, 64]) for i in range(2)]
                sa = sb(sts, "sa", [128, 8])
                OSM = sb(sts, "OSM", [16, 1024])
                Q1s = sb(sts, "Q1s", [16, 1024])
                Q2s = sb(sts, "Q2s", [16, 1024])
                st16s = sb(sts, "st16s", [16, 6, 16])
                LGf = sb(sts, "LGf", [16, 1024])
                LBf = sb(sts, "LBf", [16, 1024])
                yas = sb(sts, "yas", [16, 1024], BF16)
                yaTs = sb(sts, "yaTs", [128, 8, 16], BF16)
                S.dma("sp", SEL2.ap[:], sel2_d, writes=[SEL2])
                S.dma("sp", LGf.ap[:], lg_d.partition_broadcast(16), writes=[LGf])
                S.dma("sp", LBf.ap[:], lb_d.partition_broadcast(16), writes=[LBf])
                DVE(lambda e: e.tensor_tensor(out=I2.ap[:], in0=CB.ap[:, O_ID:O_ID + 64], in1=CB.ap[:, O_ID + 64:O_ID + 128], op=ALU.add), [CB], [I2])
                for vec in range(5):
                    for half in range(2):
                        P = ps()
                        for jj in range(4):
                            j = half * 4 + jj
                            PE(lambda e: e.matmul(P.ap[0:32, jj * 64:(jj + 1) * 64], lhsT=SMZ.ap[:, j, vec, :], rhs=I2.ap[:], start=True, stop=True), [SMZ, I2], [P], inc=(jj == 3))
                        ACT(lambda e: e.activation(out=ROWS2.ap[:, vec, half * 256:(half + 1) * 256], in_=P.ap[0:32, 0:256], func=AF.Copy), [P], [ROWS2])
                for half in range(2):
                    P = ps()
                    for jj in range(4):
                        j = half * 4 + jj
                        PE(lambda e: e.transpose(out=P.ap[0:16, jj * 128:(jj + 1) * 128], in_=SMv.ap[:, j, :], identity=ident), [SMv, CB], [P], inc=(jj == 3))
                    ACT(lambda e: e.activation(out=ROWSV.ap[:, half * 512:(half + 1) * 512], in_=P.ap[0:16, :], func=AF.Copy), [P], [ROWSV])
                yield

                def bc(vec, b):
                    Pq = ps()
                    PE(lambda e: e.matmul(Pq.ap[:, :], lhsT=SEL2.ap[0:32, b * 128:(b + 1) * 128], rhs=ROWS2.ap[0:32, vec, :], start=True, stop=True), [ROWS2, SEL2], [Pq])
                    return Pq

                pv = lambda P_: P_.ap[:, :].rearrange("p (h k) -> p h k", h=8)
                for b in range(16):
                    Si = Sin[b % 2]
                    Xo = X2[b % 2]
                    S.dma("sp", Si.ap[:], swkv[b].rearrange("(hp h2) v k -> (h2 v) hp k", h2=2), writes=[Si])
                    Pq = bc(0, b)
                    DVE(lambda e: e.tensor_tensor(out=X1.ap[:], in0=Si.ap[:], in1=pv(Pq), op=ALU.mult), [Si, Pq], [X1])
                    DVE(lambda e: e.tensor_reduce(out=sa.ap[:], in_=X1.ap[:], axis=AX.X, op=ALU.add), [X1], [sa])
                    Pq = bc(4, b)
                    DVE(lambda e: e.tensor_tensor(out=Xo.ap[:], in0=Si.ap[:], in1=pv(Pq), op=ALU.mult), [Si, Pq], [Xo])
                    Pq = bc(1, b)
                    DVE(lambda e: e.tensor_tensor(out=X1.ap[:], in0=pv(Pq), in1=sa.ap[:].unsqueeze(2).to_broadcast([128, 8, 64]), op=ALU.mult), [sa, Pq], [X1])
                    DVE(lambda e: e.tensor_tensor(out=Xo.ap[:], in0=Xo.ap[:], in1=X1.ap[:], op=ALU.add), [Xo, X1], [Xo])
                    Pq = bc(2, b)
                    DVE(lambda e: e.tensor_tensor(out=X1.ap[:], in0=pv(Pq), in1=SMv.ap[:, :, b:b + 1].to_broadcast([128, 8, 64]), op=ALU.mult), [SMv, Pq], [X1])
                    DVE(lambda e: e.tensor_tensor(out=Xo.ap[:], in0=Xo.ap[:], in1=X1.ap[:], op=ALU.add), [Xo, X1], [Xo])
                    S.dma("sp", wkvs_o[b].rearrange("(hp h2) v k -> (h2 v) hp k", h2=2), Xo.ap[:], reads=[Xo], is_output=True)
                    Pq = bc(3, b)
                    DVE(lambda e: e.tensor_tensor(out=X1.ap[:], in0=Xo.ap[:], in1=pv(Pq), op=ALU.mult), [Xo, Pq], [X1])
                    DVE(lambda e: e.tensor_reduce(out=OS.ap[:, :, b], in_=X1.ap[:], axis=AX.X, op=ALU.add), [X1], [OS])
                    yield
                for half in range(2):
                    P = ps()
                    for jj in range(4):
                        hp_ = half * 4 + jj
                        PE(lambda e: e.transpose(out=P.ap[0:16, jj * 128:(jj + 1) * 128], in_=OS.ap[:, hp_, :], identity=ident), [OS, CB], [P], inc=(jj == 3))
                    ACT(lambda e: e.activation(out=OSM.ap[:, half * 512:(half + 1) * 512], in_=P.ap[0:16, :], func=AF.Copy), [P], [OSM])
                v3 = lambda t: t.ap[:].rearrange("p (h v) -> p h v", h=16)
                v4 = lambda t: t.ap[:].rearrange("p (c h v) -> p c h v", c=1, h=16)
                post(16, 1, 16, OSM, v3(OSM), None, Q1s, v3(Q1s), v4(Q1s), Q2s, v3(Q2s), st16s, v4(LGf), v4(LBf), LGf, LBf)
                DVE(lambda e: e.tensor_tensor(out=v3(Q2s), in0=ROWSV.ap[:, :].rearrange("p (h v) -> p h v", h=16), in1=RKS.ap[0:16, :].unsqueeze(2).to_broadcast([16, 16, 64]), op=ALU.mult), [ROWSV, RKS], [Q2s])
                DVE(lambda e: e.tensor_tensor(out=Q1s.ap[:], in0=Q1s.ap[:], in1=Q2s.ap[:], op=ALU.add), [Q1s, Q2s], [Q1s])
                for half in range(2):
                    P = ps()
                    cs = slice(half * 512, (half + 1) * 512)
                    PE(lambda e: e.matmul(P.ap[0:16, :], lhsT=sgT1.ap[:, 1024:1040], rhs=gup1.ap[:, cs], start=True, stop=False), [sgT1, gup1], [P], inc=False)
                    PE(lambda e: e.matmul(P.ap[0:16, :], lhsT=sgT2.ap[0:32, 1024:1040], rhs=gup2.ap[0:32, cs], start=False, stop=True), [sgT2, gup2], [P])
                    DVE(lambda e: e.tensor_tensor(out=yas.ap[:, cs], in0=P.ap[0:16, :], in1=Q1s.ap[:, cs], op=ALU.mult), [P, Q1s], [yas])
                P = ps()
                pb = P.ap[:].bitcast(BF16)
                for j in range(8):
                    PE(lambda e: e.transpose(out=pb[:, j * 16:(j + 1) * 16], in_=yas.ap[0:16, j * 128:(j + 1) * 128], identity=IDB.ap[0:16, 0:16]), [yas, IDB], [P], inc=(j == 7))
                ACT(lambda e: e.activation(out=yaTs.ap[:].rearrange("p j t -> p (j t)"), in_=pb[:, 0:128], func=AF.Copy), [P], [yaTs])
                S.dma("sp", YT_d[0:8, :, 1024:1040].rearrange("j p t -> p j t"), yaTs.ap[:], reads=[yaTs], writes=[YTk])

        def sgu_gen(stg):
            if True:
                VZ = [sb(stg, "VZ%d" % g, [128, 1040]) for g in range(8)]
                UZ = sb(stg, "UZ", [128, 8, 1040], BF16)
                MEAN = sb(stg, "MEAN", [128, 1040])
                RSTD = sb(stg, "RSTD", [128, 1040])
                SQ = sb(stg, "SQ", [128, 1040])
                WsT = sb(stg, "WsT", [128, 8, 128], BF16)
                Wsf = sb(stg, "Wsf", [128, 8, 128])
                SGB = sb(stg, "SGB", [128, 1024])
                VTb = sb(stg, "VTb", [128, 4, 128], BF16)
                tmpm = sb(stg, "tmpm", [128, 4, 128])
                ybT = [sb(stg, "ybT%d" % i, [128, 1040], BF16) for i in range(2)]
                VNs = sb(stg, "VNs", [128, 8, 16])
                sguvT = sb(stg, "sguvT", [16, 1024])
                S.dma("sp", Wsf.ap[:], sgw_d.rearrange("g i j -> i g j"), writes=[Wsf])
                S.dma("sp", SGB.ap[:], sgb_d.partition_broadcast(128), writes=[SGB])
                yield
                for g4 in range(2):
                    P = ps()
                    for gg in range(4):
                        g = g4 * 4 + gg
                        PE(lambda e: e.transpose(out=P.ap[:, gg * 128:(gg + 1) * 128], in_=Wsf.ap[:, g, :], identity=ident), [Wsf, CB], [P], inc=(gg == 3))
                    DVE(lambda e: e.tensor_tensor(out=WsT.ap[:, g4 * 4:(g4 + 1) * 4, :], in0=P.ap[:].rearrange("p (g t) -> p g t", g=4),
                                                  in1=CB.ap[:, O_M4 + 128:O_M4 + 256].unsqueeze(1).to_broadcast([128, 4, 128]), op=ALU.mult), [P, CB], [WsT])
                yield
                for g in range(8):
                    S.dma("sp", VZ[g].ap[:], VZ_d[g], reads=[VZtk], writes=[VZ[g]])
                S.dma("sp", UZ.ap[:], UZ_d.rearrange("g p t -> p g t"), reads=[UZtk], writes=[UZ])
                yield
                for (t0, t1) in groups3:
                    n = t1 - t0
                    P1 = ps()
                    for g in range(8):
                        PE(lambda e: e.matmul(P1.ap[:, 0:n], lhsT=CB.ap[:, O_OM:O_OM + 128], rhs=VZ[g].ap[:, t0:t1], start=(g == 0), stop=(g == 7)), [VZ[g], CB], [P1], inc=(g == 7))
                    ACT(lambda e: e.activation(out=MEAN.ap[:, t0:t1], in_=P1.ap[:, 0:n], func=AF.Copy), [P1], [MEAN])
                    P2 = ps()
                    for g in range(8):
                        ACT(lambda e: e.activation(out=SQ.ap[:, 0:n], in_=VZ[g].ap[:, t0:t1], func=AF.Square), [VZ[g]], [SQ])
                        PE(lambda e: e.matmul(P2.ap[:, 0:n], lhsT=CB.ap[:, O_OM:O_OM + 128], rhs=SQ.ap[:, 0:n], start=(g == 0), stop=(g == 7)), [SQ, CB], [P2], inc=True)
                    ACT(lambda e: e.activation(out=RSTD.ap[:, t0:t1], in_=P2.ap[:, 0:n], func=AF.Copy), [P2], [RSTD])
                yield
                DVE(lambda e: e.tensor_tensor(out=SQ.ap[:], in0=MEAN.ap[:], in1=MEAN.ap[:], op=ALU.mult), [MEAN], [SQ])
                DVE(lambda e: e.tensor_tensor(out=RSTD.ap[:], in0=RSTD.ap[:], in1=SQ.ap[:], op=ALU.subtract), [RSTD, SQ], [RSTD])
                DVE(lambda e: e.tensor_scalar(out=RSTD.ap[:], in0=RSTD.ap[:], scalar1=LN_EPS, scalar2=None, op0=ALU.add), [RSTD], [RSTD])
                ACT(lambda e: e.activation(out=RSTD.ap[:], in_=RSTD.ap[:], func=AF.Sqrt), [RSTD], [RSTD])
                DVE(lambda e: e.reciprocal(out=RSTD.ap[:], in_=RSTD.ap[:]), [RSTD], [RSTD])
                yield
                for g in range(8):
                    DVE(lambda e: e.tensor_tensor(out=VZ[g].ap[:], in0=VZ[g].ap[:], in1=MEAN.ap[:], op=ALU.subtract), [VZ[g], MEAN], [VZ[g]])
                    DVE(lambda e: e.tensor_tensor(out=VZ[g].ap[:], in0=VZ[g].ap[:], in1=RSTD.ap[:], op=ALU.mult), [VZ[g], RSTD], [VZ[g]])
                    DVE(lambda e: e.tensor_scalar(out=VZ[g].ap[:], in0=VZ[g].ap[:], scalar1=VB.ap[:, V_SG + g:V_SG + g + 1], scalar2=VB.ap[:, V_SB + g:V_SB + g + 1], op0=ALU.mult, op1=ALU.add), [VZ[g], VB], [VZ[g]])
                    DVE(lambda e: e.tensor_copy(out=VNs.ap[:, g, :], in_=VZ[g].ap[:, 1024:1040]), [VZ[g]], [VNs])
                    yield
                yield
                for half in range(2):
                    P = ps()
                    for gg in range(4):
                        g = half * 4 + gg
                        PE(lambda e: e.transpose(out=P.ap[0:16, gg * 128:(gg + 1) * 128], in_=VNs.ap[:, g, :], identity=ident), [VNs, CB], [P], inc=(gg == 3))
                    ACT(lambda e: e.activation(out=sguvT.ap[:, half * 512:(half + 1) * 512], in_=P.ap[0:16, :], func=AF.Copy), [P], [sguvT])
                S.dma("sp", sguv_o, sguvT.ap[:], reads=[sguvT], is_output=True)
                yield
                for g in range(8):
                    yb = ybT[g % 2]
                    for c4 in range(2):
                        P = ps()
                        for cc in range(4):
                            c = c4 * 4 + cc
                            PE(lambda e: e.transpose(out=P.ap[:, cc * 128:(cc + 1) * 128], in_=VZ[g].ap[:, c * 128:(c + 1) * 128], identity=ident), [VZ[g], CB], [P], inc=(cc == 3))
                        ACT(lambda e: e.activation(out=VTb.ap[:], in_=P.ap[:].rearrange("p (c t) -> p c t", c=4), func=AF.Copy), [P], [VTb])
                        PM = ps()
                        for cc in range(4):
                            PE(lambda e: e.matmul(PM.ap[:, cc * 128:(cc + 1) * 128], lhsT=VTb.ap[:, cc, :], rhs=WsT.ap[:, g, :], start=True, stop=True), [VTb, WsT], [PM], inc=(cc == 3))
                        DVE(lambda e: e.tensor_tensor(out=tmpm.ap[:], in0=PM.ap[:].rearrange("p (c t) -> p c t", c=4), in1=SGB.ap[:, g * 128:(g + 1) * 128].unsqueeze(1).to_broadcast([128, 4, 128]), op=ALU.add), [PM, SGB], [tmpm])
                        DVE(lambda e: e.tensor_tensor(out=yb.ap[:, c4 * 512:(c4 + 1) * 512], in0=tmpm.ap[:].rearrange("p c t -> p (c t)"), in1=UZ.ap[:, g, c4 * 512:(c4 + 1) * 512], op=ALU.mult), [tmpm, UZ], [yb])
                    DVE(lambda e: e.tensor_scalar(out=tmpm.ap[:, 0, 0:16], in0=VZ[g].ap[:, 1024:1040], scalar1=VB.ap[:, V_W00 + g:V_W00 + g + 1], scalar2=VB.ap[:, V_B0 + g:V_B0 + g + 1], op0=ALU.mult, op1=ALU.add), [VZ[g], VB], [tmpm])
                    DVE(lambda e: e.tensor_tensor(out=yb.ap[:, 1024:1040], in0=tmpm.ap[:, 0, 0:16], in1=UZ.ap[:, g, 1024:1040], op=ALU.mult), [tmpm, UZ], [yb])
                    S.dma("sp", YT_d[8 + g], yb.ap[:], reads=[yb], writes=[YTk])
                    yield

        def rest_phase():
            with contextlib.ExitStack() as str_:
                x1 = sb(str_, "x1", [128, 9, D])
                h2T = sb(str_, "h2T", [128, 16, 1040], BF16)
                tiles = [(ti, 128 if ti < 8 else 16) for ti in range(9)]
                x1t = [Tk(x1.ap[:, ti, :], "x1_%d" % ti) for ti in range(9)]
                with contextlib.ExitStack() as s1:
                    yT = sb(s1, "yT", [128, 16, 1040], BF16)
                    wos = [sb(s1, "wo%d" % i, [128, 16, 512], BF16) for i in range(2)]
                    S.dma("sp", yT.ap[:, 0:8, :], YT_d[0:8].rearrange("k p t -> p k t"), reads=[YTk], writes=[yT])
                    S.dma("sp", yT.ap[:, 8:16, :], YT_d[8:16].rearrange("k p t -> p k t"), reads=[YTk], writes=[yT])
                    for ti in range(8):
                        S.dma("sp", x1.ap[:, ti, :], xin[1024 + ti * 128:1024 + (ti + 1) * 128, :], writes=[x1t[ti]])
                    S.dma("sp", x1.ap[0:16, 8, :], xsm[0:16, :], writes=[x1t[8]])
                    for dg in range(4):
                        ds_ = slice(dg * 512, (dg + 1) * 512)
                        wo = wos[dg % 2]
                        S.dma("pool", wo.ap[:], w_out_v[:, :, ds_], writes=[wo])
                        for ti, n in tiles:
                            cols = slice(ti * 128, ti * 128 + n)
                            P = ps()
                            for kb in range(16):
                                PE(lambda e: e.matmul(P.ap[0:n, :], lhsT=yT.ap[:, kb, cols], rhs=wo.ap[:, kb, :], start=(kb == 0), stop=(kb == 15)), [yT, wo], [P], inc=(kb == 15))
                            DVE(lambda e: e.tensor_tensor(out=x1.ap[0:n, ti, ds_], in0=x1.ap[0:n, ti, ds_], in1=P.ap[0:n, :], op=ALU.add), [x1t[ti], P], [x1t[ti]])
                    S.barrier()
                sff = contextlib.ExitStack()
                wfu0 = sb(sff, "wfu0", [128, 16, 512], BF16)
                wfd = sb(sff, "wfd", [128, 4, D], BF16)
                S.dma("pool", wfu0.ap[:], w_fu_v[:, :, 0:512], writes=[wfu0])
                S.dma("pool", wfd.ap[:], w_fd[0:512, :].rearrange("(b p) d -> p b d", p=128), writes=[wfd])
                with contextlib.ExitStack() as s1b:
                    G2 = sb(s1b, "G2", [128, D])
                    hbs2 = [sb(s1b, "hb2_%d" % i, [128, D], BF16) for i in range(3)]
                    S.dma("sp", G2.ap[:], g2_d.partition_broadcast(128), writes=[G2])
                    norm_pipe([((lambda ti=ti: (x1t[ti], x1.ap[:, ti, :])), n, ti * 128) for ti, n in tiles], G2, hbs2, h2T)
                    S.barrier()
                ck(12)
                with contextlib.ExitStack() as s2:
                    wfu = [wfu0, sb(s2, "wfu1", [128, 16, 512], BF16)]
                    actT = sb(s2, "actT", [128, 4, 1040], BF16)
                    RL = sb(s2, "RL", [128, 512])
                    for f in range(16):
                        wu = wfu[f % 2]
                        if f > 0:
                            S.dma("pool", wu.ap[:], w_fu_v[:, :, f * 512:(f + 1) * 512], writes=[wu])
                            S.dma("pool", wfd.ap[:], w_fd[f * 512:(f + 1) * 512, :].rearrange("(b p) d -> p b d", p=128), writes=[wfd])
                        for fb in range(4):
                            for (t0, t1) in groups3:
                                n = t1 - t0
                                P = ps()
                                for kc in range(16):
                                    PE(lambda e: e.matmul(P.ap[:, 0:n], lhsT=wu.ap[:, kc, fb * 128:(fb + 1) * 128], rhs=h2T.ap[:, kc, t0:t1], start=(kc == 0), stop=(kc == 15)), [wu, h2T], [P], inc=(kc == 15))
                                ACT(lambda e: e.activation(out=RL.ap[:, 0:n], in_=P.ap[:, 0:n], func=AF.Relu), [P], [RL])
                                DVE(lambda e: e.tensor_tensor(out=actT.ap[:, fb, t0:t1], in0=RL.ap[:, 0:n], in1=RL.ap[:, 0:n], op=ALU.mult), [RL], [actT])
                        for ti, n in tiles:
                            cols = slice(ti * 128, ti * 128 + n)
                            for dg in range(4):
                                ds_ = slice(dg * 512, (dg + 1) * 512)
                                P = ps()
                                for fb in range(4):
                                    PE(lambda e: e.matmul(P.ap[0:n, :], lhsT=actT.ap[:, fb, cols], rhs=wfd.ap[:, fb, ds_], start=(fb == 0), stop=(fb == 3)), [actT, wfd], [P], inc=(fb == 3))
                                DVE(lambda e: e.tensor_tensor(out=x1.ap[0:n, ti, ds_], in0=x1.ap[0:n, ti, ds_], in1=P.ap[0:n, :], op=ALU.add), [x1t[ti], P], [x1t[ti]])
                    S.barrier()
                sff.close()
                with contextlib.ExitStack() as s3:
                    GF = sb(s3, "GF", [128, D])
                    yt = [sb(s3, "yt%d" % i, [128, D]) for i in range(3)]
                    SHT = sb(s3, "SHT", [17, 27 * 128])
                    S.dma("sp", GF.ap[:], gf_d.partition_broadcast(128), writes=[GF])
                    fst = {}
                    for step in range(len(tiles) + 1):
                        if step < len(tiles):
                            ti, n = tiles[step]
                            xa = x1.ap[:, ti, :]
                            y_ = yt[ti % 3]
                            ss_t = next_ss()
                            fst[step] = ss_t
                            DVE(lambda e: e.memset(ss_t.ap[0:n, 0:1], 0.0), [], [ss_t])
                            ACT(lambda e: e.activation(out=y_.ap[0:n, :], in_=xa[0:n, :], func=AF.Square, accum_out=ss_t.ap[0:n, 0:1]), [x1t[ti]], [y_, ss_t])
                        q = step - 1
                        if 0 <= q < len(tiles):
                            ti, n = tiles[q]
                            xa = x1.ap[:, ti, :]
                            y_ = yt[ti % 3]
                            ss_t = fst[q]
                            DVE(lambda e: e.tensor_scalar(out=ss_t.ap[0:n, 1:2], in0=ss_t.ap[0:n, 0:1], scalar1=1.0 / D, scalar2=RMS_EPS, op0=ALU.mult, op1=ALU.add), [ss_t], [ss_t])
                            ACT(lambda e: e.activation(out=ss_t.ap[0:n, 2:3], in_=ss_t.ap[0:n, 1:2], func=AF.Sqrt), [ss_t], [ss_t])
                            DVE(lambda e: e.reciprocal(out=ss_t.ap[0:n, 3:4], in_=ss_t.ap[0:n, 2:3]), [ss_t], [ss_t])
                            DVE(lambda e: e.scalar_tensor_tensor(out=y_.ap[0:n, :], in0=xa[0:n, :], scalar=ss_t.ap[0:n, 3:4], in1=GF.ap[0:n, :], op0=ALU.mult, op1=ALU.mult), [x1t[ti], ss_t, GF], [y_])
                            if ti < 8:
                                S.dma("sp", y_o[ti * 128:(ti + 1) * 128, :], y_.ap[:], reads=[y_], is_output=True)
                            else:
                                S.dma("sp", ysm_o, y_.ap[0:16, :], reads=[y_], is_output=True)
                    for b4 in range(7):
                        P = ps()
                        nb = min(4, 27 - b4 * 4)
                        for bb in range(nb):
                            blk = b4 * 4 + bb
                            PE(lambda e: e.transpose(out=P.ap[0:17, bb * 128:(bb + 1) * 128], in_=SH.ap[:, blk, :], identity=ident), [SH, CB], [P], inc=(bb == nb - 1))
                        ACT(lambda e: e.activation(out=SHT.ap[:, b4 * 512:b4 * 512 + nb * 128], in_=P.ap[0:17, 0:nb * 128], func=AF.Copy), [P], [SHT])
                    S.dma("sp", shp_o, SHT.ap[:], reads=[SHT], is_output=True)

        def run_tasks(tasks):
            done = set()
            running = []
            pending = list(tasks)
            while pending or running:
                for t in list(pending):
                    if all(d in done for d in t[2]):
                        pending.remove(t)
                        running.append((t[0], t[1](), t[3]))
                assert running, "task graph stuck"
                for r in list(running):
                    pool_sel[0] = r[2]
                    try:
                        next(r[1])
                    except StopIteration:
                        running.remove(r)
                        done.add(r[0])
                pool_sel[0] = None


        ck(1)
        for own in (False, True):
            NT = 1041 if own else 1024
            NM = 1040 if own else 1024
            NCH = 8
            groups = [(0, 512), (512, 1024)] + ([(1024, 1041)] if own else [])
            with contextlib.ExitStack() as stp:
                TW = sb(stp, "TW", [128, NT], BF16)
                aLo = sb(stp, "aLo", [128, NT], BF16)
                sgT1 = sb(stp, "sgT1", [128, NT], BF16)
                sgT2 = sb(stp, "sgT2", [128, NT], BF16)
                DVE(lambda e: e.memset(TW.ap[64:65, :], 1.0), [], [TW])
                shT = sb(stp, "shT", [128, 27 * 16])
                sth = contextlib.ExitStack()
                hT = sb(sth, "hT", [128, 16, NT], BF16)
                with contextlib.ExitStack() as sta:
                    G1 = sb(sta, "G1", [128, D])
                    S.dma("sp", G1.ap[:], g1_d.partition_broadcast(128), writes=[G1])
                    xts = [sb(sta, "xt%d" % i, [128, D]) for i in range(3)]
                    hbs = [sb(sta, "hb%d" % i, [128, D], BF16) for i in range(3)]
                    base = 1024 if own else 0
                    items = []
                    for i in range(8):
                        def ld(i=i):
                            xt = xts[i % 3]
                            S.dma("sp", xt.ap[:], xin[base + i * 128:base + (i + 1) * 128, :], writes=[xt])
                            return xt, xt.ap
                        items.append((ld, 128, i * 128))
                    if own:
                        def ld17():
                            xt = xts[8 % 3]
                            S.dma("sp", xt.ap[0:17, :], xsm, writes=[xt])
                            return xt, xt.ap
                        items.append((ld17, 17, 1024))
                    norm_pipe(items, G1, hbs, hT)
                    S.barrier()
                ck(7 if own else 2)
                if own:
                    S.dma("sp", shT.ap[:], shT_d, writes=[shT])

                def evac_copy(dst):
                    def f(P, t0, t1, dst=dst):
                        n = dst_np[0]
                        ACT(lambda e: e.activation(out=dst.ap[0:n, t0:t1], in_=P.ap[0:n, 0:t1 - t0], func=AF.Copy), [P], [dst])
                    return f
                dst_np = [128]

                def mix(p, d, blk, npart):
                    mu = VB.ap[0:npart, V_MU + blk:V_MU + blk + 1]
                    if own:
                        DVE(lambda e: e.tensor_copy(out=SH.ap[0:npart, blk, :], in_=p.ap[0:npart, 1023:1040]), [p], [SH])
                    DVE(lambda e: e.tensor_tensor(out=d.ap[0:npart, 1:1024], in0=p.ap[0:npart, 0:1023], in1=p.ap[0:npart, 1:1024], op=ALU.subtract), [p], [d])
                    if own:
                        DVE(lambda e: e.tensor_tensor(out=d.ap[0:npart, 0:1], in0=p.ap[0:npart, 1040:1041], in1=p.ap[0:npart, 0:1], op=ALU.subtract), [p], [d])
                        DVE(lambda e: e.tensor_tensor(out=d.ap[0:npart, 1024:1040], in0=shT.ap[0:npart, blk * 16:(blk + 1) * 16], in1=p.ap[0:npart, 1024:1040], op=ALU.subtract), [p, shT], [d])
                    else:
                        DVE(lambda e: e.tensor_scalar(out=d.ap[0:npart, 0:1], in0=p.ap[0:npart, 0:1], scalar1=-1.0, scalar2=None, op0=ALU.mult), [p], [d])
                    DVE(lambda e: e.scalar_tensor_tensor(out=d.ap[0:npart, 0:NM], in0=d.ap[0:npart, 0:NM], scalar=mu, in1=p.ap[0:npart, 0:NM], op0=ALU.mult, op1=ALU.add), [p, d, VB], [d])

                with contextlib.ExitStack() as stl:
                    wL = sb(stl, "wL", [128, 16, 288], BF16)
                    T0 = sb(stl, "La", [128, NT])
                    T1 = sb(stl, "Lb", [128, NT])
                    ncl = 288 if own else 128
                    S.dma("pool", wL.ap[:, :, 0:ncl], w_in_v[:, :, 3072:3072 + ncl], writes=[wL])
                    dst_np[0] = 128
                    proj(wL, 0, 128, hT, groups, evac_copy(T0))
                    mix(T0, T1, 24, 128)
                    ACT(lambda e: e.activation(out=TW.ap[0:64, 0:NM], in_=T1.ap[0:64, 0:NM], func=AF.Tanh), [T1], [TW])
                    ACT(lambda e: e.activation(out=aLo.ap[64:128, 0:NM], in_=T1.ap[64:128, 0:NM], func=AF.Copy), [T1], [aLo])
                    if own:
                        proj(wL, 128, 128, hT, groups, evac_copy(T0))
                        mix(T0, T1, 25, 128)
                        ACT(lambda e: e.activation(out=sgT1.ap[:, 0:NM], in_=T1.ap[:, 0:NM], func=AF.Sigmoid), [T1], [sgT1])
                        dst_np[0] = 32
                        proj(wL, 256, 32, hT, groups, evac_copy(T0))
                        mix(T0, T1, 26, 32)
                        ACT(lambda e: e.activation(out=sgT2.ap[0:32, 0:NM], in_=T1.ap[0:32, 0:NM], func=AF.Sigmoid), [T1], [sgT2])
                        dst_np[0] = 128
                    S.barrier()

                with contextlib.ExitStack() as stq:
                    wPs = [sb(stq, "wP%d" % i, [128, 16, 384], BF16) for i in range(2)]
                    RB = [sb(stq, "RB%d" % i, [128, NT]) for i in range(4)]
                    RM = [sb(stq, "RM%d" % i, [128, NT]) for i in range(3)]
                    rbi = [0]
                    for j in range(8):
                        wP = wPs[j % 2]
                        ncw = 384 if own else 256
                        S.dma("pool", wP.ap[:, :, 0:ncw], w_in_v[:, :, j * 384:j * 384 + ncw], writes=[wP])
                        for q in range(3 if own else 2):
                            rb = RB[rbi[0] % 4]
                            rm = RM[rbi[0] % 3]
                            rbi[0] += 1
                            proj(wP, q * 128, 128, hT, groups, evac_copy(rb))
                            mix(rb, rm, (8 + j, 16 + j, j)[q], 128)
                            S.dma("sp", PR_d[3 * j + q, :, 0:NM], rm.ap[:, 0:NM], reads=[rm], writes=[PRtk[3 * j + q]])
                    if own:
                        wSs = [sb(stq, "wS%d" % i, [128, 16, 512], BF16) for i in range(2)]
                        RU = [sb(stq, "RU%d" % i, [128, 1040], BF16) for i in range(2)]
                        for half, c0 in ((0, 4384), (1, 3360)):
                            for g4 in range(2):
                                wS = wSs[g4]
                                S.dma("pool", wS.ap[:], w_in_v[:, :, c0 + g4 * 512:c0 + (g4 + 1) * 512], writes=[wS])
                                for gg in range(4):
                                    g = g4 * 4 + gg
                                    if half == 0:
                                        rb = RB[rbi[0] % 4]
                                    else:
                                        rb = RU[rbi[0] % 2]
                                    rbi[0] += 1

                                    def evg(P, t0, t1, rb=rb):
                                        ACT(lambda e: e.activation(out=rb.ap[:, t0:t1], in_=P.ap[:, 0:t1 - t0], func=AF.Gelu), [P], [rb])
                                    proj(wS, gg * 128, 128, hT, groups3, evg)
                                    if half == 0:
                                        S.dma("sp", VZ_d[g], rb.ap[:, 0:1040], reads=[rb], writes=[VZtk])
                                    else:
                                        S.dma("sp", UZ_d[g], rb.ap[:, 0:1040], reads=[rb], writes=[UZtk])
                    S.barrier()
                sth.close()
                with contextlib.ExitStack() as stb:
                    TS = [sb(stb, "TS%d" % i, [128, NT]) for i in range(6)]
                    T0, T1, T2, T3, T5, T6 = TS
                    sqb = sb(stb, "sqb", [128, NT], BF16)
                    st16w = sb(stb, "st16w", [128, 16])
                    sgtok = sb(stb, "sgtok", [128, 8, 128])
                    Lf = sb(stb, "Lf", [128, 8, 258])
                    ED = sb(stb, "ED", [128, 8, 128])

                    class Ctx:
                        pass
                    bufs = []
                    for bi in range(2):
                        B = Ctx()
                        B.SC = sb(stb, "SC", [128, 16])
                        B.aT = sb(stb, "aT", [128, 1024], BF16)
                        B.bT = sb(stb, "bT", [128, 1024], BF16)
                        B.kT = sb(stb, "kT", [128, 1024], BF16)
                        B.rT = sb(stb, "rT", [128, 1024], BF16)
                        B.prodb = sb(stb, "prodb", [128, 1040], BF16)
                        B.Zar = sb(stb, "Zar", [128, 8, 4, 128], BF16)
                        B.Zb = sb(stb, "Zb", [128, 8, 2, 128], BF16)
                        DVE(lambda e: e.memset(B.Zar.ap[:], 0.0), [], [B.Zar])
                        DVE(lambda e: e.memset(B.Zb.ap[:], 0.0), [], [B.Zb])
                        B.Bhat = sb(stb, "Bhat", [128, 8, 128], BF16)
                        B.Khat = sb(stb, "Khat", [128, 8, 128], BF16)
                        B.Vtok = sb(stb, "Vtok", [128, 8, 128], BF16)
                        B.LGp = sb(stb, "LGp", [128, 128])
                        B.LBp = sb(stb, "LBp", [128, 128])
                        bufs.append(B)
                    ctxs = []
                    for ci in range(4):
                        cx = Ctx()
                        cx.ATm = sb(stb, "ATm", [128, 4, 256], BF16)
                        cx.AKm = sb(stb, "AKm", [128, 4, 256], BF16)
                        cx.Am = sb(stb, "Am", [128, 4, 128], BF16)
                        cx.An = [sb(stb, "An%d" % i, [128, 4, 128], BF16) for i in range(2)]
                        cx.Bn = [sb(stb, "Bn%d" % i, [128, 4, 128], BF16) for i in range(2)]
                        cx.Pt = [sb(stb, "Pt%d" % i, [128, 4, 128], BF16) for i in range(2)]
                        ctxs.append(cx)
                    Yb = sb(stb, "Yb", [128, 128], BF16)
                    Ub = sb(stb, "Ub", [128, 128], BF16)
                    Otok = sb(stb, "Otok", [128, 8, 128])
                    Q1 = sb(stb, "Q1", [128, 8, 128])
                    Q2 = sb(stb, "Q2", [128, 8, 128])
                    st16 = sb(stb, "st16", [128, 6, 16])
                    yatok = sb(stb, "yatok", [128, 8, 128], BF16)
                    yaT = sb(stb, "yaT", [128, 1040], BF16)
                    print("SBUF free in pairs scope", nc.sbuf_bytes_remaining)

                    def amats(cx, cp, j, B):
                        NW = 512 if own else 256
                        mstr = CB.ap[:, O_M4:O_M4 + 128].unsqueeze(1).to_broadcast([128, 2, 128])
                        minc = CB.ap[:, O_M4 + 128:O_M4 + 256].unsqueeze(1).to_broadcast([128, 2, 128])
                        for cl in range(2):
                            c = cp * 2 + cl
                            tcs = slice(c * 128, (c + 1) * 128)
                            us = slice(cl * 2, cl * 2 + 2)
                            PAT, PAK, PA = ps(), ps(), ps()
                            zr = B.Zar.ap[:, c, :, :].rearrange("p a t -> p (a t)")[:, 0:NW]
                            PE(lambda e: e.matmul(PAT.ap[:, 0:NW], lhsT=B.bT.ap[:, tcs], rhs=zr, start=True, stop=True), [B.bT, B.Zar], [PAT])
                            PE(lambda e: e.matmul(PAK.ap[:, 0:NW], lhsT=B.kT.ap[:, tcs], rhs=zr, start=True, stop=True), [B.kT, B.Zar], [PAK])
                            PE(lambda e: e.matmul(PA.ap[:, 0:256], lhsT=B.aT.ap[:, tcs], rhs=B.Zb.ap[:, c, :, :].rearrange("p a t -> p (a t)"), start=True, stop=True), [B.aT, B.Zb], [PA])
                            DVE(lambda e: e.tensor_tensor(out=cx.ATm.ap[:, us, 0:128], in0=PAT.ap[:, 0:256].rearrange("p (u t) -> p u t", u=2), in1=mstr, op=ALU.mult), [PAT, CB], [cx.ATm])
                            DVE(lambda e: e.tensor_tensor(out=cx.AKm.ap[:, us, 0:128], in0=PAK.ap[:, 0:256].rearrange("p (u t) -> p u t", u=2), in1=mstr, op=ALU.mult), [PAK, CB], [cx.AKm])
                            if own:
                                DVE(lambda e: e.tensor_tensor(out=cx.ATm.ap[:, us, 128:256], in0=PAT.ap[:, 256:512].rearrange("p (u t) -> p u t", u=2), in1=minc, op=ALU.mult), [PAT, CB], [cx.ATm])
                                DVE(lambda e: e.tensor_tensor(out=cx.AKm.ap[:, us, 128:256], in0=PAK.ap[:, 256:512].rearrange("p (u t) -> p u t", u=2), in1=minc, op=ALU.mult), [PAK, CB], [cx.AKm])
                            DVE(lambda e: e.tensor_tensor(out=cx.Am.ap[:, us, :], in0=PA.ap[:, 0:256].rearrange("p (u t) -> p u t", u=2), in1=CB.ap[:, O_MA:O_MA + 256].rearrange("p (u t) -> p u t", u=2), op=ALU.mult), [PA, CB], [cx.Am])
                        DVE(lambda e: e.tensor_tensor(out=cx.Pt[0].ap[:], in0=cx.ATm.ap[:, :, 0:128], in1=IDB.ap[:].unsqueeze(1).to_broadcast([128, 4, 128]), op=ALU.add), [cx.ATm, IDB], [cx.Pt[0]])
                        cx.Acur, cx.Bcur, cx.Pcur = cx.Am, cx.ATm, cx.Pt[0]
                        cx.Bcur_ap = cx.ATm.ap[:, :, 0:128]

                    def inverse(cxs):
                        for lvl in range(1, 7):
                            for cx in cxs:
                                cx.PAn = ps()
                                for u in range(4):
                                    PE(lambda e: e.matmul(cx.PAn.ap[:, u * 128:(u + 1) * 128], lhsT=cx.Bcur_ap[:, u, :], rhs=cx.Acur.ap[:, u, :], start=True, stop=True), [cx.Bcur, cx.Acur], [cx.PAn], inc=(u == 3))
                                if lvl < 6:
                                    cx.PBn = ps()
                                    for u in range(4):
                                        PE(lambda e: e.matmul(cx.PBn.ap[:, u * 128:(u + 1) * 128], lhsT=cx.Acur.ap[:, u, :], rhs=cx.Bcur_ap[:, u, :], start=True, stop=True), [cx.Bcur, cx.Acur], [cx.PBn], inc=(u == 3))
                            yield
                            for cx in cxs:
                                cx.Anew = cx.An[lvl % 2]
                                ACT(lambda e: e.activation(out=cx.Anew.ap[:], in_=cx.PAn.ap[:].rearrange("p (u t) -> p u t", u=4), func=AF.Copy), [cx.PAn], [cx.Anew])
                                if lvl < 6:
                                    cx.Bnew = cx.Bn[lvl % 2]
                                    ACT(lambda e: e.activation(out=cx.Bnew.ap[:], in_=cx.PBn.ap[:].rearrange("p (u t) -> p u t", u=4), func=AF.Copy), [cx.PBn], [cx.Bnew])
                            for cx in cxs:
                                cx.PP = ps()
                                for u in range(4):
                                    PE(lambda e: e.matmul(cx.PP.ap[:, u * 128:(u + 1) * 128], lhsT=cx.Anew.ap[:, u, :], rhs=cx.Pcur.ap[:, u, :], start=True, stop=True), [cx.Anew, cx.Pcur], [cx.PP], inc=(u == 3))
                            yield
                            for cx in cxs:
                                Pnew = cx.Pt[lvl % 2]
                                DVE(lambda e: e.tensor_tensor(out=Pnew.ap[:], in0=cx.PP.ap[:].rearrange("p (u t) -> p u t", u=4), in1=cx.Pcur.ap[:], op=ALU.add), [cx.PP, cx.Pcur], [Pnew])
                                cx.Acur, cx.Pcur = cx.Anew, Pnew
                                if lvl < 6:
                                    cx.Bcur = cx.Bnew
                                    cx.Bcur_ap = cx.Bnew.ap[:]

                    def chain(cx, cp, j, B):
                        for cl in range(2):
                            c = cp * 2 + cl
                            tcs = slice(c * 128, (c + 1) * 128)
                            PY = ps()
                            for h in range(2):
                                u = cl * 2 + h
                                hp = slice(h * 64, (h + 1) * 64)
                                PE(lambda e: e.matmul(PY.ap[:, hp], lhsT=B.aT.ap[:, tcs], rhs=SBF[j].ap[:, hp], start=True, stop=False), [B.aT, SBF[j]], [PY], inc=False)
                                PE(lambda e: e.matmul(PY.ap[:, hp], lhsT=cx.AKm.ap[:, u, 0:128], rhs=B.Vtok.ap[:, c, hp], start=False, stop=True), [cx.AKm, B.Vtok], [PY], inc=(h == 1))
                            ACT(lambda e: e.activation(out=Yb.ap[:], in_=PY.ap[:, 0:128], func=AF.Copy), [PY], [Yb])
                            PU = ps()
                            for h in range(2):
                                u = cl * 2 + h
                                hp = slice(h * 64, (h + 1) * 64)
                                PE(lambda e: e.matmul(PU.ap[:, hp], lhsT=cx.Pcur.ap[:, u, :], rhs=Yb.ap[:, hp], start=True, stop=True), [cx.Pcur, Yb], [PU], inc=(h == 1))
                            ACT(lambda e: e.activation(out=Ub.ap[:], in_=PU.ap[:, 0:128], func=AF.Copy), [PU], [Ub])
                            if own:
                                PO = ps()
                                for h in range(2):
                                    u = cl * 2 + h
                                    hp = slice(h * 64, (h + 1) * 64)
                                    PE(lambda e: e.matmul(PO.ap[:, hp], lhsT=B.rT.ap[:, tcs], rhs=SBF[j].ap[:, hp], start=True, stop=False), [B.rT, SBF[j]], [PO], inc=False)
                                    PE(lambda e: e.matmul(PO.ap[:, hp], lhsT=cx.ATm.ap[:, u, 128:256], rhs=Ub.ap[:, hp], start=False, stop=False), [cx.ATm, Ub], [PO], inc=False)
                                    PE(lambda e: e.matmul(PO.ap[:, hp], lhsT=cx.AKm.ap[:, u, 128:256], rhs=B.Vtok.ap[:, c, hp], start=False, stop=True), [cx.AKm, B.Vtok], [PO], inc=(h == 1))
                                ACT(lambda e: e.activation(out=Otok.ap[:, c, :], in_=PO.ap[:, 0:128], func=AF.Copy), [PO], [Otok])
                            PSn = ps()
                            for h in range(2):
                                hp = slice(h * 64, (h + 1) * 64)
                                PE(lambda e: e.matmul(PSn.ap[:, hp], lhsT=B.Bhat.ap[:, c, :], rhs=Ub.ap[:, hp], start=True, stop=False), [B.Bhat, Ub], [PSn], inc=False)
                                PE(lambda e: e.matmul(PSn.ap[:, hp], lhsT=B.Khat.ap[:, c, :], rhs=B.Vtok.ap[:, c, hp], start=False, stop=True), [B.Khat, B.Vtok], [PSn], inc=(h == 1))
                            DVE(lambda e: e.scalar_tensor_tensor(out=SST[j].ap[:], in0=SST[j].ap[:], scalar=B.SC.ap[:, c:c + 1], in1=PSn.ap[:, 0:128], op0=ALU.mult, op1=ALU.add), [SST[j], B.SC, PSn], [SST[j]])
                            DVE(lambda e: e.tensor_tensor(out=SBF[j].ap[:], in0=SST[j].ap[:], in1=CB.ap[:, O_BO:O_BO + 128], op=ALU.mult), [SST[j], CB], [SBF[j]])
                            yield


                    def prep(j, B):
                        jc = slice(j * 128, (j + 1) * 128)
                        kkc = VB.ap[:, V_KK + j:V_KK + j + 1]
                        kac = VB.ap[:, V_KA + j:V_KA + j + 1]
                        rkc = VB.ap[:, V_RK + j:V_RK + j + 1]
                        a0c = VB.ap[:, V_A0 + j:V_A0 + j + 1]
                        S.dma("sp", T1.ap[:, 0:NM], PR_d[3 * j, :, 0:NM], reads=[PRtk[3 * j]], writes=[T1])
                        S.dma("sp", T6.ap[:, 0:NM], PR_d[3 * j + 1, :, 0:NM], reads=[PRtk[3 * j + 1]], writes=[T6])
                        if own:
                            S.dma("sp", T5.ap[:, 0:NM], PR_d[3 * j + 2, :, 0:NM], reads=[PRtk[3 * j + 2]], writes=[T5])
                            S.dma("sp", B.LGp.ap[:], lg_d[:, jc].partition_broadcast(128), writes=[B.LGp])
                            S.dma("sp", B.LBp.ap[:], lb_d[:, jc].partition_broadcast(128), writes=[B.LBp])
                        yield
                        for c4 in range(2):
                            P = ps()
                            for cc in range(4):
                                c = c4 * 4 + cc
                                PE(lambda e: e.matmul(P.ap[:, cc * 128:(cc + 1) * 128], lhsT=TW.ap[0:65, c * 128:(c + 1) * 128], rhs=wupa.ap[0:65, jc], start=True, stop=True), [TW, wupa], [P], inc=(cc == 3))
                            ACT(lambda e: e.activation(out=sgtok.ap[:, c4 * 4:(c4 + 1) * 4, :], in_=P.ap[:].rearrange("p (c t) -> p c t", c=4), func=AF.Sigmoid), [P], [sgtok])
                        yield
                        if own:
                            P = ps()
                            PE(lambda e: e.matmul(P.ap[:, 0:16], lhsT=wupa.ap[0:65, jc], rhs=TW.ap[0:65, 1024:1040], start=True, stop=True), [TW, wupa], [P])
                            ACT(lambda e: e.activation(out=st16w.ap[:, 0:16], in_=P.ap[:, 0:16], func=AF.Sigmoid), [P], [st16w])
                            for hh in range(2):
                                hq = slice(hh * 64, (hh + 1) * 64)
                                ACT(lambda e: e.activation(out=SMZ.ap[hq, j, 4, hh * 16:(hh + 1) * 16], in_=st16w.ap[hq, 0:16], func=AF.Exp, scale=-C_DEC), [st16w], [SMZ])
                        yield
                        for (t0, t1) in groups:
                            t1 = min(t1, NM)
                            P = ps()
                            PE(lambda e: e.matmul(P.ap[:, 0:t1 - t0], lhsT=aup.ap[64:128, jc], rhs=aLo.ap[64:128, t0:t1], start=True, stop=True), [aup, aLo], [P])
                            ACT(lambda e: e.activation(out=T2.ap[:, t0:t1], in_=P.ap[:, 0:t1 - t0], func=AF.Sigmoid, bias=a0c), [P, VB], [T2])
                        yield
                        yield
                        for c in range(NCH):
                            P = ps()
                            PE(lambda e: e.matmul(P.ap[:, 0:258], lhsT=sgtok.ap[:, c, :], rhs=CB.ap[:, O_CM:O_CM + 258], start=True, stop=True), [sgtok, CB], [P])
                            ACT(lambda e: e.activation(out=Lf.ap[:, c, :], in_=P.ap[:, 0:258], func=AF.Copy), [P], [Lf])
                        yield
                        for c4 in range(2):
                            P = ps()
                            for cc in range(4):
                                c = c4 * 4 + cc
                                last = (c == NCH - 1)
                                PE(lambda e: e.matmul(P.ap[:, cc * 128:(cc + 1) * 128], lhsT=CB.ap[:, O_T2:O_T2 + 128], rhs=sgtok.ap[:, c, :], start=True, stop=last), [sgtok, CB], [P], inc=last)
                                if not last:
                                    PE(lambda e: e.matmul(P.ap[:, cc * 128:(cc + 1) * 128], lhsT=CB.ap[:, O_F64:O_F64 + 128], rhs=sgtok.ap[:, c + 1, :], start=False, stop=True), [sgtok, CB], [P], inc=True)
                            ACT(lambda e: e.activation(out=ED.ap[:, c4 * 4:(c4 + 1) * 4, :], in_=P.ap[:].rearrange("p (c t) -> p c t", c=4), func=AF.Exp), [P], [ED])
                        yield
                        yield
                        DVE(lambda e: e.tensor_tensor(out=B.SC.ap[:, 0:NCH - 1].unsqueeze(2), in0=Lf.ap[:, 0:NCH - 1, 256:257], in1=Lf.ap[:, 1:NCH, 257:258], op=ALU.add), [Lf], [B.SC])
                        yield
                        DVE(lambda e: e.tensor_copy(out=B.SC.ap[:, NCH - 1:NCH], in_=Lf.ap[:, NCH - 1, 256:257]), [Lf], [B.SC])
                        yield
                        DVE(lambda e: e.tensor_copy(out=B.SC.ap[:, 8:9], in_=Lf.ap[:, 0, 257:258]), [Lf], [B.SC])
                        yield
                        ACT(lambda e: e.activation(out=B.SC.ap[:, 0:9], in_=B.SC.ap[:, 0:9], func=AF.Exp), [B.SC], [B.SC])
                        yield
                        DVE(lambda e: e.tensor_scalar(out=SST[j].ap[:], in0=SST[j].ap[:], scalar1=B.SC.ap[:, 8:9], scalar2=None, op0=ALU.mult), [SST[j], B.SC], [SST[j]])
                        yield
                        DVE(lambda e: e.tensor_tensor(out=SBF[j].ap[:], in0=SST[j].ap[:], in1=CB.ap[:, O_BO:O_BO + 128], op=ALU.mult), [SST[j], CB], [SBF[j]])
                        yield
                        yield
                        yield
                        ACT(lambda e: e.activation(out=sqb.ap[:, 0:NM], in_=T1.ap[:, 0:NM], func=AF.Square, scale=kkc), [T1, VB], [sqb])
                        yield
                        for (t0, t1) in groups:
                            t1 = min(t1, NM)
                            P = ps()
                            PE(lambda e: e.matmul(P.ap[:, 0:t1 - t0], lhsT=BOB.ap[:], rhs=sqb.ap[:, t0:t1], start=True, stop=True), [BOB, sqb], [P])
                            DVE(lambda e: e.tensor_scalar(out=T0.ap[:, t0:t1], in0=P.ap[:, 0:t1 - t0], scalar1=1e-24, scalar2=None, op0=ALU.max), [P], [T0])
                        yield
                        ACT(lambda e: e.activation(out=T0.ap[:, 0:NM], in_=T0.ap[:, 0:NM], func=AF.Sqrt), [T0], [T0])
                        yield
                        ACT(lambda e: e.activation(out=T0.ap[:, 0:NM], in_=T0.ap[:, 0:NM], func=AF.Ln), [T0], [T0])
                        yield
                        ACT(lambda e: e.activation(out=T0.ap[:, 0:NM], in_=T0.ap[:, 0:NM], func=AF.Exp, scale=-1.0), [T0], [T0])
                        yield
                        DVE(lambda e: e.scalar_tensor_tensor(out=T0.ap[:, 0:NM], in0=T1.ap[:, 0:NM], scalar=kkc, in1=T0.ap[:, 0:NM], op0=ALU.mult, op1=ALU.mult), [T0, T1, VB], [T0])
                        yield
                        DVE(lambda e: e.tensor_tensor(out=T3.ap[:, 0:NM], in0=T0.ap[:, 0:NM], in1=T2.ap[:, 0:NM], op=ALU.mult), [T0, T2], [T3])
                        yield
                        DVE(lambda e: e.tensor_scalar(out=T2.ap[:, 0:NM], in0=T2.ap[:, 0:NM], scalar1=-1.0, scalar2=kac, op0=ALU.add, op1=ALU.mult), [T2, VB], [T2])
                        yield
                        DVE(lambda e: e.scalar_tensor_tensor(out=T2.ap[:, 0:NM], in0=T2.ap[:, 0:NM], scalar=1.0, in1=T1.ap[:, 0:NM], op0=ALU.add, op1=ALU.mult), [T2, T1], [T2])
                        yield
                        T1v = T1.ap[:, 0:1024].rearrange("p (c t) -> p c t", c=8)
                        yield
                        ACT(lambda e: e.activation(out=T1v, in_=Lf.ap[:, :, 128:256], func=AF.Exp), [Lf], [T1])
                        yield
                        DVE(lambda e: e.scalar_tensor_tensor(out=B.aT.ap[:], in0=T0.ap[:, 0:1024], scalar=-1.0, in1=T1.ap[:, 0:1024], op0=ALU.mult, op1=ALU.mult), [T0, T1], [B.aT])
                        yield
                        ACT(lambda e: e.activation(out=T1v, in_=Lf.ap[:, :, 0:128], func=AF.Exp, scale=-1.0), [Lf], [T1])
                        yield
                        DVE(lambda e: e.tensor_tensor(out=B.bT.ap[:], in0=T3.ap[:, 0:1024], in1=T1.ap[:, 0:1024], op=ALU.mult), [T3, T1], [B.bT])
                        yield
                        DVE(lambda e: e.tensor_tensor(out=B.kT.ap[:], in0=T2.ap[:, 0:1024], in1=T1.ap[:, 0:1024], op=ALU.mult), [T2, T1], [B.kT])
                        yield
                        for hh in range(2):
                            hq = slice(hh * 64, (hh + 1) * 64)
                            ACT(lambda e: e.activation(out=B.Zar.ap[hq, :, hh, :], in_=B.aT.ap[hq, :].rearrange("p (c t) -> p c t", c=8), func=AF.Copy), [B.aT], [B.Zar])
                            ACT(lambda e: e.activation(out=B.Zb.ap[hq, :, hh, :], in_=B.bT.ap[hq, :].rearrange("p (c t) -> p c t", c=8), func=AF.Copy), [B.bT], [B.Zb])
                        yield
                        if own:
                            ACT(lambda e: e.activation(out=T1v, in_=Lf.ap[:, :, 0:128], func=AF.Exp), [Lf], [T1])
                            DVE(lambda e: e.tensor_tensor(out=B.rT.ap[:], in0=T5.ap[:, 0:1024], in1=T1.ap[:, 0:1024], op=ALU.mult), [T5, T1], [B.rT])
                            for hh in range(2):
                                hq = slice(hh * 64, (hh + 1) * 64)
                                ACT(lambda e: e.activation(out=B.Zar.ap[hq, :, 2 + hh, :], in_=B.rT.ap[hq, :].rearrange("p (c t) -> p c t", c=8), func=AF.Copy), [B.rT], [B.Zar])
                            DVE(lambda e: e.scalar_tensor_tensor(out=B.prodb.ap[:, 0:1040], in0=T5.ap[:, 0:1040], scalar=rkc, in1=T2.ap[:, 0:1040], op0=ALU.mult, op1=ALU.mult), [T5, T2, VB], [B.prodb])
                        yield
                        yield
                        for c4 in range(2):
                            c4s = slice(c4 * 4, (c4 + 1) * 4)
                            for (src, kind) in ((T3, 0), (T2, 1), (T6, 2)):
                                Pq_ = ps()
                                for cc in range(4):
                                    c = c4 * 4 + cc
                                    tcs = slice(c * 128, (c + 1) * 128)
                                    PE(lambda e: e.transpose(out=Pq_.ap[:, cc * 128:(cc + 1) * 128], in_=src.ap[:, tcs], identity=ident), [src, CB], [Pq_], inc=(cc == 3))
                                pv = Pq_.ap[:].rearrange("p (c t) -> p c t", c=4)
                                if kind == 0:
                                    DVE(lambda e: e.tensor_tensor(out=B.Bhat.ap[:, c4s, :], in0=pv, in1=ED.ap[:, c4s, :], op=ALU.mult), [Pq_, ED], [B.Bhat])
                                elif kind == 1:
                                    DVE(lambda e: e.tensor_tensor(out=B.Khat.ap[:, c4s, :], in0=pv, in1=ED.ap[:, c4s, :], op=ALU.mult), [Pq_, ED], [B.Khat])
                                else:
                                    ACT(lambda e: e.activation(out=B.Vtok.ap[:, c4s, :], in_=pv, func=AF.Copy), [Pq_], [B.Vtok])
                            yield
                        yield
                        if own:
                            for hh in range(2):
                                hq = slice(hh * 64, (hh + 1) * 64)
                                zc = slice(hh * 16, (hh + 1) * 16)
                                DVE(lambda e: e.tensor_scalar(out=SMZ.ap[hq, j, 0, zc], in0=T0.ap[hq, 1024:1040], scalar1=-1.0, scalar2=None, op0=ALU.mult), [T0], [SMZ])
                                DVE(lambda e: e.tensor_copy(out=SMZ.ap[hq, j, 1, zc], in_=T3.ap[hq, 1024:1040]), [T3], [SMZ])
                                DVE(lambda e: e.tensor_copy(out=SMZ.ap[hq, j, 2, zc], in_=T2.ap[hq, 1024:1040]), [T2], [SMZ])
                                DVE(lambda e: e.tensor_copy(out=SMZ.ap[hq, j, 3, zc], in_=T5.ap[hq, 1024:1040]), [T5], [SMZ])
                            DVE(lambda e: e.tensor_copy(out=SMv.ap[:, j, :], in_=T6.ap[:, 1024:1040]), [T6], [SMv])
                            P = ps()
                            PE(lambda e: e.matmul(P.ap[0:16, 0:2], lhsT=B.prodb.ap[:, 1024:1040], rhs=BSB.ap[:], start=True, stop=True), [B.prodb, BSB], [P])
                            ACT(lambda e: e.activation(out=RKS.ap[0:16, 2 * j:2 * j + 2], in_=P.ap[0:16, 0:2], func=AF.Copy), [P], [RKS])

                    def Xg(j, g, B):
                        cxa, cxb = ctxs[2 * g], ctxs[2 * g + 1]
                        amats(cxa, 2 * g, j, B)
                        yield
                        amats(cxb, 2 * g + 1, j, B)
                        yield
                        for _ in inverse([cxa, cxb]):
                            yield

                    def Cg(j, g, B):
                        for _ in chain(ctxs[2 * g], 2 * g, j, B):
                            yield
                        for _ in chain(ctxs[2 * g + 1], 2 * g + 1, j, B):
                            yield

                    def postg(j, B):
                        jc = slice(j * 128, (j + 1) * 128)
                        yield
                        if own:
                            yield
                            P = ps()
                            PE(lambda e: e.transpose(out=P.ap[:, 0:128], in_=SST[j].ap[:], identity=ident), [SST[j], CB], [P])
                            ACT(lambda e: e.activation(out=Q1.ap[:, 0, :], in_=P.ap[:, 0:128], func=AF.Copy), [P], [Q1])
                            for h in range(2):
                                hp = slice(h * 64, (h + 1) * 64)
                                S.dma("sp", wkvp_o[2 * j + h], Q1.ap[hp, 0, hp], reads=[Q1], is_output=True)
                            post(128, 8, 2, Otok, Otok.ap[:].rearrange("p c (h v) -> p (c h) v", h=2), None,
                                 Q1, Q1.ap[:].rearrange("p c (h v) -> p (c h) v", h=2), Q1.ap[:].rearrange("p c (h v) -> p c h v", h=2),
                                 Q2, Q2.ap[:].rearrange("p c (h v) -> p (c h) v", h=2), st16,
                                 B.LGp.ap[:].rearrange("p (h v) -> p h v", h=2).unsqueeze(1).to_broadcast([128, 8, 2, 64]),
                                 B.LBp.ap[:].rearrange("p (h v) -> p h v", h=2).unsqueeze(1).to_broadcast([128, 8, 2, 64]), B.LGp, B.LBp)
                            P = ps()
                            for c in range(8):
                                PE(lambda e: e.matmul(P.ap[:, 2 * c:2 * c + 2], lhsT=B.prodb.ap[:, c * 128:(c + 1) * 128], rhs=BSB.ap[:], start=True, stop=True), [B.prodb, BSB], [P], inc=(c == 7))
                            ACT(lambda e: e.activation(out=st16.ap[:, 5, :], in_=P.ap[:, 0:16], func=AF.Copy), [P], [st16])
                            Q1v = Q1.ap[:].rearrange("p c (h v) -> p (c h) v", h=2)
                            Q2v = Q2.ap[:].rearrange("p c (h v) -> p (c h) v", h=2)
                            DVE(lambda e: e.tensor_tensor(out=Q2v, in0=B.Vtok.ap[:].rearrange("p c (h v) -> p (c h) v", h=2), in1=st16.ap[:, 5, :].unsqueeze(2).to_broadcast([128, 16, 64]), op=ALU.mult), [B.Vtok, st16], [Q2])
                            DVE(lambda e: e.tensor_tensor(out=Q1.ap[:], in0=Q1.ap[:], in1=Q2.ap[:], op=ALU.add), [Q1, Q2], [Q1])
                            for c4 in range(2):
                                P = ps()
                                for cc in range(4):
                                    c = c4 * 4 + cc
                                    tcs = slice(c * 128, (c + 1) * 128)
                                    PE(lambda e: e.matmul(P.ap[:, cc * 128:(cc + 1) * 128], lhsT=sgT1.ap[:, tcs], rhs=gup1.ap[:, jc], start=True, stop=False), [sgT1, gup1], [P], inc=False)
                                    PE(lambda e: e.matmul(P.ap[:, cc * 128:(cc + 1) * 128], lhsT=sgT2.ap[0:32, tcs], rhs=gup2.ap[0:32, jc], start=False, stop=True), [sgT2, gup2], [P], inc=(cc == 3))
                                c4s = slice(c4 * 4, (c4 + 1) * 4)
                                DVE(lambda e: e.tensor_tensor(out=yatok.ap[:, c4s, :], in0=P.ap[:].rearrange("p (c t) -> p c t", c=4), in1=Q1.ap[:, c4s, :], op=ALU.mult), [P, Q1], [yatok])
                            P = ps()
                            pb = P.ap[:].bitcast(BF16)
                            for c in range(8):
                                PE(lambda e: e.transpose(out=pb[:, c * 128:(c + 1) * 128], in_=yatok.ap[:, c, :], identity=IDB.ap[:]), [yatok, IDB], [P], inc=(c == 7))
                            ACT(lambda e: e.activation(out=yaT.ap[:, 0:1024], in_=pb, func=AF.Copy), [P], [yaT])
                            S.dma("sp", YT_d[j, :, 0:1024], yaT.ap[:, 0:1024], reads=[yaT], writes=[YTk])

                    tasks = []
                    PX, PC, PP_ = (0, 4), (4, 2), (6, 2)
                    for j in range(8):
                        B = bufs[j % 2]
                        tasks.append((("P", j), (lambda j=j, B=B: prep(j, B)), [("P", j - 1)] * (j > 0) + [("O", j - 2)] * (j > 1), PP_))
                        tasks.append((("X", j, 0), (lambda j=j, B=B: Xg(j, 0, B)), [("P", j)] + [("C", j - 1, 0)] * (j > 0), PX))
                        tasks.append((("X", j, 1), (lambda j=j, B=B: Xg(j, 1, B)), [("X", j, 0)] + [("C", j - 1, 1)] * (j > 0), PX))
                        tasks.append((("C", j, 0), (lambda j=j, B=B: Cg(j, 0, B)), [("X", j, 0)] + [("O", j - 1)] * (j > 0), PC))
                        tasks.append((("C", j, 1), (lambda j=j, B=B: Cg(j, 1, B)), [("X", j, 1), ("C", j, 0)], PC))
                        tasks.append((("O", j), (lambda j=j, B=B: postg(j, B)), [("C", j, 1)], PC))
                    run_tasks(tasks)
                    S.barrier()
                    ck(9 if own else 6)
                if own:
                    sts_ = contextlib.ExitStack()
                    stg_ = contextlib.ExitStack()
                    run_tasks([("S", (lambda: sample_gen(sts_, (sgT1, sgT2, gup1, gup2))), [], (0, 4)),
                               ("G", (lambda: sgu_gen(stg_)), [], (4, 4))])
                    S.barrier()
                    stg_.close()
                    sts_.close()
                    ck(11)
                S.barrier()
        rest_phase()
        S.stopped = False
        S.finish("sp")
        print("instructions", S.n_ins, "waits", S.n_wait)
    return nc


_NC_CACHE = {}


def _perm_w_in(w):
    idx = []
    for j in range(8):
        idx += list(range(1024 + j * 128, 1024 + (j + 1) * 128))
        idx += list(range(2048 + j * 128, 2048 + (j + 1) * 128))
        idx += list(range(j * 128, (j + 1) * 128))
    idx += list(range(3072, 5408))
    return np.ascontiguousarray(w[:, np.asarray(idx)])


def kernel(**inp):
    f32 = np.float32
    xp = np.asarray(inp["x_prompt"], f32)
    xs = np.asarray(inp["x_sample"], f32)
    swkv = np.asarray(inp["state_wkv"], f32)
    ssh = np.asarray(inp["state_shift"], f32)
    if "nc" not in _NC_CACHE:
        _NC_CACHE["nc"] = build()
    nc = _NC_CACHE["nc"]
    cb = make_consts()
    vb = np.zeros((128, NVB), f32)
    mu = np.zeros(27 * 128, f32)
    mu[:DSH] = np.asarray(inp["mu_shift"], f32)[0]
    vb[:, V_MU:V_MU + 27] = mu.reshape(27, 128).T
    vb[:, V_KK:V_KK + 8] = np.asarray(inp["k_k"], f32)[0].reshape(8, 128).T
    vb[:, V_KA:V_KA + 8] = np.asarray(inp["k_a"], f32)[0].reshape(8, 128).T
    vb[:, V_RK:V_RK + 8] = np.asarray(inp["r_k"], f32)[0].reshape(8, 128).T
    vb[:, V_A0:V_A0 + 8] = np.asarray(inp["a0"], f32)[0].reshape(8, 128).T
    vb[:, V_SG:V_SG + 8] = np.asarray(inp["sgu_norm_g"], f32)[0].reshape(8, 128).T
    vb[:, V_SB:V_SB + 8] = np.asarray(inp["sgu_norm_b"], f32)[0].reshape(8, 128).T
    vb[:, V_W00:V_W00 + 8] = np.asarray(inp["sgu_w"], f32)[0][:, 0, 0][None, :]
    vb[:, V_B0:V_B0 + 8] = np.asarray(inp["sgu_b"], f32)[0][:, 0][None, :]
    wupa = np.concatenate([np.asarray(inp["w_up"], f32)[0], np.asarray(inp["w0"], f32)[0][None, :]], 0)
    common = {
        "w_in": _perm_w_in(np.asarray(inp["w_in"], f32)[0]),
        "w_out": np.ascontiguousarray(np.asarray(inp["w_out"], f32)[0]),
        "w_fu": np.ascontiguousarray(np.asarray(inp["w_ffn_up"], f32)[0]),
        "w_fd": np.ascontiguousarray(np.asarray(inp["w_ffn_down"], f32)[0]),
        "wupa": np.ascontiguousarray(wupa),
        "aup": np.ascontiguousarray(np.asarray(inp["a_up"], f32)[0]),
        "gup": np.ascontiguousarray(np.asarray(inp["g_up"], f32)[0]),
        "g1": np.ascontiguousarray(np.asarray(inp["norm1_g"], f32).reshape(1, D)),
        "g2": np.ascontiguousarray(np.asarray(inp["norm2_g"], f32).reshape(1, D)),
        "gf": np.ascontiguousarray(np.asarray(inp["norm_f_g"], f32).reshape(1, D)),
        "lg": np.ascontiguousarray(np.asarray(inp["lnx_g"], f32).reshape(1, 1024)),
        "lb": np.ascontiguousarray(np.asarray(inp["lnx_b"], f32).reshape(1, 1024)),
        "sgb": np.ascontiguousarray(np.asarray(inp["sgu_b"], f32).reshape(1, 1024)),
        "sgw": np.ascontiguousarray(np.asarray(inp["sgu_w"], f32)[0]),
        "cb": cb,
        "sel2": make_sel2(),
        "vb": vb,
    }
    in_maps = []
    for c in range(8):
        b, half = divmod(c, 2)
        xin = np.zeros((2048, D), f32)
        xsm = np.zeros((17, D), f32)
        if half == 1:
            xin[:] = xp[b]
            xsm[16] = xp[b, 1023]
        else:
            xin[1024:] = xp[b, :1024]
        xsm[:16] = xs[16 * c:16 * (c + 1), 0]
        shpad = np.zeros((16, 27 * 128), f32)
        shpad[:, :DSH] = ssh[0, 16 * c:16 * (c + 1)]
        shT = np.ascontiguousarray(shpad.reshape(16, 27, 128).transpose(2, 1, 0).reshape(128, 27 * 16))
        m = dict(common)
        m.update({"xin": xin, "xsm": xsm, "swkv": np.ascontiguousarray(swkv[0, 16 * c:16 * (c + 1)]), "shT": shT})
        in_maps.append(m)
    res = run_bass_kernel_spmd(nc, in_maps, core_ids=list(range(8)))
    R = res.results
    y_prompt = np.zeros((4, 2048, D), f32)
    y_sample = np.zeros((128, 1, D), f32)
    wkv_prompt = np.zeros((1, 4, 16, 64, 64), f32)
    shift_prompt = np.zeros((1, 4, DSH), f32)
    wkv_sample = np.zeros((1, 128, 16, 64, 64), f32)
    shift_sample = np.zeros((1, 128, DSH), f32)
    sgu_v_sample = np.zeros((1, 128, 1, 1024), f32)
    for c in range(8):
        b, half = divmod(c, 2)
        r = R[c]
        y_prompt[b, half * 1024:(half + 1) * 1024] = r["y"]
        y_sample[16 * c:16 * (c + 1), 0] = r["ysm"]
        if half == 1:
            wkv_prompt[0, b] = r["wkvp"]
            shift_prompt[0, b] = r["shp"][0, :DSH]
        wkv_sample[0, 16 * c:16 * (c + 1)] = r["wkvs"]
        shift_sample[0, 16 * c:16 * (c + 1)] = r["shp"][1:17, :DSH]
        sgu_v_sample[0, 16 * c:16 * (c + 1), 0] = r["sguv"]
    return (y_prompt, y_sample, wkv_prompt, shift_prompt, wkv_sample, shift_sample, sgu_v_sample)
```

```python
import contextlib
import numpy as np
import concourse.bass as bass
import concourse.mybir as mybir
from concourse.bass_utils import run_bass_kernel_spmd

F32 = mybir.dt.float32
BF16 = mybir.dt.bfloat16
AF = mybir.ActivationFunctionType
ALU = mybir.AluOpType
AX = mybir.AxisListType


class Tk:
    __slots__ = ("ap", "w", "rs", "name")

    def __init__(self, ap, name=""):
        self.ap = ap
        self.w = None
        self.rs = []
        self.name = name

    def __getitem__(self, k):
        return self.ap[k]


class Q:
    def __init__(self, name, h, sem):
        self.name = name
        self.h = h
        self.sem = sem
        self.cnt = 0
        self.seen = {}


class Sched:
    N_DMA_SEMS = 30
    N_SW_SEMS = 64

    def __init__(self, nc, stack):
        self.nc = nc
        self.sems = {}
        self.q = {}
        for name, h in (("pe", nc.tensor), ("act", nc.scalar), ("dve", nc.vector),
                        ("pool", nc.gpsimd), ("sp", nc.sync)):
            s = stack.enter_context(nc.semaphore("q_" + name))
            self.sems[name] = s
            self.q[name] = Q(name, h, s)
        self.dma_sems = []
        for i in range(self.N_DMA_SEMS):
            key = "d%d" % i
            s = stack.enter_context(nc.semaphore(key))
            self.sems[key] = s
            self.dma_sems.append([key, 0])
        self.dma_rr = 0
        self.sw_sems = []
        for i in range(self.N_SW_SEMS):
            key = "w%d" % i
            s = stack.enter_context(nc.semaphore(key))
            self.sems[key] = s
            self.sw_sems.append([key, 0])
        self.sw_next = 0
        self.out_deps = []
        self.n_ins = 0
        self.n_wait = 0
        self.stopped = False

    def _wait(self, q, deps):
        best = {}
        for d in deps:
            if d is None:
                continue
            k, v = d
            if best.get(k, 0) < v:
                best[k] = v
        for k, v in best.items():
            if q.seen.get(k, 0) >= v:
                continue
            if k == q.name and k == "pe":
                continue
            q.h.wait_ge(self.sems[k], v)
            q.seen[k] = v
            self.n_wait += 1

    def _deps(self, reads, writes):
        deps = []
        for t in reads:
            deps.append(t.w)
        for t in writes:
            deps.append(t.w)
            deps.extend(t.rs)
        return deps

    def _mark(self, tok, reads, writes):
        for t in reads:
            t.rs.append(tok)
        for t in writes:
            t.w = tok
            t.rs = []

    def op(self, qn, fn, reads=(), writes=(), inc=True):
        if self.stopped:
            return None
        q = self.q[qn]
        self._wait(q, self._deps(reads, writes))
        ins = fn(q.h)
        self.n_ins += 1
        if inc:
            q.cnt += 1
            ins.then_inc(q.sem, 1)
            tok = (qn, q.cnt)
            q.seen[qn] = max(q.seen.get(qn, 0), 0)
        else:
            tok = (qn, q.cnt + 1)
        self._mark(tok, reads, writes)
        return ins

    def dma(self, qn, out, in_, reads=(), writes=(), is_output=False, **kw):
        if self.stopped:
            return None
        q = self.q[qn]
        if qn == "pool":
            slot = self.sw_sems[self.sw_next]
            self.sw_next += 1
        else:
            slot = self.dma_sems[self.dma_rr]
            self.dma_rr = (self.dma_rr + 1) % len(self.dma_sems)
        deps = self._deps(reads, writes)
        if slot[1] > 0:
            deps.append((slot[0], slot[1]))
        self._wait(q, deps)
        ins = q.h.dma_start(out=out, in_=in_, **kw)
        slot[1] += 16
        ins.then_inc(self.sems[slot[0]], 16)
        self.n_ins += 1
        tok = (slot[0], slot[1])
        self._mark(tok, reads, writes)
        if is_output:
            self.out_deps.append(tok)
        return tok

    def finish(self, qn="sp"):
        q = self.q[qn]
        self._wait(q, self.out_deps)

    def barrier(self):
        if self.stopped:
            return
        toks = []
        for qn, q in self.q.items():
            if q.cnt > 0:
                toks.append((qn, q.cnt))
        for k, v in self.dma_sems + self.sw_sems:
            if v > 0:
                toks.append((k, v))
        for qn, q in self.q.items():
            self._wait(q, toks)

D = 2048
DSH = 3360
DFF = 8192
C_DEC = float(np.exp(-0.5))
RMS_EPS = 1e-5
LN_EPS = 1e-5
GN_EPS = 64e-5

O_ID = 0
O_CM = O_ID + 128
O_T2 = O_CM + 258
O_F64 = O_T2 + 128
O_M4 = O_F64 + 128
O_MA = O_M4 + 512
O_BO = O_MA + 512
O_BS = O_BO + 128
O_OM = O_BS + 2
O_SEL = O_OM + 128
NCB = O_SEL + 1024
V_MU = 0
V_KK = 27
V_KA = 35
V_RK = 43
V_A0 = 51
V_SG = 59
V_SB = 67
V_W00 = 75
V_B0 = 83
NVB = 91


def make_consts():
    cb = np.zeros((128, NCB), np.float32)
    t = np.arange(128)
    cb[:, O_ID:O_ID + 128] = np.eye(128)
    le63 = (t <= 63).astype(np.float32)[:, None]
    cb[:, O_CM:O_CM + 128] = -C_DEC * ((t[:, None] <= t[None, :]).astype(np.float32) - le63)
    cb[:, O_CM + 128:O_CM + 256] = -C_DEC * ((t[:, None] < t[None, :]).astype(np.float32) - le63)
    cb[:, O_CM + 256] = -C_DEC * (t >= 64)
    cb[:, O_CM + 257] = -C_DEC * (t <= 63)
    cb[:, O_T2:O_T2 + 128] = -C_DEC * (t[:, None] > t[None, :])
    cb[:, O_F64:O_F64 + 128] = -C_DEC * le63
    mus = (t[:, None] < t[None, :]).astype(np.float32)
    mui = (t[:, None] <= t[None, :]).astype(np.float32)
    cb[:, O_M4:O_M4 + 512] = np.concatenate([mus, mui, mus, mui], 1)
    mls = (t[None, :] < t[:, None]).astype(np.float32)
    cb[:, O_MA:O_MA + 512] = np.concatenate([mls] * 4, 1)
    cb[:, O_BO:O_BO + 128] = (t[:, None] // 64 == t[None, :] // 64)
    cb[:, O_BS] = (t < 64)
    cb[:, O_BS + 1] = (t >= 64)
    cb[:, O_OM:O_OM + 128] = 1.0 / 1024
    for b in range(16):
        cb[b, O_SEL + b * 64:O_SEL + (b + 1) * 64] = 1.0
    return cb


class StopBuild(Exception):
    pass


def make_sel2():
    z = np.zeros((32, 16 * 128), np.float32)
    for b in range(16):
        z[b, b * 128:b * 128 + 64] = 1.0
        z[16 + b, b * 128 + 64:(b + 1) * 128] = 1.0
    return z


def build():
    import os
    STAGE = int(os.environ.get("KSTAGE", "0"))

    def ck(n):
        if STAGE == n and n > 0:
            SREF[0].stopped = True
    SREF = [None]
    nc = bass.Bass("TRN2", target_bir_lowering=False)

    def din(name, shape):
        return nc.dram_tensor(name, shape, F32, kind="ExternalInput").ap()

    def dout(name, shape):
        return nc.dram_tensor(name, shape, F32, kind="ExternalOutput").ap()

    xin = din("xin", [2048, D])
    xsm = din("xsm", [17, D])
    swkv = din("swkv", [16, 16, 64, 64])
    shT_d = din("shT", [128, 27 * 16])
    w_in = din("w_in", [D, 5408])
    w_out = din("w_out", [D, D])
    w_fu = din("w_fu", [D, DFF])
    w_fd = din("w_fd", [DFF, D])
    wupa_d = din("wupa", [65, 1024])
    aup_d = din("aup", [64, 1024])
    gup_d = din("gup", [160, 1024])
    g1_d = din("g1", [1, D])
    g2_d = din("g2", [1, D])
    gf_d = din("gf", [1, D])
    lg_d = din("lg", [1, 1024])
    lb_d = din("lb", [1, 1024])
    sgb_d = din("sgb", [1, 1024])
    sgw_d = din("sgw", [8, 128, 128])
    cb_d = din("cb", [128, NCB])
    vb_d = din("vb", [128, NVB])
    sel2_d = din("sel2", [32, 16 * 128])

    y_o = dout("y", [1024, D])
    ysm_o = dout("ysm", [16, D])
    wkvp_o = dout("wkvp", [16, 64, 64])
    shp_o = dout("shp", [17, 27 * 128])
    wkvs_o = dout("wkvs", [16, 16, 64, 64])
    sguv_o = dout("sguv", [16, 1024])

    YT_d = nc.dram_tensor("YT", [16, 128, 1040], BF16).ap()
    PR_d = nc.dram_tensor("PR", [24, 128, 1041], F32).ap()
    VZ_d = nc.dram_tensor("VZ", [8, 128, 1040], F32).ap()
    UZ_d = nc.dram_tensor("UZ", [8, 128, 1040], BF16).ap()

    w_in_v = w_in.rearrange("(kc p) c -> p kc c", p=128)
    w_fu_v = w_fu.rearrange("(kc p) c -> p kc c", p=128)
    w_out_v = w_out.rearrange("(kb p) d -> p kb d", p=128)

    with contextlib.ExitStack() as st0:
        S = Sched(nc, st0)
        SREF[0] = S

        uid = [0]

        def sb(st, name, shape, dt=F32):
            uid[0] += 1
            nm = "s%d_%s" % (uid[0], name)
            return Tk(st.enter_context(nc.sbuf_tensor(nm, shape, dt)), nm)

        PS = [Tk(st0.enter_context(nc.psum_tensor("ps%d" % i, [128, 512], F32)), "ps%d" % i) for i in range(8)]
        psi = [0]

        pool_sel = [None]
        pool_cnt = {}

        def ps():
            if pool_sel[0] is not None:
                base, size = pool_sel[0]
                k = pool_cnt.get(base, 0)
                pool_cnt[base] = k + 1
                return PS[base + k % size]
            t = PS[psi[0] % 8]
            psi[0] += 1
            return t

        ACT = lambda fn, r, w: S.op("act", fn, r, w)
        DVE = lambda fn, r, w: S.op("dve", fn, r, w)
        PE = lambda fn, r, w, inc=True: S.op("pe", fn, r, w, inc=inc)
        YTk = Tk(YT_d, "YT")
        PRtk = [Tk(PR_d[b], "PR%d" % b) for b in range(24)]
        VZtk = Tk(VZ_d, "VZd")
        UZtk = Tk(UZ_d, "UZd")

        CB = sb(st0, "CB", [128, NCB])
        VB = sb(st0, "VB", [128, NVB])
        S.dma("sp", CB.ap[:], cb_d, writes=[CB])
        S.dma("sp", VB.ap[:], vb_d, writes=[VB])
        IDB = sb(st0, "IDB", [128, 128], BF16)
        BOB = sb(st0, "BOB", [128, 128], BF16)
        BSB = sb(st0, "BSB", [128, 2], BF16)
        DVE(lambda e: e.tensor_copy(out=IDB.ap[:], in_=CB.ap[:, O_ID:O_ID + 128]), [CB], [IDB])
        DVE(lambda e: e.tensor_copy(out=BOB.ap[:], in_=CB.ap[:, O_BO:O_BO + 128]), [CB], [BOB])
        DVE(lambda e: e.tensor_copy(out=BSB.ap[:], in_=CB.ap[:, O_BS:O_BS + 2]), [CB], [BSB])
        ident = CB.ap[:, O_ID:O_ID + 128]
        SST = [sb(st0, "sst%d" % j, [128, 128]) for j in range(8)]
        SBF = [sb(st0, "sbf%d" % j, [128, 128], BF16) for j in range(8)]
        for j in range(8):
            DVE(lambda e: e.memset(SST[j].ap[:], 0.0), [], [SST[j]])
            DVE(lambda e: e.memset(SBF[j].ap[:], 0.0), [], [SBF[j]])
        SH = sb(st0, "SH", [128, 27, 17])
        DVE(lambda e: e.memset(SH.ap[:], 0.0), [], [SH])
        SMZ = sb(st0, "SMZ", [128, 8, 5, 32])
        SMv = sb(st0, "SMv", [128, 8, 16])
        DVE(lambda e: e.memset(SMZ.ap[:], 0.0), [], [SMZ])
        RKS = sb(st0, "RKS", [128, 16])
        ss_all = sb(st0, "ss", [128, 32])
        ss_list = [Tk(ss_all.ap[:, 4 * i:4 * i + 4], "ss%d" % i) for i in range(8)]
        ss_rr = [0]

        def next_ss():
            t = ss_list[ss_rr[0] % 8]
            ss_rr[0] += 1
            return t
        wupa = sb(st0, "wupa", [128, 1024], BF16)
        aup = sb(st0, "aup", [128, 1024], BF16)
        gup1 = sb(st0, "gup1", [128, 1024], BF16)
        gup2 = sb(st0, "gup2", [128, 1024], BF16)
        S.dma("pool", wupa.ap[0:65, :], wupa_d, writes=[wupa])
        S.dma("pool", aup.ap[64:128, :], aup_d, writes=[aup])
        S.dma("pool", gup1.ap[:, :], gup_d[0:128, :], writes=[gup1])
        S.dma("pool", gup2.ap[0:32, :], gup_d[128:160, :], writes=[gup2])

        def norm_S1(xt, xap, n, hb, ss_t):
            DVE(lambda e: e.memset(ss_t.ap[0:n, 0:1], 0.0), [], [ss_t])
            ACT(lambda e: e.activation(out=hb.ap[0:n, :], in_=xap[0:n, :], func=AF.Square, accum_out=ss_t.ap[0:n, 0:1]), [xt], [hb, ss_t])

        def norm_S2(xt, xap, n, G, hb, ss_t):
            DVE(lambda e: e.tensor_scalar(out=ss_t.ap[0:n, 1:2], in0=ss_t.ap[0:n, 0:1], scalar1=1.0 / D, scalar2=RMS_EPS, op0=ALU.mult, op1=ALU.add), [ss_t], [ss_t])
            ACT(lambda e: e.activation(out=ss_t.ap[0:n, 2:3], in_=ss_t.ap[0:n, 1:2], func=AF.Sqrt), [ss_t], [ss_t])
            DVE(lambda e: e.reciprocal(out=ss_t.ap[0:n, 3:4], in_=ss_t.ap[0:n, 2:3]), [ss_t], [ss_t])
            DVE(lambda e: e.scalar_tensor_tensor(out=hb.ap[0:n, :], in0=xap[0:n, :], scalar=ss_t.ap[0:n, 3:4], in1=G.ap[0:n, :], op0=ALU.mult, op1=ALU.mult), [xt, ss_t, G], [hb])

        def norm_S3(n, hb, hT, col0):
            for half in range(2):
                P = ps()
                pb = P.ap[:].bitcast(BF16)
                for k8 in range(8):
                    kc = half * 8 + k8
                    PE(lambda e: e.transpose(out=pb[:, k8 * 128:k8 * 128 + n], in_=hb.ap[0:n, kc * 128:(kc + 1) * 128], identity=IDB.ap[0:n, 0:n]), [hb, IDB], [P], inc=(k8 == 7))
                ACT(lambda e: e.activation(out=hT.ap[:, half * 8:(half + 1) * 8, col0:col0 + n], in_=pb.rearrange("p (k t) -> p k t", k=8)[:, :, 0:n], func=AF.Copy), [P], [hT])

        def norm_pipe(items, G, hbs, hT):
            nI = len(items)
            st = {}
            for step in range(nI + 2):
                if step < nI:
                    ld, n, col0 = items[step]
                    xt, xap = ld()
                    st[step] = (xt, xap, next_ss())
                    norm_S1(xt, xap, n, hbs[step % len(hbs)], st[step][2])
                q = step - 1
                if 0 <= q < nI:
                    xt, xap, sst = st[q]
                    norm_S2(xt, xap, items[q][1], G, hbs[q % len(hbs)], sst)
                q = step - 2
                if 0 <= q < nI:
                    norm_S3(items[q][1], hbs[q % len(hbs)], hT, items[q][2])

        def proj(wt, c0, npart, hT, groups, evac):
            for (t0, t1) in groups:
                P = ps()
                for kc in range(16):
                    PE(lambda e: e.matmul(P.ap[0:npart, 0:t1 - t0], lhsT=wt.ap[:, kc, c0:c0 + npart], rhs=hT.ap[:, kc, t0:t1], start=(kc == 0), stop=(kc == 15)), [wt, hT], [P], inc=(kc == 15))
                evac(P, t0, t1)

        def post(n, C, H, O_tk, O3, O4unused, Q1, Q1_3, Q1_4, Q2, Q2_3, st16, LG4, LB4, LGtk, LBtk):
            DVE(lambda e: e.tensor_reduce(out=st16.ap[0:n, 0, :], in_=O3, axis=AX.X, op=ALU.add), [O_tk], [st16])
            ACT(lambda e: e.activation(out=Q2_3, in_=O3, func=AF.Square), [O_tk], [Q2])
            DVE(lambda e: e.tensor_reduce(out=st16.ap[0:n, 1, :], in_=Q2_3, axis=AX.X, op=ALU.add), [Q2, st16], [st16])
            DVE(lambda e: e.tensor_scalar(out=st16.ap[0:n, 0, :], in0=st16.ap[0:n, 0, :], scalar1=1.0 / 64, scalar2=None, op0=ALU.mult), [st16], [st16])
            DVE(lambda e: e.tensor_tensor(out=st16.ap[0:n, 2, :], in0=st16.ap[0:n, 0, :], in1=st16.ap[0:n, 0, :], op=ALU.mult), [st16], [st16])
            DVE(lambda e: e.scalar_tensor_tensor(out=st16.ap[0:n, 1, :], in0=st16.ap[0:n, 1, :], scalar=1.0 / 64, in1=st16.ap[0:n, 2, :], op0=ALU.mult, op1=ALU.subtract), [st16], [st16])
            DVE(lambda e: e.tensor_scalar(out=st16.ap[0:n, 1, :], in0=st16.ap[0:n, 1, :], scalar1=GN_EPS, scalar2=None, op0=ALU.add), [st16], [st16])
            ACT(lambda e: e.activation(out=st16.ap[0:n, 2, :], in_=st16.ap[0:n, 1, :], func=AF.Sqrt), [st16], [st16])
            DVE(lambda e: e.reciprocal(out=st16.ap[0:n, 3, :], in_=st16.ap[0:n, 2, :]), [st16], [st16])
            DVE(lambda e: e.tensor_tensor(out=Q1_3, in0=O3, in1=st16.ap[0:n, 0, :].unsqueeze(2).to_broadcast([n, 16, 64]), op=ALU.subtract), [O_tk, st16], [Q1])
            DVE(lambda e: e.tensor_tensor(out=Q1_3, in0=Q1_3, in1=st16.ap[0:n, 3, :].unsqueeze(2).to_broadcast([n, 16, 64]), op=ALU.mult), [Q1, st16], [Q1])
            DVE(lambda e: e.tensor_tensor(out=Q1_4, in0=Q1_4, in1=LG4, op=ALU.mult), [Q1, LGtk], [Q1])
            DVE(lambda e: e.tensor_tensor(out=Q1_4, in0=Q1_4, in1=LB4, op=ALU.add), [Q1, LBtk], [Q1])

        groups3 = [(0, 512), (512, 1024), (1024, 1040)]

        def sample_gen(sts, stp_ctx):
            sgT1, sgT2, gup1, gup2 = stp_ctx
            if True:
                ROWS2 = sb(sts, "ROWS2", [32, 5, 512])
                ROWSV = sb(sts, "ROWSV", [16, 1024])
                SEL2 = sb(sts, "SEL2", [32, 16 * 128])
                I2 = sb(sts, "I2", [128, 64])
                OS = sb(sts, "OS", [128, 8, 16])
                Sin = [sb(sts, "Sin%d" % i, [128, 8, 64]) for i in range(2)]
                X1 = sb(sts, "X1", [128, 8, 64])
                X2 = [sb(sts, "X2%d" % i, [128, 8, 64]) for i in range(2)]
                sa = sb(sts, "sa", [128, 8])
                OSM = sb(sts, "OSM", [16, 1024])
                Q1s = sb(sts, "Q1s", [16, 1024])
                Q2s = sb(sts, "Q2s", [16, 1024])
                st16s = sb(sts, "st16s", [16, 6, 16])
                LGf = sb(sts, "LGf", [16, 1024])
                LBf = sb(sts, "LBf", [16, 1024])
                yas = sb(sts, "yas", [16, 1024], BF16)
                yaTs = sb(sts, "yaTs", [128, 8, 16], BF16)
                S.dma("sp", SEL2.ap[:], sel2_d, writes=[SEL2])
                S.dma("sp", LGf.ap[:], lg_d.partition_broadcast(16), writes=[LGf])
                S.dma("sp", LBf.ap[:], lb_d.partition_broadcast(16), writes=[LBf])
                DVE(lambda e: e.tensor_tensor(out=I2.ap[:], in0=CB.ap[:, O_ID:O_ID + 64], in1=CB.ap[:, O_ID + 64:O_ID + 128], op=ALU.add), [CB], [I2])
                for vec in range(5):
                    for half in range(2):
                        P = ps()
                        for jj in range(4):
                            j = half * 4 + jj
                            PE(lambda e: e.matmul(P.ap[0:32, jj * 64:(jj + 1) * 64], lhsT=SMZ.ap[:, j, vec, :], rhs=I2.ap[:], start=True, stop=True), [SMZ, I2], [P], inc=(jj == 3))
                        ACT(lambda e: e.activation(out=ROWS2.ap[:, vec, half * 256:(half + 1) * 256], in_=P.ap[0:32, 0:256], func=AF.Copy), [P], [ROWS2])
                for half in range(2):
                    P = ps()
                    for jj in range(4):
                        j = half * 4 + jj
                        PE(lambda e: e.transpose(out=P.ap[0:16, jj * 128:(jj + 1) * 128], in_=SMv.ap[:, j, :], identity=ident), [SMv, CB], [P], inc=(jj == 3))
                    ACT(lambda e: e.activation(out=ROWSV.ap[:, half * 512:(half + 1) * 512], in_=P.ap[0:16, :], func=AF.Copy), [P], [ROWSV])
                yield

                def bc(vec, b):
                    Pq = ps()
                    PE(lambda e: e.matmul(Pq.ap[:, :], lhsT=SEL2.ap[0:32, b * 128:(b + 1) * 128], rhs=ROWS2.ap[0:32, vec, :], start=True, stop=True), [ROWS2, SEL2], [Pq])
                    return Pq

                pv = lambda P_: P_.ap[:, :].rearrange("p (h k) -> p h k", h=8)
                for b in range(16):
                    Si = Sin[b % 2]
                    Xo = X2[b % 2]
                    S.dma("sp", Si.ap[:], swkv[b].rearrange("(hp h2) v k -> (h2 v) hp k", h2=2), writes=[Si])
                    Pq = bc(0, b)
                    DVE(lambda e: e.tensor_tensor(out=X1.ap[:], in0=Si.ap[:], in1=pv(Pq), op=ALU.mult), [Si, Pq], [X1])
                    DVE(lambda e: e.tensor_reduce(out=sa.ap[:], in_=X1.ap[:], axis=AX.X, op=ALU.add), [X1], [sa])
                    Pq = bc(4, b)
                    DVE(lambda e: e.tensor_tensor(out=Xo.ap[:], in0=Si.ap[:], in1=pv(Pq), op=ALU.mult), [Si, Pq], [Xo])
                    Pq = bc(1, b)
                    DVE(lambda e: e.tensor_tensor(out=X1.ap[:], in0=pv(Pq), in1=sa.ap[:].unsqueeze(2).to_broadcast([128, 8, 64]), op=ALU.mult), [sa, Pq], [X1])
                    DVE(lambda e: e.tensor_tensor(out=Xo.ap[:], in0=Xo.ap[:], in1=X1.ap[:], op=ALU.add), [Xo, X1], [Xo])
                    Pq = bc(2, b)
                    DVE(lambda e: e.tensor_tensor(out=X1.ap[:], in0=pv(Pq), in1=SMv.ap[:, :, b:b + 1].to_broadcast([128, 8, 64]), op=ALU.mult), [SMv, Pq], [X1])
                    DVE(lambda e: e.tensor_tensor(out=Xo.ap[:], in0=Xo.ap[:], in1=X1.ap[:], op=ALU.add), [Xo, X1], [Xo])
                    S.dma("sp", wkvs_o[b].rearrange("(hp h2) v k -> (h2 v) hp k", h2=2), Xo.ap[:], reads=[Xo], is_output=True)
                    Pq = bc(3, b)
                    DVE(lambda e: e.tensor_tensor(out=X1.ap[:], in0=Xo.ap[:], in1=pv(Pq), op=ALU.mult), [Xo, Pq], [X1])
                    DVE(lambda e: e.tensor_reduce(out=OS.ap[:, :, b], in_=X1.ap[:], axis=AX.X, op=ALU.add), [X1], [OS])
                    yield
                for half in range(2):
                    P = ps()
                    for jj in range(4):
                        hp_ = half * 4 + jj
                        PE(lambda e: e.transpose(out=P.ap[0:16, jj * 128:(jj + 1) * 128], in_=OS.ap[:, hp_, :], identity=ident), [OS, CB], [P], inc=(jj == 3))
                    ACT(lambda e: e.activation(out=OSM.ap[:, half * 512:(half + 1) * 512], in_=P.ap[0:16, :], func=AF.Copy), [P], [OSM])
                v3 = lambda t: t.ap[:].rearrange("p (h v) -> p h v", h=16)
                v4 = lambda t: t.ap[:].rearrange("p (c h v) -> p c h v", c=1, h=16)
                post(16, 1, 16, OSM, v3(OSM), None, Q1s, v3(Q1s), v4(Q1s), Q2s, v3(Q2s), st16s, v4(LGf), v4(LBf), LGf, LBf)
                DVE(lambda e: e.tensor_tensor(out=v3(Q2s), in0=ROWSV.ap[:, :].rearrange("p (h v) -> p h v", h=16), in1=RKS.ap[0:16, :].unsqueeze(2).to_broadcast([16, 16, 64]), op=ALU.mult), [ROWSV, RKS], [Q2s])
                DVE(lambda e: e.tensor_tensor(out=Q1s.ap[:], in0=Q1s.ap[:], in1=Q2s.ap[:], op=ALU.add), [Q1s, Q2s], [Q1s])
                for half in range(2):
                    P = ps()
                    cs = slice(half * 512, (half + 1) * 512)
                    PE(lambda e: e.matmul(P.ap[0:16, :], lhsT=sgT1.ap[:, 1024:1040], rhs=gup1.ap[:, cs], start=True, stop=False), [sgT1, gup1], [P], inc=False)
                    PE(lambda e: e.matmul(P.ap[0:16, :], lhsT=sgT2.ap[0:32, 1024:1040], rhs=gup2.ap[0:32, cs], start=False, stop=True), [sgT2, gup2], [P])
                    DVE(lambda e: e.tensor_tensor(out=yas.ap[:, cs], in0=P.ap[0:16, :], in1=Q1s.ap[:, cs], op=ALU.mult), [P, Q1s], [yas])
                P = ps()
                pb = P.ap[:].bitcast(BF16)
                for j in range(8):
                    PE(lambda e: e.transpose(out=pb[:, j * 16:(j + 1) * 16], in_=yas.ap[0:16, j * 128:(j + 1) * 128], identity=IDB.ap[0:16, 0:16]), [yas, IDB], [P], inc=(j == 7))
                ACT(lambda e: e.activation(out=yaTs.ap[:].rearrange("p j t -> p (j t)"), in_=pb[:, 0:128], func=AF.Copy), [P], [yaTs])
                S.dma("sp", YT_d[0:8, :, 1024:1040].rearrange("j p t -> p j t"), yaTs.ap[:], reads=[yaTs], writes=[YTk])

        def sgu_gen(stg):
            if True:
                VZ = [sb(stg, "VZ%d" % g, [128, 1040]) for g in range(8)]
                UZ = sb(stg, "UZ", [128, 8, 1040], BF16)
                MEAN = sb(stg, "MEAN", [128, 1040])
                RSTD = sb(stg, "RSTD", [128, 1040])
                SQ = sb(stg, "SQ", [128, 1040])
                WsT = sb(stg, "WsT", [128, 8, 128], BF16)
                Wsf = sb(stg, "Wsf", [128, 8, 128])
                SGB = sb(stg, "SGB", [128, 1024])
                VTb = sb(stg, "VTb", [128, 4, 128], BF16)
                tmpm = sb(stg, "tmpm", [128, 4, 128])
                ybT = [sb(stg, "ybT%d" % i, [128, 1040], BF16) for i in range(2)]
                VNs = sb(stg, "VNs", [128, 8, 16])
                sguvT = sb(stg, "sguvT", [16, 1024])
                S.dma("sp", Wsf.ap[:], sgw_d.rearrange("g i j -> i g j"), writes=[Wsf])
                S.dma("sp", SGB.ap[:], sgb_d.partition_broadcast(128), writes=[SGB])
                yield
                for g4 in range(2):
                    P = ps()
                    for gg in range(4):
                        g = g4 * 4 + gg
                        PE(lambda e: e.transpose(out=P.ap[:, gg * 128:(gg + 1) * 128], in_=Wsf.ap[:, g, :], identity=ident), [Wsf, CB], [P], inc=(gg == 3))
                    DVE(lambda e: e.tensor_tensor(out=WsT.ap[:, g4 * 4:(g4 + 1) * 4, :], in0=P.ap[:].rearrange("p (g t) -> p g t", g=4),
                                                  in1=CB.ap[:, O_M4 + 128:O_M4 + 256].unsqueeze(1).to_broadcast([128, 4, 128]), op=ALU.mult), [P, CB], [WsT])
                yield
                for g in range(8):
                    S.dma("sp", VZ[g].ap[:], VZ_d[g], reads=[VZtk], writes=[VZ[g]])
                S.dma("sp", UZ.ap[:], UZ_d.rearrange("g p t -> p g t"), reads=[UZtk], writes=[UZ])
                yield
                for (t0, t1) in groups3:
                    n = t1 - t0
                    P1 = ps()
                    for g in range(8):
                        PE(lambda e: e.matmul(P1.ap[:, 0:n], lhsT=CB.ap[:, O_OM:O_OM + 128], rhs=VZ[g].ap[:, t0:t1], start=(g == 0), stop=(g == 7)), [VZ[g], CB], [P1], inc=(g == 7))
                    ACT(lambda e: e.activation(out=MEAN.ap[:, t0:t1], in_=P1.ap[:, 0:n], func=AF.Copy), [P1], [MEAN])
                    P2 = ps()
                    for g in range(8):
                        ACT(lambda e: e.activation(out=SQ.ap[:, 0:n], in_=VZ[g].ap[:, t0:t1], func=AF.Square), [VZ[g]], [SQ])
                        PE(lambda e: e.matmul(P2.ap[:, 0:n], lhsT=CB.ap[:, O_OM:O_OM + 128], rhs=SQ.ap[:, 0:n], start=(g == 0), stop=(g == 7)), [SQ, CB], [P2], inc=True)
                    ACT(lambda e: e.activation(out=RSTD.ap[:, t0:t1], in_=P2.ap[:, 0:n], func=AF.Copy), [P2], [RSTD])
                yield
                DVE(lambda e: e.tensor_tensor(out=SQ.ap[:], in0=MEAN.ap[:], in1=MEAN.ap[:], op=ALU.mult), [MEAN], [SQ])
                DVE(lambda e: e.tensor_tensor(out=RSTD.ap[:], in0=RSTD.ap[:], in1=SQ.ap[:], op=ALU.subtract), [RSTD, SQ], [RSTD])
                DVE(lambda e: e.tensor_scalar(out=RSTD.ap[:], in0=RSTD.ap[:], scalar1=LN_EPS, scalar2=None, op0=ALU.add), [RSTD], [RSTD])
                ACT(lambda e: e.activation(out=RSTD.ap[:], in_=RSTD.ap[:], func=AF.Sqrt), [RSTD], [RSTD])
                DVE(lambda e: e.reciprocal(out=RSTD.ap[:], in_=RSTD.ap[:]), [RSTD], [RSTD])
                yield
                for g in range(8):
                    DVE(lambda e: e.tensor_tensor(out=VZ[g].ap[:], in0=VZ[g].ap[:], in1=MEAN.ap[:], op=ALU.subtract), [VZ[g], MEAN], [VZ[g]])
                    DVE(lambda e: e.tensor_tensor(out=VZ[g].ap[:], in0=VZ[g].ap[:], in1=RSTD.ap[:], op=ALU.mult), [VZ[g], RSTD], [VZ[g]])
                    DVE(lambda e: e.tensor_scalar(out=VZ[g].ap[:], in0=VZ[g].ap[:], scalar1=VB.ap[:, V_SG + g:V_SG + g + 1], scalar2=VB.ap[:, V_SB + g:V_SB + g + 1], op0=ALU.mult, op1=ALU.add), [VZ[g], VB], [VZ[g]])
                    DVE(lambda e: e.tensor_copy(out=VNs.ap[:, g, :], in_=VZ[g].ap[:, 1024:1040]), [VZ[g]], [VNs])
                    yield
                yield
                for half in range(2):
                    P = ps()
                    for gg in range(4):
                        g = half * 4 + gg
                        PE(lambda e: e.transpose(out=P.ap[0:16, gg * 128:(gg + 1) * 128], in_=VNs.ap[:, g, :], identity=ident), [VNs, CB], [P], inc=(gg == 3))
                    ACT(lambda e: e.activation(out=sguvT.ap[:, half * 512:(half + 1) * 512], in_=P.ap[0:16, :], func=AF.Copy), [P], [sguvT])
                S.dma("sp", sguv_o, sguvT.ap[:], reads=[sguvT], is_output=True)
                yield
                for g in range(8):
                    yb = ybT[g % 2]
                    for c4 in range(2):
                        P = ps()
                        for cc in range(4):
                            c = c4 * 4 + cc
                            PE(lambda e: e.transpose(out=P.ap[:, cc * 128:(cc + 1) * 128], in_=VZ[g].ap[:, c * 128:(c + 1) * 128], identity=ident), [VZ[g], CB], [P], inc=(cc == 3))
                        ACT(lambda e: e.activation(out=VTb.ap[:], in_=P.ap[:].rearrange("p (c t) -> p c t", c=4), func=AF.Copy), [P], [VTb])
                        PM = ps()
                        for cc in range(4):
                            PE(lambda e: e.matmul(PM.ap[:, cc * 128:(cc + 1) * 128], lhsT=VTb.ap[:, cc, :], rhs=WsT.ap[:, g, :], start=True, stop=True), [VTb, WsT], [PM], inc=(cc == 3))
                        DVE(lambda e: e.tensor_tensor(out=tmpm.ap[:], in0=PM.ap[:].rearrange("p (c t) -> p c t", c=4), in1=SGB.ap[:, g * 128:(g + 1) * 128].unsqueeze(1).to_broadcast([128, 4, 128]), op=ALU.add), [PM, SGB], [tmpm])
                        DVE(lambda e: e.tensor_tensor(out=yb.ap[:, c4 * 512:(c4 + 1) * 512], in0=tmpm.ap[:].rearrange("p c t -> p (c t)"), in1=UZ.ap[:, g, c4 * 512:(c4 + 1) * 512], op=ALU.mult), [tmpm, UZ], [yb])
                    DVE(lambda e: e.tensor_scalar(out=tmpm.ap[:, 0, 0:16], in0=VZ[g].ap[:, 1024:1040], scalar1=VB.ap[:, V_W00 + g:V_W00 + g + 1], scalar2=VB.ap[:, V_B0 + g:V_B0 + g + 1], op0=ALU.mult, op1=ALU.add), [VZ[g], VB], [tmpm])
                    DVE(lambda e: e.tensor_tensor(out=yb.ap[:, 1024:1040], in0=tmpm.ap[:, 0, 0:16], in1=UZ.ap[:, g, 1024:1040], op=ALU.mult), [tmpm, UZ], [yb])
                    S.dma("sp", YT_d[8 + g], yb.ap[:], reads=[yb], writes=[YTk])
                    yield

        def rest_phase():
            with contextlib.ExitStack() as str_:
                x1 = sb(str_, "x1", [128, 9, D])
                h2T = sb(str_, "h2T", [128, 16, 1040], BF16)
                tiles = [(ti, 128 if ti < 8 else 16) for ti in range(9)]
                x1t = [Tk(x1.ap[:, ti, :], "x1_%d" % ti) for ti in range(9)]
                with contextlib.ExitStack() as s1:
                    yT = sb(s1, "yT", [128, 16, 1040], BF16)
                    wos = [sb(s1, "wo%d" % i, [128, 16, 512], BF16) for i in range(2)]
                    S.dma("sp", yT.ap[:, 0:8, :], YT_d[0:8].rearrange("k p t -> p k t"), reads=[YTk], writes=[yT])
                    S.dma("sp", yT.ap[:, 8:16, :], YT_d[8:16].rearrange("k p t -> p k t"), reads=[YTk], writes=[yT])
                    for ti in range(8):
                        S.dma("sp", x1.ap[:, ti, :], xin[1024 + ti * 128:1024 + (ti + 1) * 128, :], writes=[x1t[ti]])
                    S.dma("sp", x1.ap[0:16, 8, :], xsm[0:16, :], writes=[x1t[8]])
                    for dg in range(4):
                        ds_ = slice(dg * 512, (dg + 1) * 512)
                        wo = wos[dg % 2]
                        S.dma("pool", wo.ap[:], w_out_v[:, :, ds_], writes=[wo])
                        for ti, n in tiles:
                            cols = slice(ti * 128, ti * 128 + n)
                            P = ps()
                            for kb in range(16):
                                PE(lambda e: e.matmul(P.ap[0:n, :], lhsT=yT.ap[:, kb, cols], rhs=wo.ap[:, kb, :], start=(kb == 0), stop=(kb == 15)), [yT, wo], [P], inc=(kb == 15))
                            DVE(lambda e: e.tensor_tensor(out=x1.ap[0:n, ti, ds_], in0=x1.ap[0:n, ti, ds_], in1=P.ap[0:n, :], op=ALU.add), [x1t[ti], P], [x1t[ti]])
                    S.barrier()
                sff = contextlib.ExitStack()
                wfu0 = sb(sff, "wfu0", [128, 16, 512], BF16)
                wfd = sb(sff, "wfd", [128, 4, D], BF16)
                S.dma("pool", wfu0.ap[:], w_fu_v[:, :, 0:512], writes=[wfu0])
                S.dma("pool", wfd.ap[:], w_fd[0:512, :].rearrange("(b p) d -> p b d", p=128), writes=[wfd])
                with contextlib.ExitStack() as s1b:
                    G2 = sb(s1b, "G2", [128, D])
                    hbs2 = [sb(s1b, "hb2_%d" % i, [128, D], BF16) for i in range(3)]
                    S.dma("sp", G2.ap[:], g2_d.partition_broadcast(128), writes=[G2])
                    norm_pipe([((lambda ti=ti: (x1t[ti], x1.ap[:, ti, :])), n, ti * 128) for ti, n in tiles], G2, hbs2, h2T)
                    S.barrier()
                ck(12)
                with contextlib.ExitStack() as s2:
                    wfu = [wfu0, sb(s2, "wfu1", [128, 16, 512], BF16)]
                    actT = sb(s2, "actT", [128, 4, 1040], BF16)
                    RL = sb(s2, "RL", [128, 512])
                    for f in range(16):
                        wu = wfu[f % 2]
                        if f > 0:
                            S.dma("pool", wu.ap[:], w_fu_v[:, :, f * 512:(f + 1) * 512], writes=[wu])
                            S.dma("pool", wfd.ap[:], w_fd[f * 512:(f + 1) * 512, :].rearrange("(b p) d -> p b d", p=128), writes=[wfd])
                        for fb in range(4):
                            for (t0, t1) in groups3:
                                n = t1 - t0
                                P = ps()
                                for kc in range(16):
                                    PE(lambda e: e.matmul(P.ap[:, 0:n], lhsT=wu.ap[:, kc, fb * 128:(fb + 1) * 128], rhs=h2T.ap[:, kc, t0:t1], start=(kc == 0), stop=(kc == 15)), [wu, h2T], [P], inc=(kc == 15))
                                ACT(lambda e: e.activation(out=RL.ap[:, 0:n], in_=P.ap[:, 0:n], func=AF.Relu), [P], [RL])
                                DVE(lambda e: e.tensor_tensor(out=actT.ap[:, fb, t0:t1], in0=RL.ap[:, 0:n], in1=RL.ap[:, 0:n], op=ALU.mult), [RL], [actT])
                        for ti, n in tiles:
                            cols = slice(ti * 128, ti * 128 + n)
                            for dg in range(4):
                                ds_ = slice(dg * 512, (dg + 1) * 512)
                                P = ps()
                                for fb in range(4):
                                    PE(lambda e: e.matmul(P.ap[0:n, :], lhsT=actT.ap[:, fb, cols], rhs=wfd.ap[:, fb, ds_], start=(fb == 0), stop=(fb == 3)), [actT, wfd], [P], inc=(fb == 3))
                                DVE(lambda e: e.tensor_tensor(out=x1.ap[0:n, ti, ds_], in0=x1.ap[0:n, ti, ds_], in1=P.ap[0:n, :], op=ALU.add), [x1t[ti], P], [x1t[ti]])
                    S.barrier()
                sff.close()
                with contextlib.ExitStack() as s3:
                    GF = sb(s3, "GF", [128, D])
                    yt = [sb(s3, "yt%d" % i, [128, D]) for i in range(3)]
                    SHT = sb(s3, "SHT", [17, 27 * 128])
                    S.dma("sp", GF.ap[:], gf_d.partition_broadcast(128), writes=[GF])
                    fst = {}
                    for step in range(len(tiles) + 1):
                        if step < len(tiles):
                            ti, n = tiles[step]
                            xa = x1.ap[:, ti, :]
                            y_ = yt[ti % 3]
                            ss_t = next_ss()
                            fst[step] = ss_t
                            DVE(lambda e: e.memset(ss_t.ap[0:n, 0:1], 0.0), [], [ss_t])
                            ACT(lambda e: e.activation(out=y_.ap[0:n, :], in_=xa[0:n, :], func=AF.Square, accum_out=ss_t.ap[0:n, 0:1]), [x1t[ti]], [y_, ss_t])
                        q = step - 1
                        if 0 <= q < len(tiles):
                            ti, n = tiles[q]
                            xa = x1.ap[:, ti, :]
                            y_ = yt[ti % 3]
                            ss_t = fst[q]
                            DVE(lambda e: e.tensor_scalar(out=ss_t.ap[0:n, 1:2], in0=ss_t.ap[0:n, 0:1], scalar1=1.0 / D, scalar2=RMS_EPS, op0=ALU.mult, op1=ALU.add), [ss_t], [ss_t])
                            ACT(lambda e: e.activation(out=ss_t.ap[0:n, 2:3], in_=ss_t.ap[0:n, 1:2], func=AF.Sqrt), [ss_t], [ss_t])
                            DVE(lambda e: e.reciprocal(out=ss_t.ap[0:n, 3:4], in_=ss_t.ap[0:n, 2:3]), [ss_t], [ss_t])
                            DVE(lambda e: e.scalar_tensor_tensor(out=y_.ap[0:n, :], in0=xa[0:n, :], scalar=ss_t.ap[0:n, 3:4], in1=GF.ap[0:n, :], op0=ALU.mult, op1=ALU.mult), [x1t[ti], ss_t, GF], [y_])
                            if ti < 8:
                                S.dma("sp", y_o[ti * 128:(ti + 1) * 128, :], y_.ap[:], reads=[y_], is_output=True)
                            else:
                                S.dma("sp", ysm_o, y_.ap[0:16, :], reads=[y_], is_output=True)
                    for b4 in range(7):
                        P = ps()
                        nb = min(4, 27 - b4 * 4)
                        for bb in range(nb):
                            blk = b4 * 4 + bb
                            PE(lambda e: e.transpose(out=P.ap[0:17, bb * 128:(bb + 1) * 128], in_=SH.ap[:, blk, :], identity=ident), [SH, CB], [P], inc=(bb == nb - 1))
                        ACT(lambda e: e.activation(out=SHT.ap[:, b4 * 512:b4 * 512 + nb * 128], in_=P.ap[0:17, 0:nb * 128], func=AF.Copy), [P], [SHT])
                    S.dma("sp", shp_o, SHT.ap[:], reads=[SHT], is_output=True)

        def run_tasks(tasks):
            done = set()
            running = []
            pending = list(tasks)
            while pending or running:
                for t in list(pending):
                    if all(d in done for d in t[2]):
                        pending.remove(t)
                        running.append((t[0], t[1](), t[3]))
                assert running, "task graph stuck"
                for r in list(running):
                    pool_sel[0] = r[2]
                    try:
                        next(r[1])
                    except StopIteration:
                        running.remove(r)
                        done.add(r[0])
                pool_sel[0] = None


        ck(1)
        for own in (False, True):
            NT = 1041 if own else 1024
            NM = 1040 if own else 1024
            NCH = 8
            groups = [(0, 512), (512, 1024)] + ([(1024, 1041)] if own else [])
            with contextlib.ExitStack() as stp:
                TW = sb(stp, "TW", [128, NT], BF16)
                aLo = sb(stp, "aLo", [128, NT], BF16)
                sgT1 = sb(stp, "sgT1", [128, NT], BF16)
                sgT2 = sb(stp, "sgT2", [128, NT], BF16)
                DVE(lambda e: e.memset(TW.ap[64:65, :], 1.0), [], [TW])
                shT = sb(stp, "shT", [128, 27 * 16])
                sth = contextlib.ExitStack()
                hT = sb(sth, "hT", [128, 16, NT], BF16)
                wL = sb(sth, "wL", [128, 16, 288], BF16)
                wPs = [sb(sth, "wP%d" % i, [128, 16, 384], BF16) for i in range(2)]
                S.dma("pool", wL.ap[:, :, 0:(288 if own else 128)], w_in_v[:, :, 3072:3072 + (288 if own else 128)], writes=[wL])
                S.dma("pool", wPs[0].ap[:, :, 0:(384 if own else 256)], w_in_v[:, :, 0:(384 if own else 256)], writes=[wPs[0]])
                with contextlib.ExitStack() as sta:
                    G1 = sb(sta, "G1", [128, D])
                    S.dma("sp", G1.ap[:], g1_d.partition_broadcast(128), writes=[G1])
                    xts = [sb(sta, "xt%d" % i, [128, D]) for i in range(3)]
                    hbs = [sb(sta, "hb%d" % i, [128, D], BF16) for i in range(3)]
                    base = 1024 if own else 0
                    items = []
                    for i in range(8):
                        def ld(i=i):
                            xt = xts[i % 3]
                            S.dma("sp", xt.ap[:], xin[base + i * 128:base + (i + 1) * 128, :], writes=[xt])
                            return xt, xt.ap
                        items.append((ld, 128, i * 128))
                    if own:
                        def ld17():
                            xt = xts[8 % 3]
                            S.dma("sp", xt.ap[0:17, :], xsm, writes=[xt])
                            return xt, xt.ap
                        items.append((ld17, 17, 1024))
                    norm_pipe(items, G1, hbs, hT)
                    S.barrier()
                ck(7 if own else 2)
                if own:
                    S.dma("sp", shT.ap[:], shT_d, writes=[shT])

                def evac_copy(dst):
                    def f(P, t0, t1, dst=dst):
                        n = dst_np[0]
                        ACT(lambda e: e.activation(out=dst.ap[0:n, t0:t1], in_=P.ap[0:n, 0:t1 - t0], func=AF.Copy), [P], [dst])
                    return f
                dst_np = [128]

                def mix(p, d, blk, npart):
                    mu = VB.ap[0:npart, V_MU + blk:V_MU + blk + 1]
                    if own:
                        DVE(lambda e: e.tensor_copy(out=SH.ap[0:npart, blk, :], in_=p.ap[0:npart, 1023:1040]), [p], [SH])
                    DVE(lambda e: e.tensor_tensor(out=d.ap[0:npart, 1:1024], in0=p.ap[0:npart, 0:1023], in1=p.ap[0:npart, 1:1024], op=ALU.subtract), [p], [d])
                    if own:
                        DVE(lambda e: e.tensor_tensor(out=d.ap[0:npart, 0:1], in0=p.ap[0:npart, 1040:1041], in1=p.ap[0:npart, 0:1], op=ALU.subtract), [p], [d])
                        DVE(lambda e: e.tensor_tensor(out=d.ap[0:npart, 1024:1040], in0=shT.ap[0:npart, blk * 16:(blk + 1) * 16], in1=p.ap[0:npart, 1024:1040], op=ALU.subtract), [p, shT], [d])
                    else:
                        DVE(lambda e: e.tensor_scalar(out=d.ap[0:npart, 0:1], in0=p.ap[0:npart, 0:1], scalar1=-1.0, scalar2=None, op0=ALU.mult), [p], [d])
                    DVE(lambda e: e.scalar_tensor_tensor(out=d.ap[0:npart, 0:NM], in0=d.ap[0:npart, 0:NM], scalar=mu, in1=p.ap[0:npart, 0:NM], op0=ALU.mult, op1=ALU.add), [p, d, VB], [d])

                with contextlib.ExitStack() as stl:
                    T0 = sb(stl, "La", [128, NT])
                    T1 = sb(stl, "Lb", [128, NT])
                    ncl = 288 if own else 128
                    dst_np[0] = 128
                    proj(wL, 0, 128, hT, groups, evac_copy(T0))
                    mix(T0, T1, 24, 128)
                    ACT(lambda e: e.activation(out=TW.ap[0:64, 0:NM], in_=T1.ap[0:64, 0:NM], func=AF.Tanh), [T1], [TW])
                    ACT(lambda e: e.activation(out=aLo.ap[64:128, 0:NM], in_=T1.ap[64:128, 0:NM], func=AF.Copy), [T1], [aLo])
                    if own:
                        proj(wL, 128, 128, hT, groups, evac_copy(T0))
                        mix(T0, T1, 25, 128)
                        ACT(lambda e: e.activation(out=sgT1.ap[:, 0:NM], in_=T1.ap[:, 0:NM], func=AF.Sigmoid), [T1], [sgT1])
                        dst_np[0] = 32
                        proj(wL, 256, 32, hT, groups, evac_copy(T0))
                        mix(T0, T1, 26, 32)
                        ACT(lambda e: e.activation(out=sgT2.ap[0:32, 0:NM], in_=T1.ap[0:32, 0:NM], func=AF.Sigmoid), [T1], [sgT2])
                        dst_np[0] = 128
                    S.barrier()

                with contextlib.ExitStack() as stq:
                    RB = [sb(stq, "RB%d" % i, [128, NT]) for i in range(4)]
                    RM = [sb(stq, "RM%d" % i, [128, NT]) for i in range(3)]
                    rbi = [0]
                    for j in range(8):
                        wP = wPs[j % 2]
                        ncw = 384 if own else 256
                        if j > 0:
                            S.dma("pool", wP.ap[:, :, 0:ncw], w_in_v[:, :, j * 384:j * 384 + ncw], writes=[wP])
                        for q in range(3 if own else 2):
                            rb = RB[rbi[0] % 4]
                            rm = RM[rbi[0] % 3]
                            rbi[0] += 1
                            proj(wP, q * 128, 128, hT, groups, evac_copy(rb))
                            mix(rb, rm, (8 + j, 16 + j, j)[q], 128)
                            S.dma("sp", PR_d[3 * j + q, :, 0:NM], rm.ap[:, 0:NM], reads=[rm], writes=[PRtk[3 * j + q]])
                    if own:
                        wSs = [sb(stq, "wS%d" % i, [128, 16, 512], BF16) for i in range(2)]
                        RU = [sb(stq, "RU%d" % i, [128, 1040], BF16) for i in range(2)]
                        for half, c0 in ((0, 4384), (1, 3360)):
                            for g4 in range(2):
                                wS = wSs[g4]
                                S.dma("pool", wS.ap[:], w_in_v[:, :, c0 + g4 * 512:c0 + (g4 + 1) * 512], writes=[wS])
                                for gg in range(4):
                                    g = g4 * 4 + gg
                                    if half == 0:
                                        rb = RB[rbi[0] % 4]
                                    else:
                                        rb = RU[rbi[0] % 2]
                                    rbi[0] += 1

                                    def evg(P, t0, t1, rb=rb):
                                        ACT(lambda e: e.activation(out=rb.ap[:, t0:t1], in_=P.ap[:, 0:t1 - t0], func=AF.Gelu), [P], [rb])
                                    proj(wS, gg * 128, 128, hT, groups3, evg)
                                    if half == 0:
                                        S.dma("sp", VZ_d[g], rb.ap[:, 0:1040], reads=[rb], writes=[VZtk])
                                    else:
                                        S.dma("sp", UZ_d[g], rb.ap[:, 0:1040], reads=[rb], writes=[UZtk])
                    S.barrier()
                sth.close()
                with contextlib.ExitStack() as stb:
                    TS = [sb(stb, "TS%d" % i, [128, NT]) for i in range(6)]
                    T0, T1, T2, T3, T5, T6 = TS
                    sqb = sb(stb, "sqb", [128, NT], BF16)
                    st16w = sb(stb, "st16w", [128, 16])
                    sgtok = sb(stb, "sgtok", [128, 8, 128])
                    Lf = sb(stb, "Lf", [128, 8, 258])
                    ED = sb(stb, "ED", [128, 8, 128])

                    class Ctx:
                        pass
                    bufs = []
                    for bi in range(2):
                        B = Ctx()
                        B.SC = sb(stb, "SC", [128, 16])
                        B.aT = sb(stb, "aT", [128, 1024], BF16)
                        B.bT = sb(stb, "bT", [128, 1024], BF16)
                        B.kT = sb(stb, "kT", [128, 1024], BF16)
                        B.rT = sb(stb, "rT", [128, 1024], BF16)
                        B.prodb = sb(stb, "prodb", [128, 1040], BF16)
                        B.Zar = sb(stb, "Zar", [128, 8, 4, 128], BF16)
                        B.Zb = sb(stb, "Zb", [128, 8, 2, 128], BF16)
                        DVE(lambda e: e.memset(B.Zar.ap[:], 0.0), [], [B.Zar])
                        DVE(lambda e: e.memset(B.Zb.ap[:], 0.0), [], [B.Zb])
                        B.Bhat = sb(stb, "Bhat", [128, 8, 128], BF16)
                        B.Khat = sb(stb, "Khat", [128, 8, 128], BF16)
                        B.Vtok = sb(stb, "Vtok", [128, 8, 128], BF16)
                        B.LGp = sb(stb, "LGp", [128, 128])
                        B.LBp = sb(stb, "LBp", [128, 128])
                        bufs.append(B)
                    ctxs = []
                    for ci in range(4):
                        cx = Ctx()
                        cx.ATm = sb(stb, "ATm", [128, 4, 256], BF16)
                        cx.AKm = sb(stb, "AKm", [128, 4, 256], BF16)
                        cx.Am = sb(stb, "Am", [128, 4, 128], BF16)
                        cx.An = [sb(stb, "An%d" % i, [128, 4, 128], BF16) for i in range(2)]
                        cx.Bn = [sb(stb, "Bn%d" % i, [128, 4, 128], BF16) for i in range(2)]
                        cx.Pt = [sb(stb, "Pt%d" % i, [128, 4, 128], BF16) for i in range(2)]
                        ctxs.append(cx)
                    Yb = sb(stb, "Yb", [128, 128], BF16)
                    Ub = sb(stb, "Ub", [128, 128], BF16)
                    Otok = sb(stb, "Otok", [128, 8, 128])
                    Q1 = sb(stb, "Q1", [128, 8, 128])
                    Q2 = sb(stb, "Q2", [128, 8, 128])
                    st16 = sb(stb, "st16", [128, 6, 16])
                    yatok = sb(stb, "yatok", [128, 8, 128], BF16)
                    yaT = sb(stb, "yaT", [128, 1040], BF16)
                    print("SBUF free in pairs scope", nc.sbuf_bytes_remaining)

                    def amats(cx, cp, j, B):
                        NW = 512 if own else 256
                        mstr = CB.ap[:, O_M4:O_M4 + 128].unsqueeze(1).to_broadcast([128, 2, 128])
                        minc = CB.ap[:, O_M4 + 128:O_M4 + 256].unsqueeze(1).to_broadcast([128, 2, 128])
                        for cl in range(2):
                            c = cp * 2 + cl
                            tcs = slice(c * 128, (c + 1) * 128)
                            us = slice(cl * 2, cl * 2 + 2)
                            PAT, PAK, PA = ps(), ps(), ps()
                            zr = B.Zar.ap[:, c, :, :].rearrange("p a t -> p (a t)")[:, 0:NW]
                            PE(lambda e: e.matmul(PAT.ap[:, 0:NW], lhsT=B.bT.ap[:, tcs], rhs=zr, start=True, stop=True), [B.bT, B.Zar], [PAT])
                            PE(lambda e: e.matmul(PAK.ap[:, 0:NW], lhsT=B.kT.ap[:, tcs], rhs=zr, start=True, stop=True), [B.kT, B.Zar], [PAK])
                            PE(lambda e: e.matmul(PA.ap[:, 0:256], lhsT=B.aT.ap[:, tcs], rhs=B.Zb.ap[:, c, :, :].rearrange("p a t -> p (a t)"), start=True, stop=True), [B.aT, B.Zb], [PA])
                            DVE(lambda e: e.tensor_tensor(out=cx.ATm.ap[:, us, 0:128], in0=PAT.ap[:, 0:256].rearrange("p (u t) -> p u t", u=2), in1=mstr, op=ALU.mult), [PAT, CB], [cx.ATm])
                            DVE(lambda e: e.tensor_tensor(out=cx.AKm.ap[:, us, 0:128], in0=PAK.ap[:, 0:256].rearrange("p (u t) -> p u t", u=2), in1=mstr, op=ALU.mult), [PAK, CB], [cx.AKm])
                            if own:
                                DVE(lambda e: e.tensor_tensor(out=cx.ATm.ap[:, us, 128:256], in0=PAT.ap[:, 256:512].rearrange("p (u t) -> p u t", u=2), in1=minc, op=ALU.mult), [PAT, CB], [cx.ATm])
                                DVE(lambda e: e.tensor_tensor(out=cx.AKm.ap[:, us, 128:256], in0=PAK.ap[:, 256:512].rearrange("p (u t) -> p u t", u=2), in1=minc, op=ALU.mult), [PAK, CB], [cx.AKm])
                            DVE(lambda e: e.tensor_tensor(out=cx.Am.ap[:, us, :], in0=PA.ap[:, 0:256].rearrange("p (u t) -> p u t", u=2), in1=CB.ap[:, O_MA:O_MA + 256].rearrange("p (u t) -> p u t", u=2), op=ALU.mult), [PA, CB], [cx.Am])
                        DVE(lambda e: e.tensor_tensor(out=cx.Pt[0].ap[:], in0=cx.ATm.ap[:, :, 0:128], in1=IDB.ap[:].unsqueeze(1).to_broadcast([128, 4, 128]), op=ALU.add), [cx.ATm, IDB], [cx.Pt[0]])
                        cx.Acur, cx.Bcur, cx.Pcur = cx.Am, cx.ATm, cx.Pt[0]
                        cx.Bcur_ap = cx.ATm.ap[:, :, 0:128]

                    def inverse(cxs):
                        for lvl in range(1, 7):
                            for cx in cxs:
                                cx.PAn = ps()
                                for u in range(4):
                                    PE(lambda e: e.matmul(cx.PAn.ap[:, u * 128:(u + 1) * 128], lhsT=cx.Bcur_ap[:, u, :], rhs=cx.Acur.ap[:, u, :], start=True, stop=True), [cx.Bcur, cx.Acur], [cx.PAn], inc=(u == 3))
                                if lvl < 6:
                                    cx.PBn = ps()
                                    for u in range(4):
                                        PE(lambda e: e.matmul(cx.PBn.ap[:, u * 128:(u + 1) * 128], lhsT=cx.Acur.ap[:, u, :], rhs=cx.Bcur_ap[:, u, :], start=True, stop=True), [cx.Bcur, cx.Acur], [cx.PBn], inc=(u == 3))
                            yield
                            for cx in cxs:
                                cx.Anew = cx.An[lvl % 2]
                                ACT(lambda e: e.activation(out=cx.Anew.ap[:], in_=cx.PAn.ap[:].rearrange("p (u t) -> p u t", u=4), func=AF.Copy), [cx.PAn], [cx.Anew])
                                if lvl < 6:
                                    cx.Bnew = cx.Bn[lvl % 2]
                                    ACT(lambda e: e.activation(out=cx.Bnew.ap[:], in_=cx.PBn.ap[:].rearrange("p (u t) -> p u t", u=4), func=AF.Copy), [cx.PBn], [cx.Bnew])
                            for cx in cxs:
                                cx.PP = ps()
                                for u in range(4):
                                    PE(lambda e: e.matmul(cx.PP.ap[:, u * 128:(u + 1) * 128], lhsT=cx.Anew.ap[:, u, :], rhs=cx.Pcur.ap[:, u, :], start=True, stop=True), [cx.Anew, cx.Pcur], [cx.PP], inc=(u == 3))
                            yield
                            for cx in cxs:
                                Pnew = cx.Pt[lvl % 2]
                                DVE(lambda e: e.tensor_tensor(out=Pnew.ap[:], in0=cx.PP.ap[:].rearrange("p (u t) -> p u t", u=4), in1=cx.Pcur.ap[:], op=ALU.add), [cx.PP, cx.Pcur], [Pnew])
                                cx.Acur, cx.Pcur = cx.Anew, Pnew
                                if lvl < 6:
                                    cx.Bcur = cx.Bnew
                                    cx.Bcur_ap = cx.Bnew.ap[:]

                    def chain(cx, cp, j, B):
                        for cl in range(2):
                            c = cp * 2 + cl
                            tcs = slice(c * 128, (c + 1) * 128)
                            PY = ps()
                            for h in range(2):
                                u = cl * 2 + h
                                hp = slice(h * 64, (h + 1) * 64)
                                PE(lambda e: e.matmul(PY.ap[:, hp], lhsT=B.aT.ap[:, tcs], rhs=SBF[j].ap[:, hp], start=True, stop=False), [B.aT, SBF[j]], [PY], inc=False)
                                PE(lambda e: e.matmul(PY.ap[:, hp], lhsT=cx.AKm.ap[:, u, 0:128], rhs=B.Vtok.ap[:, c, hp], start=False, stop=True), [cx.AKm, B.Vtok], [PY], inc=(h == 1))
                            ACT(lambda e: e.activation(out=Yb.ap[:], in_=PY.ap[:, 0:128], func=AF.Copy), [PY], [Yb])
                            PU = ps()
                            for h in range(2):
                                u = cl * 2 + h
                                hp = slice(h * 64, (h + 1) * 64)
                                PE(lambda e: e.matmul(PU.ap[:, hp], lhsT=cx.Pcur.ap[:, u, :], rhs=Yb.ap[:, hp], start=True, stop=True), [cx.Pcur, Yb], [PU], inc=(h == 1))
                            ACT(lambda e: e.activation(out=Ub.ap[:], in_=PU.ap[:, 0:128], func=AF.Copy), [PU], [Ub])
                            if own:
                                PO = ps()
                                for h in range(2):
                                    u = cl * 2 + h
                                    hp = slice(h * 64, (h + 1) * 64)
                                    PE(lambda e: e.matmul(PO.ap[:, hp], lhsT=B.rT.ap[:, tcs], rhs=SBF[j].ap[:, hp], start=True, stop=False), [B.rT, SBF[j]], [PO], inc=False)
                                    PE(lambda e: e.matmul(PO.ap[:, hp], lhsT=cx.ATm.ap[:, u, 128:256], rhs=Ub.ap[:, hp], start=False, stop=False), [cx.ATm, Ub], [PO], inc=False)
                                    PE(lambda e: e.matmul(PO.ap[:, hp], lhsT=cx.AKm.ap[:, u, 128:256], rhs=B.Vtok.ap[:, c, hp], start=False, stop=True), [cx.AKm, B.Vtok], [PO], inc=(h == 1))
                                ACT(lambda e: e.activation(out=Otok.ap[:, c, :], in_=PO.ap[:, 0:128], func=AF.Copy), [PO], [Otok])
                            PSn = ps()
                            for h in range(2):
                                hp = slice(h * 64, (h + 1) * 64)
                                PE(lambda e: e.matmul(PSn.ap[:, hp], lhsT=B.Bhat.ap[:, c, :], rhs=Ub.ap[:, hp], start=True, stop=False), [B.Bhat, Ub], [PSn], inc=False)
                                PE(lambda e: e.matmul(PSn.ap[:, hp], lhsT=B.Khat.ap[:, c, :], rhs=B.Vtok.ap[:, c, hp], start=False, stop=True), [B.Khat, B.Vtok], [PSn], inc=(h == 1))
                            DVE(lambda e: e.scalar_tensor_tensor(out=SST[j].ap[:], in0=SST[j].ap[:], scalar=B.SC.ap[:, c:c + 1], in1=PSn.ap[:, 0:128], op0=ALU.mult, op1=ALU.add), [SST[j], B.SC, PSn], [SST[j]])
                            DVE(lambda e: e.tensor_tensor(out=SBF[j].ap[:], in0=SST[j].ap[:], in1=CB.ap[:, O_BO:O_BO + 128], op=ALU.mult), [SST[j], CB], [SBF[j]])
                            yield


                    def prep(j, B):
                        jc = slice(j * 128, (j + 1) * 128)
                        kkc = VB.ap[:, V_KK + j:V_KK + j + 1]
                        kac = VB.ap[:, V_KA + j:V_KA + j + 1]
                        rkc = VB.ap[:, V_RK + j:V_RK + j + 1]
                        a0c = VB.ap[:, V_A0 + j:V_A0 + j + 1]
                        S.dma("sp", T1.ap[:, 0:NM], PR_d[3 * j, :, 0:NM], reads=[PRtk[3 * j]], writes=[T1])
                        S.dma("sp", T6.ap[:, 0:NM], PR_d[3 * j + 1, :, 0:NM], reads=[PRtk[3 * j + 1]], writes=[T6])
                        if own:
                            S.dma("sp", T5.ap[:, 0:NM], PR_d[3 * j + 2, :, 0:NM], reads=[PRtk[3 * j + 2]], writes=[T5])
                            S.dma("sp", B.LGp.ap[:], lg_d[:, jc].partition_broadcast(128), writes=[B.LGp])
                            S.dma("sp", B.LBp.ap[:], lb_d[:, jc].partition_broadcast(128), writes=[B.LBp])
                        yield
                        for c4 in range(2):
                            P = ps()
                            for cc in range(4):
                                c = c4 * 4 + cc
                                PE(lambda e: e.matmul(P.ap[:, cc * 128:(cc + 1) * 128], lhsT=TW.ap[0:65, c * 128:(c + 1) * 128], rhs=wupa.ap[0:65, jc], start=True, stop=True), [TW, wupa], [P], inc=(cc == 3))
                            ACT(lambda e: e.activation(out=sgtok.ap[:, c4 * 4:(c4 + 1) * 4, :], in_=P.ap[:].rearrange("p (c t) -> p c t", c=4), func=AF.Sigmoid), [P], [sgtok])
                        yield
                        if own:
                            P = ps()
                            PE(lambda e: e.matmul(P.ap[:, 0:16], lhsT=wupa.ap[0:65, jc], rhs=TW.ap[0:65, 1024:1040], start=True, stop=True), [TW, wupa], [P])
                            ACT(lambda e: e.activation(out=st16w.ap[:, 0:16], in_=P.ap[:, 0:16], func=AF.Sigmoid), [P], [st16w])
                            for hh in range(2):
                                hq = slice(hh * 64, (hh + 1) * 64)
                                ACT(lambda e: e.activation(out=SMZ.ap[hq, j, 4, hh * 16:(hh + 1) * 16], in_=st16w.ap[hq, 0:16], func=AF.Exp, scale=-C_DEC), [st16w], [SMZ])
                        yield
                        for (t0, t1) in groups:
                            t1 = min(t1, NM)
                            P = ps()
                            PE(lambda e: e.matmul(P.ap[:, 0:t1 - t0], lhsT=aup.ap[64:128, jc], rhs=aLo.ap[64:128, t0:t1], start=True, stop=True), [aup, aLo], [P])
                            ACT(lambda e: e.activation(out=T2.ap[:, t0:t1], in_=P.ap[:, 0:t1 - t0], func=AF.Sigmoid, bias=a0c), [P, VB], [T2])
                        yield
                        yield
                        for c in range(NCH):
                            P = ps()
                            PE(lambda e: e.matmul(P.ap[:, 0:258], lhsT=sgtok.ap[:, c, :], rhs=CB.ap[:, O_CM:O_CM + 258], start=True, stop=True), [sgtok, CB], [P])
                            ACT(lambda e: e.activation(out=Lf.ap[:, c, :], in_=P.ap[:, 0:258], func=AF.Copy), [P], [Lf])
                        yield
                        for c4 in range(2):
                            P = ps()
                            for cc in range(4):
                                c = c4 * 4 + cc
                                last = (c == NCH - 1)
                                PE(lambda e: e.matmul(P.ap[:, cc * 128:(cc + 1) * 128], lhsT=CB.ap[:, O_T2:O_T2 + 128], rhs=sgtok.ap[:, c, :], start=True, stop=last), [sgtok, CB], [P], inc=last)
                                if not last:
                                    PE(lambda e: e.matmul(P.ap[:, cc * 128:(cc + 1) * 128], lhsT=CB.ap[:, O_F64:O_F64 + 128], rhs=sgtok.ap[:, c + 1, :], start=False, stop=True), [sgtok, CB], [P], inc=True)
                            ACT(lambda e: e.activation(out=ED.ap[:, c4 * 4:(c4 + 1) * 4, :], in_=P.ap[:].rearrange("p (c t) -> p c t", c=4), func=AF.Exp), [P], [ED])
                        yield
                        yield
                        DVE(lambda e: e.tensor_tensor(out=B.SC.ap[:, 0:NCH - 1].unsqueeze(2), in0=Lf.ap[:, 0:NCH - 1, 256:257], in1=Lf.ap[:, 1:NCH, 257:258], op=ALU.add), [Lf], [B.SC])
                        yield
                        DVE(lambda e: e.tensor_copy(out=B.SC.ap[:, NCH - 1:NCH], in_=Lf.ap[:, NCH - 1, 256:257]), [Lf], [B.SC])
                        yield
                        DVE(lambda e: e.tensor_copy(out=B.SC.ap[:, 8:9], in_=Lf.ap[:, 0, 257:258]), [Lf], [B.SC])
                        yield
                        ACT(lambda e: e.activation(out=B.SC.ap[:, 0:9], in_=B.SC.ap[:, 0:9], func=AF.Exp), [B.SC], [B.SC])
                        yield
                        DVE(lambda e: e.tensor_scalar(out=SST[j].ap[:], in0=SST[j].ap[:], scalar1=B.SC.ap[:, 8:9], scalar2=None, op0=ALU.mult), [SST[j], B.SC], [SST[j]])
                        yield
                        DVE(lambda e: e.tensor_tensor(out=SBF[j].ap[:], in0=SST[j].ap[:], in1=CB.ap[:, O_BO:O_BO + 128], op=ALU.mult), [SST[j], CB], [SBF[j]])
                        yield
                        yield
                        yield
                        ACT(lambda e: e.activation(out=sqb.ap[:, 0:NM], in_=T1.ap[:, 0:NM], func=AF.Square, scale=kkc), [T1, VB], [sqb])
                        yield
                        for (t0, t1) in groups:
                            t1 = min(t1, NM)
                            P = ps()
                            PE(lambda e: e.matmul(P.ap[:, 0:t1 - t0], lhsT=BOB.ap[:], rhs=sqb.ap[:, t0:t1], start=True, stop=True), [BOB, sqb], [P])
                            DVE(lambda e: e.tensor_scalar(out=T0.ap[:, t0:t1], in0=P.ap[:, 0:t1 - t0], scalar1=1e-24, scalar2=None, op0=ALU.max), [P], [T0])
                        yield
                        ACT(lambda e: e.activation(out=T0.ap[:, 0:NM], in_=T0.ap[:, 0:NM], func=AF.Sqrt), [T0], [T0])
                        yield
                        ACT(lambda e: e.activation(out=T0.ap[:, 0:NM], in_=T0.ap[:, 0:NM], func=AF.Ln), [T0], [T0])
                        yield
                        ACT(lambda e: e.activation(out=T0.ap[:, 0:NM], in_=T0.ap[:, 0:NM], func=AF.Exp, scale=-1.0), [T0], [T0])
                        yield
                        DVE(lambda e: e.scalar_tensor_tensor(out=T0.ap[:, 0:NM], in0=T1.ap[:, 0:NM], scalar=kkc, in1=T0.ap[:, 0:NM], op0=ALU.mult, op1=ALU.mult), [T0, T1, VB], [T0])
                        yield
                        DVE(lambda e: e.tensor_tensor(out=T3.ap[:, 0:NM], in0=T0.ap[:, 0:NM], in1=T2.ap[:, 0:NM], op=ALU.mult), [T0, T2], [T3])
                        yield
                        DVE(lambda e: e.tensor_scalar(out=T2.ap[:, 0:NM], in0=T2.ap[:, 0:NM], scalar1=-1.0, scalar2=kac, op0=ALU.add, op1=ALU.mult), [T2, VB], [T2])
                        yield
                        DVE(lambda e: e.scalar_tensor_tensor(out=T2.ap[:, 0:NM], in0=T2.ap[:, 0:NM], scalar=1.0, in1=T1.ap[:, 0:NM], op0=ALU.add, op1=ALU.mult), [T2, T1], [T2])
                        yield
                        T1v = T1.ap[:, 0:1024].rearrange("p (c t) -> p c t", c=8)
                        yield
                        ACT(lambda e: e.activation(out=T1v, in_=Lf.ap[:, :, 128:256], func=AF.Exp), [Lf], [T1])
                        yield
                        DVE(lambda e: e.scalar_tensor_tensor(out=B.aT.ap[:], in0=T0.ap[:, 0:1024], scalar=-1.0, in1=T1.ap[:, 0:1024], op0=ALU.mult, op1=ALU.mult), [T0, T1], [B.aT])
                        yield
                        ACT(lambda e: e.activation(out=T1v, in_=Lf.ap[:, :, 0:128], func=AF.Exp, scale=-1.0), [Lf], [T1])
                        yield
                        DVE(lambda e: e.tensor_tensor(out=B.bT.ap[:], in0=T3.ap[:, 0:1024], in1=T1.ap[:, 0:1024], op=ALU.mult), [T3, T1], [B.bT])
                        yield
                        DVE(lambda e: e.tensor_tensor(out=B.kT.ap[:], in0=T2.ap[:, 0:1024], in1=T1.ap[:, 0:1024], op=ALU.mult), [T2, T1], [B.kT])
                        yield
                        for hh in range(2):
                            hq = slice(hh * 64, (hh + 1) * 64)
                            ACT(lambda e: e.activation(out=B.Zar.ap[hq, :, hh, :], in_=B.aT.ap[hq, :].rearrange("p (c t) -> p c t", c=8), func=AF.Copy), [B.aT], [B.Zar])
                            ACT(lambda e: e.activation(out=B.Zb.ap[hq, :, hh, :], in_=B.bT.ap[hq, :].rearrange("p (c t) -> p c t", c=8), func=AF.Copy), [B.bT], [B.Zb])
                        yield
                        if own:
                            ACT(lambda e: e.activation(out=T1v, in_=Lf.ap[:, :, 0:128], func=AF.Exp), [Lf], [T1])
                            DVE(lambda e: e.tensor_tensor(out=B.rT.ap[:], in0=T5.ap[:, 0:1024], in1=T1.ap[:, 0:1024], op=ALU.mult), [T5, T1], [B.rT])
                            for hh in range(2):
                                hq = slice(hh * 64, (hh + 1) * 64)
                                ACT(lambda e: e.activation(out=B.Zar.ap[hq, :, 2 + hh, :], in_=B.rT.ap[hq, :].rearrange("p (c t) -> p c t", c=8), func=AF.Copy), [B.rT], [B.Zar])
                            DVE(lambda e: e.scalar_tensor_tensor(out=B.prodb.ap[:, 0:1040], in0=T5.ap[:, 0:1040], scalar=rkc, in1=T2.ap[:, 0:1040], op0=ALU.mult, op1=ALU.mult), [T5, T2, VB], [B.prodb])
                        yield
                        yield
                        for c4 in range(2):
                            c4s = slice(c4 * 4, (c4 + 1) * 4)
                            for (src, kind) in ((T3, 0), (T2, 1), (T6, 2)):
                                Pq_ = ps()
                                for cc in range(4):
                                    c = c4 * 4 + cc
                                    tcs = slice(c * 128, (c + 1) * 128)
                                    PE(lambda e: e.transpose(out=Pq_.ap[:, cc * 128:(cc + 1) * 128], in_=src.ap[:, tcs], identity=ident), [src, CB], [Pq_], inc=(cc == 3))
                                pv = Pq_.ap[:].rearrange("p (c t) -> p c t", c=4)
                                if kind == 0:
                                    DVE(lambda e: e.tensor_tensor(out=B.Bhat.ap[:, c4s, :], in0=pv, in1=ED.ap[:, c4s, :], op=ALU.mult), [Pq_, ED], [B.Bhat])
                                elif kind == 1:
                                    DVE(lambda e: e.tensor_tensor(out=B.Khat.ap[:, c4s, :], in0=pv, in1=ED.ap[:, c4s, :], op=ALU.mult), [Pq_, ED], [B.Khat])
                                else:
                                    ACT(lambda e: e.activation(out=B.Vtok.ap[:, c4s, :], in_=pv, func=AF.Copy), [Pq_], [B.Vtok])
                            yield
                        yield
                        if own:
                            for hh in range(2):
                                hq = slice(hh * 64, (hh + 1) * 64)
                                zc = slice(hh * 16, (hh + 1) * 16)
                                DVE(lambda e: e.tensor_scalar(out=SMZ.ap[hq, j, 0, zc], in0=T0.ap[hq, 1024:1040], scalar1=-1.0, scalar2=None, op0=ALU.mult), [T0], [SMZ])
                                DVE(lambda e: e.tensor_copy(out=SMZ.ap[hq, j, 1, zc], in_=T3.ap[hq, 1024:1040]), [T3], [SMZ])
                                DVE(lambda e: e.tensor_copy(out=SMZ.ap[hq, j, 2, zc], in_=T2.ap[hq, 1024:1040]), [T2], [SMZ])
                                DVE(lambda e: e.tensor_copy(out=SMZ.ap[hq, j, 3, zc], in_=T5.ap[hq, 1024:1040]), [T5], [SMZ])
                            DVE(lambda e: e.tensor_copy(out=SMv.ap[:, j, :], in_=T6.ap[:, 1024:1040]), [T6], [SMv])
                            P = ps()
                            PE(lambda e: e.matmul(P.ap[0:16, 0:2], lhsT=B.prodb.ap[:, 1024:1040], rhs=BSB.ap[:], start=True, stop=True), [B.prodb, BSB], [P])
                            ACT(lambda e: e.activation(out=RKS.ap[0:16, 2 * j:2 * j + 2], in_=P.ap[0:16, 0:2], func=AF.Copy), [P], [RKS])

                    def Xg(j, g, B):
                        cxa, cxb = ctxs[2 * g], ctxs[2 * g + 1]
                        amats(cxa, 2 * g, j, B)
                        yield
                        amats(cxb, 2 * g + 1, j, B)
                        yield
                        for _ in inverse([cxa, cxb]):
                            yield

                    def Cg(j, g, B):
                        for _ in chain(ctxs[2 * g], 2 * g, j, B):
                            yield
                        for _ in chain(ctxs[2 * g + 1], 2 * g + 1, j, B):
                            yield

                    def postg(j, B):
                        jc = slice(j * 128, (j + 1) * 128)
                        yield
                        if own:
                            yield
                            P = ps()
                            PE(lambda e: e.transpose(out=P.ap[:, 0:128], in_=SST[j].ap[:], identity=ident), [SST[j], CB], [P])
                            ACT(lambda e: e.activation(out=Q1.ap[:, 0, :], in_=P.ap[:, 0:128], func=AF.Copy), [P], [Q1])
                            for h in range(2):
                                hp = slice(h * 64, (h + 1) * 64)
                                S.dma("sp", wkvp_o[2 * j + h], Q1.ap[hp, 0, hp], reads=[Q1], is_output=True)
                            post(128, 8, 2, Otok, Otok.ap[:].rearrange("p c (h v) -> p (c h) v", h=2), None,
                                 Q1, Q1.ap[:].rearrange("p c (h v) -> p (c h) v", h=2), Q1.ap[:].rearrange("p c (h v) -> p c h v", h=2),
                                 Q2, Q2.ap[:].rearrange("p c (h v) -> p (c h) v", h=2), st16,
                                 B.LGp.ap[:].rearrange("p (h v) -> p h v", h=2).unsqueeze(1).to_broadcast([128, 8, 2, 64]),
                                 B.LBp.ap[:].rearrange("p (h v) -> p h v", h=2).unsqueeze(1).to_broadcast([128, 8, 2, 64]), B.LGp, B.LBp)
                            P = ps()
                            for c in range(8):
                                PE(lambda e: e.matmul(P.ap[:, 2 * c:2 * c + 2], lhsT=B.prodb.ap[:, c * 128:(c + 1) * 128], rhs=BSB.ap[:], start=True, stop=True), [B.prodb, BSB], [P], inc=(c == 7))
                            ACT(lambda e: e.activation(out=st16.ap[:, 5, :], in_=P.ap[:, 0:16], func=AF.Copy), [P], [st16])
                            Q1v = Q1.ap[:].rearrange("p c (h v) -> p (c h) v", h=2)
                            Q2v = Q2.ap[:].rearrange("p c (h v) -> p (c h) v", h=2)
                            DVE(lambda e: e.tensor_tensor(out=Q2v, in0=B.Vtok.ap[:].rearrange("p c (h v) -> p (c h) v", h=2), in1=st16.ap[:, 5, :].unsqueeze(2).to_broadcast([128, 16, 64]), op=ALU.mult), [B.Vtok, st16], [Q2])
                            DVE(lambda e: e.tensor_tensor(out=Q1.ap[:], in0=Q1.ap[:], in1=Q2.ap[:], op=ALU.add), [Q1, Q2], [Q1])
                            for c4 in range(2):
                                P = ps()
                                for cc in range(4):
                                    c = c4 * 4 + cc
                                    tcs = slice(c * 128, (c + 1) * 128)
                                    PE(lambda e: e.matmul(P.ap[:, cc * 128:(cc + 1) * 128], lhsT=sgT1.ap[:, tcs], rhs=gup1.ap[:, jc], start=True, stop=False), [sgT1, gup1], [P], inc=False)
                                    PE(lambda e: e.matmul(P.ap[:, cc * 128:(cc + 1) * 128], lhsT=sgT2.ap[0:32, tcs], rhs=gup2.ap[0:32, jc], start=False, stop=True), [sgT2, gup2], [P], inc=(cc == 3))
                                c4s = slice(c4 * 4, (c4 + 1) * 4)
                                DVE(lambda e: e.tensor_tensor(out=yatok.ap[:, c4s, :], in0=P.ap[:].rearrange("p (c t) -> p c t", c=4), in1=Q1.ap[:, c4s, :], op=ALU.mult), [P, Q1], [yatok])
                            P = ps()
                            pb = P.ap[:].bitcast(BF16)
                            for c in range(8):
                                PE(lambda e: e.transpose(out=pb[:, c * 128:(c + 1) * 128], in_=yatok.ap[:, c, :], identity=IDB.ap[:]), [yatok, IDB], [P], inc=(c == 7))
                            ACT(lambda e: e.activation(out=yaT.ap[:, 0:1024], in_=pb, func=AF.Copy), [P], [yaT])
                            S.dma("sp", YT_d[j, :, 0:1024], yaT.ap[:, 0:1024], reads=[yaT], writes=[YTk])

                    tasks = []
                    PX, PC, PP_ = (0, 4), (4, 2), (6, 2)
                    for j in range(8):
                        B = bufs[j % 2]
                        tasks.append((("P", j), (lambda j=j, B=B: prep(j, B)), [("P", j - 1)] * (j > 0) + [("O", j - 2)] * (j > 1), PP_))
                        tasks.append((("X", j, 0), (lambda j=j, B=B: Xg(j, 0, B)), [("P", j)] + [("C", j - 1, 0)] * (j > 0), PX))
                        tasks.append((("X", j, 1), (lambda j=j, B=B: Xg(j, 1, B)), [("X", j, 0)] + [("C", j - 1, 1)] * (j > 0), PX))
                        tasks.append((("C", j, 0), (lambda j=j, B=B: Cg(j, 0, B)), [("X", j, 0)] + [("O", j - 1)] * (j > 0), PC))
                        tasks.append((("C", j, 1), (lambda j=j, B=B: Cg(j, 1, B)), [("X", j, 1), ("C", j, 0)], PC))
                        tasks.append((("O", j), (lambda j=j, B=B: postg(j, B)), [("C", j, 1)], PC))
                    run_tasks(tasks)
                    S.barrier()
                    ck(9 if own else 6)
                if own:
                    sts_ = contextlib.ExitStack()
                    stg_ = contextlib.ExitStack()
                    run_tasks([("S", (lambda: sample_gen(sts_, (sgT1, sgT2, gup1, gup2))), [], (0, 4)),
                               ("G", (lambda: sgu_gen(stg_)), [], (4, 4))])
                    S.barrier()
                    stg_.close()
                    sts_.close()
                    ck(11)
                S.barrier()
        rest_phase()
        S.stopped = False
        S.finish("sp")
        print("instructions", S.n_ins, "waits", S.n_wait)
    return nc


_NC_CACHE = {}


def _perm_w_in(w):
    idx = []
    for j in range(8):
        idx += list(range(1024 + j * 128, 1024 + (j + 1) * 128))
        idx += list(range(2048 + j * 128, 2048 + (j + 1) * 128))
        idx += list(range(j * 128, (j + 1) * 128))
    idx += list(range(3072, 5408))
    return np.ascontiguousarray(w[:, np.asarray(idx)])


def kernel(**inp):
    f32 = np.float32
    xp = np.asarray(inp["x_prompt"], f32)
    xs = np.asarray(inp["x_sample"], f32)
    swkv = np.asarray(inp["state_wkv"], f32)
    ssh = np.asarray(inp["state_shift"], f32)
    if "nc" not in _NC_CACHE:
        _NC_CACHE["nc"] = build()
    nc = _NC_CACHE["nc"]
    cb = make_consts()
    vb = np.zeros((128, NVB), f32)
    mu = np.zeros(27 * 128, f32)
    mu[:DSH] = np.asarray(inp["mu_shift"], f32)[0]
    vb[:, V_MU:V_MU + 27] = mu.reshape(27, 128).T
    vb[:, V_KK:V_KK + 8] = np.asarray(inp["k_k"], f32)[0].reshape(8, 128).T
    vb[:, V_KA:V_KA + 8] = np.asarray(inp["k_a"], f32)[0].reshape(8, 128).T
    vb[:, V_RK:V_RK + 8] = np.asarray(inp["r_k"], f32)[0].reshape(8, 128).T
    vb[:, V_A0:V_A0 + 8] = np.asarray(inp["a0"], f32)[0].reshape(8, 128).T
    vb[:, V_SG:V_SG + 8] = np.asarray(inp["sgu_norm_g"], f32)[0].reshape(8, 128).T
    vb[:, V_SB:V_SB + 8] = np.asarray(inp["sgu_norm_b"], f32)[0].reshape(8, 128).T
    vb[:, V_W00:V_W00 + 8] = np.asarray(inp["sgu_w"], f32)[0][:, 0, 0][None, :]
    vb[:, V_B0:V_B0 + 8] = np.asarray(inp["sgu_b"], f32)[0][:, 0][None, :]
    wupa = np.concatenate([np.asarray(inp["w_up"], f32)[0], np.asarray(inp["w0"], f32)[0][None, :]], 0)
    common = {
        "w_in": _perm_w_in(np.asarray(inp["w_in"], f32)[0]),
        "w_out": np.ascontiguousarray(np.asarray(inp["w_out"], f32)[0]),
        "w_fu": np.ascontiguousarray(np.asarray(inp["w_ffn_up"], f32)[0]),
        "w_fd": np.ascontiguousarray(np.asarray(inp["w_ffn_down"], f32)[0]),
        "wupa": np.ascontiguousarray(wupa),
        "aup": np.ascontiguousarray(np.asarray(inp["a_up"], f32)[0]),
        "gup": np.ascontiguousarray(np.asarray(inp["g_up"], f32)[0]),
        "g1": np.ascontiguousarray(np.asarray(inp["norm1_g"], f32).reshape(1, D)),
        "g2": np.ascontiguousarray(np.asarray(inp["norm2_g"], f32).reshape(1, D)),
        "gf": np.ascontiguousarray(np.asarray(inp["norm_f_g"], f32).reshape(1, D)),
        "lg": np.ascontiguousarray(np.asarray(inp["lnx_g"], f32).reshape(1, 1024)),
        "lb": np.ascontiguousarray(np.asarray(inp["lnx_b"], f32).reshape(1, 1024)),
        "sgb": np.ascontiguousarray(np.asarray(inp["sgu_b"], f32).reshape(1, 1024)),
        "sgw": np.ascontiguousarray(np.asarray(inp["sgu_w"], f32)[0]),
        "cb": cb,
        "sel2": make_sel2(),
        "vb": vb,
    }
    in_maps = []
    for c in range(8):
        b, half = divmod(c, 2)
        xin = np.zeros((2048, D), f32)
        xsm = np.zeros((17, D), f32)
        if half == 1:
            xin[:] = xp[b]
            xsm[16] = xp[b, 1023]
        else:
            xin[1024:] = xp[b, :1024]
        xsm[:16] = xs[16 * c:16 * (c + 1), 0]
        shpad = np.zeros((16, 27 * 128), f32)
        shpad[:, :DSH] = ssh[0, 16 * c:16 * (c + 1)]
        shT = np.ascontiguousarray(shpad.reshape(16, 27, 128).transpose(2, 1, 0).reshape(128, 27 * 16))
        m = dict(common)
        m.update({"xin": xin, "xsm": xsm, "swkv": np.ascontiguousarray(swkv[0, 16 * c:16 * (c + 1)]), "shT": shT})
        in_maps.append(m)
    res = run_bass_kernel_spmd(nc, in_maps, core_ids=list(range(8)))
    R = res.results
    y_prompt = np.zeros((4, 2048, D), f32)
    y_sample = np.zeros((128, 1, D), f32)
    wkv_prompt = np.zeros((1, 4, 16, 64, 64), f32)
    shift_prompt = np.zeros((1, 4, DSH), f32)
    wkv_sample = np.zeros((1, 128, 16, 64, 64), f32)
    shift_sample = np.zeros((1, 128, DSH), f32)
    sgu_v_sample = np.zeros((1, 128, 1, 1024), f32)
    for c in range(8):
        b, half = divmod(c, 2)
        r = R[c]
        y_prompt[b, half * 1024:(half + 1) * 1024] = r["y"]
        y_sample[16 * c:16 * (c + 1), 0] = r["ysm"]
        if half == 1:
            wkv_prompt[0, b] = r["wkvp"]
            shift_prompt[0, b] = r["shp"][0, :DSH]
        wkv_sample[0, 16 * c:16 * (c + 1)] = r["wkvs"]
        shift_sample[0, 16 * c:16 * (c + 1)] = r["shp"][1:17, :DSH]
        sgu_v_sample[0, 16 * c:16 * (c + 1), 0] = r["sguv"]
    return (y_prompt, y_sample, wkv_prompt, shift_prompt, wkv_sample, shift_sample, sgu_v_sample)
```
